# Optimizing a Trainium2 kernel written in Bass

```python
import jax, jax.numpy as jnp
from jax import lax
import numpy as np

D_MODEL = 1024
BATCH = 16
SEQ = 2048
DEPTH = 1
DEC_BATCH = 128
DEC_SEQ = 8
PAST_LEN = 16384
PAGE_SIZE = 128

HEAD_DIM = 64
RWKV_WIDTH = D_MODEL // 2
RWKV_HEADS = RWKV_WIDTH // HEAD_DIM
ATTN_WIDTH = D_MODEL - RWKV_WIDTH
ATTN_Q_HEADS = ATTN_WIDTH // HEAD_DIM
ATTN_KV_HEADS = 2
ATTN_GROUP = ATTN_Q_HEADS // ATTN_KV_HEADS
KV_WIDTH = ATTN_KV_HEADS * HEAD_DIM
DECAY_LORA = 64
ICLR_LORA = 64
WINDOW = 128
BLOCK = WINDOW
ROPE_THETA = 500000.0
ROPE_DIM = HEAD_DIM // 4
PLE_DIM = 256
NORM_EPS = 1e-6
GN_EPS = 64e-5
NEG_INF = -1e30
SHIFT_DIM = 3 * RWKV_WIDTH + DECAY_LORA + ICLR_LORA
IN_DIM = SHIFT_DIM + RWKV_WIDTH + ATTN_WIDTH + 2 * KV_WIDTH + ATTN_WIDTH

kernel_name = 'hymba_rwkv7_swa_sink_step'


def rmsnorm(x, g):
    xf = x.astype(jnp.float32)
    y = xf * lax.rsqrt(jnp.mean(xf * xf, axis=-1, keepdims=True) + NORM_EPS)
    return (y * g.astype(jnp.float32)).astype(x.dtype)


def apply_rope(x, pos):
    half = ROPE_DIM // 2
    inv = ROPE_THETA ** (-jnp.arange(half, dtype=jnp.float32) / half)
    ang = pos.astype(jnp.float32)[:, None] * inv[None, :]
    cos = jnp.cos(ang)[:, None, :]
    sin = jnp.sin(ang)[:, None, :]
    xf = x[..., :ROPE_DIM].astype(jnp.float32)
    x1, x2 = xf[..., :half], xf[..., half:]
    rot = jnp.concatenate([x1 * cos - x2 * sin, x2 * cos + x1 * sin], axis=-1).astype(x.dtype)
    return jnp.concatenate([rot, x[..., ROPE_DIM:]], axis=-1)


def sink_softmax(s, mask, sink):
    s = jnp.where(mask, s, NEG_INF)
    sink = jnp.broadcast_to(sink.astype(jnp.float32), s.shape[:-1] + (1,))
    return jax.nn.softmax(jnp.concatenate([s, sink], axis=-1), axis=-1)[..., :-1]


def swa_prompt(q, k, v, sinks, pos):
    B, T = q.shape[:2]
    nb = T // BLOCK
    qb = q.reshape(B, nb, BLOCK, ATTN_KV_HEADS, ATTN_GROUP, HEAD_DIM)
    pad = ((0, 0), (BLOCK, 0), (0, 0), (0, 0))
    def band(t):
        tp = jnp.pad(t, pad)[:, :T].reshape(B, nb, BLOCK, ATTN_KV_HEADS, HEAD_DIM)
        return jnp.concatenate([tp, t.reshape(B, nb, BLOCK, ATTN_KV_HEADS, HEAD_DIM)], axis=2)
    kb, vb = band(k), band(v)
    qpos = pos.reshape(nb, BLOCK)
    kpos = jnp.concatenate([qpos - BLOCK, qpos], axis=1)
    diff = qpos[:, :, None] - kpos[:, None, :]
    mask = (diff >= 0) & (diff < WINDOW) & (kpos[:, None, :] >= 0)
    s = jnp.einsum('bnqhgd,bnkhd->bnhgqk', qb, kb).astype(jnp.float32) * (HEAD_DIM ** -0.5)
    p = sink_softmax(s, mask[None, :, None, None], sinks.reshape(ATTN_KV_HEADS, ATTN_GROUP)[:, :, None, None])
    o = jnp.einsum('bnhgqk,bnkhd->bnqhgd', p.astype(v.dtype), vb)
    return o.reshape(B, T, ATTN_WIDTH)


def swa_sample(q, k, v, kbuf, vbuf, sinks, pos):
    B, T = q.shape[:2]
    W = kbuf.shape[1]
    k_all = jnp.concatenate([kbuf.astype(k.dtype), k], axis=1)
    v_all = jnp.concatenate([vbuf.astype(v.dtype), v], axis=1)
    kpos = jnp.concatenate([PAST_LEN - W + jnp.arange(W), pos])
    diff = pos[:, None] - kpos[None, :]
    mask = (diff >= 0) & (diff < WINDOW)
    qg = q.reshape(B, T, ATTN_KV_HEADS, ATTN_GROUP, HEAD_DIM)
    s = jnp.einsum('bqhgd,bkhd->bhgqk', qg, k_all).astype(jnp.float32) * (HEAD_DIM ** -0.5)
    p = sink_softmax(s, mask, sinks.reshape(ATTN_KV_HEADS, ATTN_GROUP)[:, :, None, None])
    o = jnp.einsum('bhgqk,bkhd->bqhgd', p.astype(v.dtype), v_all).reshape(B, T, ATTN_WIDTH)
    return o, k_all[:, -W:], v_all[:, -W:]


def rwkv_time_mix(z_shift, shift0, S0, mu, w0, w2, a0, a2, k_k, k_a, r_k, ln_w, ln_b):
    B, T, _ = z_shift.shape
    H, D, R = RWKV_HEADS, HEAD_DIM, RWKV_WIDTH
    f32 = jnp.float32
    prev = jnp.concatenate([shift0[:, None].astype(z_shift.dtype), z_shift[:, :-1]], axis=1)
    zs = z_shift + mu * (prev - z_shift)
    r, kx, v = zs[..., :R], zs[..., R:2 * R], zs[..., 2 * R:3 * R]
    wd, ad = zs[..., 3 * R:3 * R + DECAY_LORA], zs[..., 3 * R + DECAY_LORA:]
    w_log = -jax.nn.softplus(-(w0 + jnp.tanh(wd) @ w2).astype(f32)) - 0.5
    decay = jnp.exp(-jnp.exp(w_log))
    a = jax.nn.sigmoid((a0 + ad @ a2).astype(f32))
    heads = lambda t: t.astype(f32).reshape(B, T, H, D)
    r, kx, v, decay, a = heads(r), heads(kx), heads(v), heads(decay), heads(a)
    kk = kx * k_k.astype(f32).reshape(H, D)
    kk = kk / jnp.maximum(jnp.linalg.norm(kk, axis=-1, keepdims=True), 1e-12)
    k = kx * (1.0 + (a - 1.0) * k_a.astype(f32).reshape(H, D))

    def step(S, inp):
        r_t, w_t, k_t, v_t, kk_t, a_t = inp
        sa = jnp.einsum('bhvk,bhk->bhv', S, kk_t)
        S = (S * w_t[:, :, None, :] - sa[..., None] * (kk_t * a_t)[:, :, None, :]
             + v_t[..., None] * k_t[:, :, None, :])
        return S, jnp.einsum('bhvk,bhk->bhv', S, r_t)

    xs = tuple(jnp.moveaxis(t, 1, 0) for t in (r, decay, k, v, kk, a))
    S_fin, y = lax.scan(step, S0.astype(f32), xs)
    y = jnp.moveaxis(y, 0, 1)
    mean = jnp.mean(y, axis=-1, keepdims=True)
    var = jnp.mean(jnp.square(y - mean), axis=-1, keepdims=True)
    yn = (y - mean) * lax.rsqrt(var + GN_EPS) * ln_w.astype(f32).reshape(H, D) + ln_b.astype(f32).reshape(H, D)
    bonus = jnp.sum(r * k * r_k.astype(f32).reshape(H, D), axis=-1, keepdims=True) * v
    out = (yn + bonus).reshape(B, T, R).astype(z_shift.dtype)
    return out, S_fin, z_shift[:, -1]


def mixer_layer(h, p, pos, S0, shift0, kbuf, vbuf, g_norm, w_in, mu, w0, w2, a0, a2,
                k_k, k_a, r_k, ln_w, ln_b, sinks, w_out, g_ple, w_pg, w_pp):
    B, T, _ = h.shape
    u = rmsnorm(h, g_norm)
    z = u @ w_in
    o1 = SHIFT_DIM
    o2 = o1 + RWKV_WIDTH
    o3 = o2 + ATTN_WIDTH
    o4 = o3 + KV_WIDTH
    o5 = o4 + KV_WIDTH
    z_shift, gate_r, q, k, v, gate_a = (z[..., :o1], z[..., o1:o2], z[..., o2:o3],
                                        z[..., o3:o4], z[..., o4:o5], z[..., o5:])
    o_r, S_new, shift_new = rwkv_time_mix(z_shift, shift0, S0, mu, w0, w2, a0, a2,
                                          k_k, k_a, r_k, ln_w, ln_b)
    q = apply_rope(q.reshape(B, T, ATTN_Q_HEADS, HEAD_DIM), pos)
    k = apply_rope(k.reshape(B, T, ATTN_KV_HEADS, HEAD_DIM), pos)
    v = v.reshape(B, T, ATTN_KV_HEADS, HEAD_DIM)
    if kbuf is None:
        o_a = swa_prompt(q, k, v, sinks, pos)
        n_keep = min(WINDOW, T)
        k_new, v_new = k[:, -n_keep:], v[:, -n_keep:]
    else:
        o_a, k_new, v_new = swa_sample(q, k, v, kbuf, vbuf, sinks, pos)
    mixed = jnp.concatenate([o_r * jax.nn.silu(gate_r), o_a * jax.nn.silu(gate_a)], axis=-1)
    h = h + mixed @ w_out
    gate = jax.nn.sigmoid(rmsnorm(h, g_ple) @ w_pg)
    h = h + gate * (p @ w_pp)
    return h, S_new.astype(h.dtype), shift_new, k_new, v_new


def setup_inputs(seed: int = 0) -> dict:
    key = jax.random.key(seed)
    ks = jax.random.split(key, 32)
    f32 = jnp.float32
    nrm = lambda k, shape, s: jax.random.normal(k, shape, f32) * s
    D, R = D_MODEL, RWKV_WIDTH
    WIN = min(WINDOW, PAST_LEN)
    return {
        'x_prompt': nrm(ks[0], (BATCH, SEQ, D), 1.0),
        'x_sample': nrm(ks[1], (DEC_BATCH, DEC_SEQ, D), 1.0),
        'state_rwkv_wkv': nrm(ks[2], (DEPTH, DEC_BATCH, RWKV_HEADS, HEAD_DIM, HEAD_DIM), 0.3),
        'state_rwkv_shift': nrm(ks[3], (DEPTH, DEC_BATCH, SHIFT_DIM), 1.0),
        'cache_swa_k': nrm(ks[4], (DEPTH, DEC_BATCH, WIN, ATTN_KV_HEADS, HEAD_DIM), 1.0),
        'cache_swa_v': nrm(ks[5], (DEPTH, DEC_BATCH, WIN, ATTN_KV_HEADS, HEAD_DIM), 1.0),
        'p_prompt': nrm(ks[6], (DEPTH, BATCH, SEQ, PLE_DIM), 1.0),
        'p_sample': nrm(ks[7], (DEPTH, DEC_BATCH, DEC_SEQ, PLE_DIM), 1.0),
        'g_norm': 1.0 + nrm(ks[8], (DEPTH, D), 0.05),
        'w_in': nrm(ks[9], (DEPTH, D, IN_DIM), D ** -0.5),
        'mu_shift': jax.random.uniform(ks[10], (DEPTH, SHIFT_DIM), f32, 0.1, 0.9),
        'w0': jax.random.uniform(ks[11], (DEPTH, R), f32, -4.0, 1.0),
        'w2': nrm(ks[12], (DEPTH, DECAY_LORA, R), 0.1),
        'a0': nrm(ks[13], (DEPTH, R), 0.1),
        'a2': nrm(ks[14], (DEPTH, ICLR_LORA, R), 0.1),
        'k_k': 0.85 + nrm(ks[15], (DEPTH, R), 0.05),
        'k_a': 1.0 + nrm(ks[16], (DEPTH, R), 0.05),
        'r_k': nrm(ks[17], (DEPTH, R), 0.1),
        'ln_w': 1.0 + nrm(ks[18], (DEPTH, R), 0.05),
        'ln_b': nrm(ks[19], (DEPTH, R), 0.02),
        'sinks': nrm(ks[20], (DEPTH, ATTN_Q_HEADS), 0.5),
        'w_out': nrm(ks[21], (DEPTH, D, D), D ** -0.5),
        'g_ple': 1.0 + nrm(ks[22], (DEPTH, D), 0.05),
        'w_ple_gate': nrm(ks[23], (DEPTH, D, D), D ** -0.5),
        'w_ple_proj': nrm(ks[24], (DEPTH, PLE_DIM, D), 0.5 * PLE_DIM ** -0.5),
        'g_final': 1.0 + nrm(ks[25], (D,), 0.05),
    }


def reference(x_prompt, x_sample, state_rwkv_wkv, state_rwkv_shift, cache_swa_k, cache_swa_v,
              p_prompt, p_sample, g_norm, w_in, mu_shift, w0, w2, a0, a2, k_k, k_a, r_k,
              ln_w, ln_b, sinks, w_out, g_ple, w_ple_gate, w_ple_proj, g_final):
    pos_p = jnp.arange(x_prompt.shape[1])
    pos_s = PAST_LEN + jnp.arange(x_sample.shape[1])
    hp, hs = x_prompt, x_sample
    Bp = x_prompt.shape[0]
    wkv_p, sh_p, k_p, v_p = [], [], [], []
    wkv_s, sh_s, k_s, v_s = [], [], [], []
    for l in range(DEPTH):
        wl = (g_norm[l], w_in[l], mu_shift[l], w0[l], w2[l], a0[l], a2[l], k_k[l], k_a[l], r_k[l],
              ln_w[l], ln_b[l], sinks[l], w_out[l], g_ple[l], w_ple_gate[l], w_ple_proj[l])
        S0p = jnp.zeros((Bp, RWKV_HEADS, HEAD_DIM, HEAD_DIM), jnp.float32)
        sh0p = jnp.zeros((Bp, SHIFT_DIM), hp.dtype)
        hp, S1, s1, k1, v1 = mixer_layer(hp, p_prompt[l], pos_p, S0p, sh0p, None, None, *wl)
        hs, S2, s2, k2, v2 = mixer_layer(hs, p_sample[l], pos_s, state_rwkv_wkv[l], state_rwkv_shift[l],
                                         cache_swa_k[l], cache_swa_v[l], *wl)
        wkv_p.append(S1); sh_p.append(s1); k_p.append(k1); v_p.append(v1)
        wkv_s.append(S2); sh_s.append(s2); k_s.append(k2); v_s.append(v2)
    y_prompt = rmsnorm(hp, g_final)
    y_sample = rmsnorm(hs, g_final)
    return (y_prompt, y_sample,
            jnp.stack(wkv_p), jnp.stack(sh_p), jnp.stack(k_p), jnp.stack(v_p),
            jnp.stack(wkv_s), jnp.stack(sh_s), jnp.stack(k_s), jnp.stack(v_s))
```

```python
import contextlib
import numpy as np
import concourse.bass as bass
import concourse.mybir as mybir
from concourse.bass_utils import run_bass_kernel_spmd

F32 = mybir.dt.float32
BF16 = mybir.dt.bfloat16
AF = mybir.ActivationFunctionType
ALU = mybir.AluOpType
AX = mybir.AxisListType

ENGS = ("pe", "act", "dve", "pool", "sp")
D_MODEL = 1024
SHIFT_DIM = 1664
IN_DIM = 3456
PAST_LEN = 16384
C0 = float(np.exp(-0.5))
NORM_EPS = 1e-6
GN_EPS = 64e-5
NCONST = 2440
NV = 53


def _esize(dt):
    return 2 if dt == BF16 else 4


class Sched:
    def __init__(self, nc, n_dma_sems=48):
        self.nc = nc
        self.ops = {e: [] for e in ENGS}
        self.cnt = {e: 0 for e in ENGS}
        self.waited = {e: {} for e in ENGS}
        self.recs = {}
        self.n_dma_sems = n_dma_sems
        self.dma_vals = [0] * n_dma_sems
        self.dma_rr = 0
        self.out_dma = []

    @staticmethod
    def fp(ap):
        es = _esize(ap.dtype)
        pat = ap.ap
        pstep, pn = pat[0]
        off = ap.offset
        p0 = off // pstep
        col = off % pstep
        lo = col
        hi = col
        for s, c in pat[1:]:
            if s >= 0:
                hi += s * (c - 1)
            else:
                lo += s * (c - 1)
        return (ap.name, p0, p0 + pn, lo * es, (hi + 1) * es)

    @staticmethod
    def _ovl(a, b):
        return a[1] < b[2] and b[1] < a[2] and a[3] < b[4] and b[3] < a[4]

    @staticmethod
    def _covers(a, b):
        return a[1] <= b[1] and a[2] >= b[2] and a[3] <= b[3] and a[4] >= b[4]

    @staticmethod
    def _bank(f):
        return (f[0], 0, 128, (f[3] // 2048) * 2048, ((f[4] + 2047) // 2048) * 2048)

    def _deps(self, eng, reads, writes):
        deps = {}

        def add(r):
            if deps.get(r[1], 0) < r[2]:
                deps[r[1]] = r[2]
        for f in reads:
            ps = f[0] == "ps"
            fb = self._bank(f) if ps else None
            for r in self.recs.get(f[0], ()):
                if r[4] and self._ovl(f, r[0]):
                    add(r)
                elif ps and r[3] != eng and self._ovl(fb, r[5]):
                    add(r)
        for f in writes:
            ps = f[0] == "ps"
            fb = self._bank(f) if ps else None
            for r in self.recs.get(f[0], ()):
                if r[3] == eng and eng == "pe":
                    continue
                if self._ovl(f, r[0]) or (ps and r[3] != eng and self._ovl(fb, r[5])):
                    add(r)
        return deps

    def _record(self, eng, sk, val, reads, writes):
        for f in writes:
            lst = self.recs.setdefault(f[0], [])
            lst[:] = [r for r in lst if not self._covers(f, r[0])]
            lst.append((f, sk, val, eng, True, self._bank(f) if f[0] == "ps" else None))
        for f in reads:
            lst = self.recs.setdefault(f[0], [])
            if f[0] == "ps":
                fb = self._bank(f)
                lst[:] = [r for r in lst if r[4] or r[3] != eng or r[5] != fb]
                lst.append((f, sk, val, eng, False, fb))
                continue
            if eng != "dma":
                lst[:] = [r for r in lst if r[4] or r[3] != eng or r[0] != f]
            lst.append((f, sk, val, eng, False, None))

    def _emit_waits(self, eng, deps):
        for sk, val in deps.items():
            if self.waited[eng].get(sk, 0) >= val:
                continue
            self.waited[eng][sk] = val
            self.ops[eng].append(("wait", sk, val))

    def op(self, eng, fn, outs, ins):
        reads = [self.fp(a) for a in ins]
        writes = [self.fp(a) for a in outs]
        deps = self._deps(eng, reads, writes)
        self._emit_waits(eng, deps)
        self.cnt[eng] += 1
        self.ops[eng].append(("op", fn))
        self._record(eng, ("e", eng), self.cnt[eng], reads, writes)

    def dma(self, out, in_, q="sp", out_onchip=True, in_onchip=False, is_output=False, **kw):
        reads = [self.fp(in_)] if in_onchip else []
        writes = [self.fp(out)] if out_onchip else []
        deps = self._deps("dma", reads, writes)
        s = self.dma_rr
        self.dma_rr = (self.dma_rr + 1) % self.n_dma_sems
        if self.dma_vals[s] > 0:
            deps[("d", s)] = max(deps.get(("d", s), 0), self.dma_vals[s])
        self._emit_waits(q, deps)
        self.dma_vals[s] += 16
        val = self.dma_vals[s]
        self.ops[q].append(("dma", out, in_, s, kw))
        self._record("dma", ("d", s), val, reads, writes)
        if is_output:
            self.out_dma.append((s, val))

    def finish(self):
        nc = self.nc
        with contextlib.ExitStack() as st:
            esem = {e: st.enter_context(nc.semaphore("sem_" + e)) for e in ENGS if e != "sp"}
            dsem = [st.enter_context(nc.semaphore("dsem%d" % i)) for i in range(self.n_dma_sems)]
            for s, val in self.out_dma:
                if self.waited["sp"].get(("d", s), 0) < val:
                    self.waited["sp"][("d", s)] = val
                    self.ops["sp"].append(("wait", ("d", s), val))
            block = st.enter_context(nc.Block())

            def semof(sk):
                return esem[sk[1]] if sk[0] == "e" else dsem[sk[1]]

            def run(eng_name, e):
                for item in self.ops[eng_name]:
                    if item[0] == "wait":
                        e.wait_ge(semof(item[1]), item[2])
                    elif item[0] == "op":
                        item[1](e).then_inc(esem[eng_name], 1)
                    else:
                        _, out, in_, s, kw = item
                        e.dma_start(out=out, in_=in_, **kw).then_inc(dsem[s], 16)

            @block.tensor
            def _(e):
                run("pe", e)

            @block.scalar
            def _(e):
                run("act", e)

            @block.vector
            def _(e):
                run("dve", e)

            @block.gpsimd
            def _(e):
                run("pool", e)

            @block.sync
            def _(e):
                run("sp", e)


class Arena:
    def __init__(self, A, base, limit):
        self.A, self.off, self.limit, self.hi = A, base, limit, base

    def take(self, n):
        ap = self.A[:, self.off:self.off + n]
        self.off += n
        assert self.off <= self.limit, ("arena overflow", self.off, self.limit)
        return ap

    def sub(self, n):
        base = self.off
        self.off += n
        assert self.off <= self.limit, ("arena overflow", self.off, self.limit)
        return base, base + n


def bfv(ap):
    return ap.bitcast(BF16)


def build(nseq, S, nsb, dbg=None):
    NT = S // 128
    NPOS = S + 128
    nc = bass.Bass("TRN2", target_bir_lowering=False, dynamic_dma_scratch_size=256)

    def din(name, shape):
        return nc.dram_tensor(name, shape, F32, kind="ExternalInput")

    def dout(name, shape):
        return nc.dram_tensor(name, shape, F32, kind="ExternalOutput")

    xp = din("xp", [nseq * S, 1024]); pp = din("pp", [nseq * S, 256])
    xsm = din("xsm", [nsb * 8, 1024]); psm = din("psm", [nsb * 8, 256])
    wkv0 = din("wkv0", [nsb, 8, 64, 64]); sh0 = din("sh0", [nsb, SHIFT_DIM])
    ck = din("ck", [nsb, 128, 128]); cv = din("cv", [nsb, 128, 128])
    w_in = din("w_in", [1024, IN_DIM]); w_out = din("w_out", [1024, 1024])
    w_pg = din("w_pg", [1024, 1024]); w_pp = din("w_pp", [256, 1024])
    w2 = din("w2", [64, 512]); a2 = din("a2", [64, 512])
    pvec_d = din("pvec", [128, NV]); lnw_d = din("lnw", [1, 512]); lnb_d = din("lnb", [1, 512])
    gfin_d = din("gfin", [1, 1024]); consts_d = din("consts", [128, NCONST])
    rope_d = din("rope", [2, 128, NPOS])

    y_p = dout("y_p", [nseq * S, 1024]); y_s = dout("y_s", [nsb * 8, 1024])
    wkv_p = dout("wkv_p", [nseq, 8, 64, 64]); sh_p = dout("sh_p", [nseq, SHIFT_DIM])
    k_p = dout("k_p", [nseq, 128, 128]); v_p = dout("v_p", [nseq, 128, 128])
    wkv_s = dout("wkv_s", [nsb, 8, 64, 64]); sh_s = dout("sh_s", [nsb, SHIFT_DIM])
    k_s = dout("k_s", [nsb, 128, 128]); v_s = dout("v_s", [nsb, 128, 128])
    dbg_d = dout("dbg", [128, 8192]) if dbg is not None else None
    dbg_off = [0]

    AW = 49088
    with contextlib.ExitStack() as st:
        A = st.enter_context(nc.sbuf_tensor("arena", [128, AW], F32))
        PS = st.enter_context(nc.psum_tensor("ps", [128, 4096], F32))
        Sd = Sched(nc)
        ar = Arena(A, 0, AW)

        def tt(eng, out, in0, in1, op):
            Sd.op(eng, lambda e: e.tensor_tensor(out=out, in0=in0, in1=in1, op=op), [out], [in0, in1])

        def ts(eng, out, in0, s1, s2, op0, op1=None):
            ins = [in0] + [s for s in (s1, s2) if s is not None and not isinstance(s, (int, float))]
            if op1 is None:
                Sd.op(eng, lambda e: e.tensor_scalar(out=out, in0=in0, scalar1=s1, scalar2=None, op0=op0), [out], ins)
            else:
                Sd.op(eng, lambda e: e.tensor_scalar(out=out, in0=in0, scalar1=s1, scalar2=s2, op0=op0, op1=op1), [out], ins)

        def stt(out, in0, scalar, in1, op0, op1):
            ins = [in0, in1] + ([scalar] if not isinstance(scalar, (int, float)) else [])
            Sd.op("dve", lambda e: e.scalar_tensor_tensor(out=out, in0=in0, scalar=scalar, in1=in1, op0=op0, op1=op1), [out], ins)

        def act(out, in_, func, bias=None, scale=None, accum=None):
            ins = [in_]
            kw = {}
            if bias is not None:
                kw["bias"] = bias
                if not isinstance(bias, (int, float)):
                    ins.append(bias)
            if scale is not None:
                kw["scale"] = scale
                if not isinstance(scale, (int, float)):
                    ins.append(scale)
            outs = [out]
            if accum is not None:
                kw["accum_out"] = accum
                outs.append(accum)
            Sd.op("act", lambda e: e.activation(out=out, in_=in_, func=func, **kw), outs, ins)

        def cp(eng, out, in_):
            if eng == "act":
                act(out, in_, AF.Copy)
            else:
                Sd.op(eng, lambda e: e.tensor_copy(out=out, in_=in_), [out], [in_])

        def mm(out, lhsT, rhs, start=True, stop=True):
            Sd.op("pe", lambda e: e.matmul(out, lhsT=lhsT, rhs=rhs, start=start, stop=stop), [out], [lhsT, rhs])

        def ms(eng, out, val):
            Sd.op(eng, lambda e: e.memset(out, val), [out], [])

        def apx(base, tail, off=0, parts=None):
            p = list(base.ap[0])
            if parts is not None:
                p[1] = parts
            return bass.AP(base.tensor, base.offset + off, [p] + [list(t) for t in tail])

        def bank(i, lo=0, hi=512):
            return PS[:, i * 512 + lo:i * 512 + hi]

        def dump(name, ap, parts=128):
            if dbg is None or name not in dbg or dbg[name] is not None:
                return
            n = 1
            for s in ap.shape[1:]:
                n *= s
            if ap.dtype == BF16:
                tmp = dbgtmp[0:parts, 0:n]
                cp("dve", tmp, ap.rearrange("p a b -> p (a b)") if len(ap.shape) == 3 else ap)
                src = tmp
            else:
                src = ap
            dbg[name] = (dbg_off[0], parts, tuple(ap.shape[1:]))
            dst = dbg_d.ap()[0:parts, dbg_off[0]:dbg_off[0] + n]
            if len(ap.shape) == 3 and ap.dtype != BF16:
                dst = dst.rearrange("p (a b) -> p a b", b=ap.shape[2])
            Sd.dma(dst, src, out_onchip=False, in_onchip=True, is_output=True)
            dbg_off[0] += n

        win_b = bfv(ar.take(8 * IN_DIM // 2)).rearrange("p (k n) -> p k n", n=IN_DIM)
        wout_b = bfv(ar.take(4096)).rearrange("p (k n) -> p k n", n=1024)
        wpg_b = bfv(ar.take(4096)).rearrange("p (k n) -> p k n", n=1024)
        wpp_b = bfv(ar.take(1024)).rearrange("p (k n) -> p k n", n=1024)
        lora_b = bfv(ar.take(512)).rearrange("p (j n) -> p j n", n=512)
        consts = ar.take(NCONST)
        ident = consts[:, 0:128]
        mset = consts[:, 128:640].rearrange("p (a b) -> p a b", b=128)
        m_le = consts[:, 256:384]
        nmgt = consts[:, 640:768]
        m_gt = consts[:, 768:896]
        rot = consts[:, 1024:1152]
        rsp = consts[:, 1152:1664]
        rss = consts[:, 1664:2176]
        pvec = ar.take(NV + 27)
        mu_c = pvec[:, 0:13]; kk_c = pvec[:, 13:17]; ka_c = pvec[:, 17:21]; rk_c = pvec[:, 21:25]
        w0_c = pvec[:, 25:29]; a0_c = pvec[:, 29:33]; gn_c = pvec[:, 33:41]; gp_c = pvec[:, 41:49]
        sk_c = pvec[:, 49:53]; omka_c = pvec[:, 53:57]; es_c = pvec[:, 57:61]; hw0_c = pvec[:, 61:65]; ha0_c = pvec[:, 65:69]; mhalf_c = pvec[:, 69:77]
        lnw_bc = ar.take(512); lnb_bc = ar.take(512); gfin_bc = ar.take(1024)
        cb = bfv(ar.take(256))
        bones_b = cb[:, 0:128]; hsel_b = cb[:, 128:130]; opad_b = cb[:, 136:392].rearrange("p (a b) -> p a b", b=128)
        stat = ar.take(64)
        dbgtmp = ar.take(1024) if dbg is not None else None

        xin = ar.take(1024); xin2 = ar.take(1024); xins = [xin, xin2]; pin = ar.take(256); cs = ar.take(256).rearrange("p (a b) -> p a b", b=128)
        sgr = ar.take(512).rearrange("p (a b) -> p a b", b=128)
        sga = ar.take(512).rearrange("p (a b) -> p a b", b=128)
        qf = ar.take(512).rearrange("p (a b) -> p a b", b=128)
        kvf = ar.take(256).rearrange("p (a b) -> p a b", b=128)
        xT = bfv(ar.take(512)).rearrange("p (k t) -> p k t", t=128)
        mixT = bfv(ar.take(512)).rearrange("p (k t) -> p k t", t=128)
        qr_b = bfv(ar.take(256)).rearrange("p (k t) -> p k t", t=128)
        kr32 = ar.take(128)
        krpad = [bfv(ar.take(128)).rearrange("p (a t) -> p a t", t=128) for _ in range(2)]
        vpad = [bfv(ar.take(128)).rearrange("p (a t) -> p a t", t=128) for _ in range(2)]
        vat32 = ar.take(128); kat32 = ar.take(128)
        Hst = ar.take(256).rearrange("p (j v) -> p j v", v=64)
        Hbf = bfv(ar.take(128)).rearrange("p (j v) -> p j v", v=64)
        tmpH = ar.take(256).rearrange("p (j v) -> p j v", v=64)
        ARpad = bfv(ar.take(1024)).rearrange("p (c a q t) -> p c a q t", c=4, a=2, q=2)
        BKpad = bfv(ar.take(1024)).rearrange("p (c a q t) -> p c a q t", c=4, a=2, q=2)
        khat_tm = bfv(ar.take(256)); bhat_tm = bfv(ar.take(256)); v_tm = bfv(ar.take(256)); v_tm32 = ar.take(512)
        cbon = ar.take(8)
        zcarry = ar.take(16)
        eLinc = ar.take(512)
        ytm = ar.take(512); Xb = bfv(ar.take(128))

        u1b, u1e = ar.sub(1872 + 1024)
        zbuf_flat = A[:, u1b:u1b + 1872]
        zs = A[:, u1b + 1872:u1b + 1872 + 1664].rearrange("p (c t) -> p c t", t=128)
        ar.sub(1664 - 1024)
        a1 = Arena(A, u1b, u1e)
        Aev = bfv(a1.take(1024)).rearrange("p (h q t) -> p h q t", h=4, q=4)
        N0b = bfv(a1.take(256)).rearrange("p (h t) -> p h t", t=128)
        Pb = [bfv(a1.take(256)).rearrange("p (h t) -> p h t", t=128) for _ in range(2)]
        PTb = [bfv(a1.take(256)).rearrange("p (h t) -> p h t", t=128) for _ in range(2)]
        Xf = a1.take(256); Uneg = bfv(a1.take(256))

        u2b, u2e = ar.sub(512 * 10 + 64 + 256)
        assert u2b == u1e + 640
        a2_ = Arena(A, u2b, u2e)
        Wk = [a2_.take(512) for _ in range(10)]
        L12 = bfv(a2_.take(64)); sqb = bfv(a2_.take(256))
        xs32 = A[:, u2b:u2b + 1024]
        PTp = bfv(A[:, u1b:u1b + 512]).rearrange("p (h t) -> p h t", t=128)
        PTc = bfv(A[:, u1b + 512:u1b + 1024]).rearrange("p (h t) -> p h t", t=128)
        rden = A[:, u1b + 1024:u1b + 1536]; g2 = rden
        tmpq_b = A[:, u1b + 1024:u1b + 1536]; tmpk_b = A[:, u1b + 1536:u1b + 1664]
        gn1 = Wk[4]; gn2 = Wk[5]
        yout = A[:, u2b + 4 * 512:u2b + 6 * 512]
        hn32 = A[:, u2b + 6 * 512:u2b + 8 * 512]
        hnT = bfv(Wk[8]).rearrange("p (k t) -> p k t", t=128)
        pT = bfv(Wk[9][:, 0:128]).rearrange("p (k t) -> p k t", t=128)
        shs = A[0:nsb, u2b + 1024:u2b + 1024 + SHIFT_DIM] if nsb > 0 else None
        ws_st = Wk[6][0:64, :]; wo_st = Wk[7][0:64, :]; yall = Wk[8]
        kc_st = Wk[8][:, 0:128]; vc_st = Wk[8][:, 128:256]
        stage = [A[:, u1b:u1b + IN_DIM], A[:, u1b + IN_DIM:u1b + 2 * IN_DIM]]
        assert u1b + 2 * IN_DIM <= ar.off
        print("arena words used", ar.off, "of", AW)

        Sd.dma(consts, consts_d.ap())
        Sd.dma(pvec[:, 0:NV], pvec_d.ap())
        Sd.dma(lnw_bc, bass.AP(lnw_d, 0, [[0, 128], [1, 512]]))
        Sd.dma(lnb_bc, bass.AP(lnb_d, 0, [[0, 128], [1, 512]]))
        Sd.dma(gfin_bc, bass.AP(gfin_d, 0, [[0, 128], [1, 1024]]))
        cp("dve", bones_b, consts[:, 896:1024])
        cp("dve", hsel_b, consts[:, 2176:2178])
        cp("dve", cb[:, 136:392], consts[:, 2178:2434])
        negprev = consts[:, 896:1024]; negcur = consts[:, 2178:2306]
        ts("dve", negprev, m_le, -30000.0, None, ALU.mult)
        ts("dve", negcur, m_gt, -30000.0, None, ALU.mult)
        ts("dve", omka_c, ka_c, -1.0, 1.0, ALU.mult, ALU.add)
        act(es_c, sk_c, AF.Exp)
        ts("dve", hw0_c, w0_c, 0.5, None, ALU.mult)
        ts("dve", ha0_c, a0_c, 0.5, None, ALU.mult)
        ms("dve", mhalf_c, -0.5)
        lst = stage[0][:, 0:1024].rearrange("p (j n) -> p j n", n=512)
        ms("pool", stage[0][:, 0:1024], 0.0)
        Sd.dma(lst[0:64, 0, :], w2.ap())
        Sd.dma(lst[64:128, 1, :], a2.ap())
        cp("dve", lora_b, lst)
        si = 0
        for k in range(8):
            sg = stage[si % 2]; si += 1
            Sd.dma(sg, w_in.ap()[k * 128:(k + 1) * 128, :])
            if k % 2 == 0:
                act(win_b[:, k, :], sg, AF.Copy, scale=gn_c[:, k:k + 1])
            else:
                ts("dve", win_b[:, k, :], sg, gn_c[:, k:k + 1], None, ALU.mult)
        for (wd, wb, gc, nk) in ((w_out, wout_b, None, 8), (w_pg, wpg_b, gp_c, 8), (w_pp, wpp_b, None, 2)):
            for k0 in range(0, nk, 2):
                sg = stage[si % 2]; si += 1
                sgv = sg[:, 0:2048].rearrange("p (k n) -> p k n", n=1024)
                Sd.dma(sgv, wd.ap()[k0 * 128:(k0 + 2) * 128, :].rearrange("(k p) n -> p k n", p=128))
                for kk_ in range(2):
                    k = k0 + kk_
                    if gc is None:
                        sc_ = 0.5 if wd is w_out else 1.0
                        if kk_ == 0:
                            act(wb[:, k, :], sgv[:, kk_, :], AF.Copy, scale=sc_)
                        else:
                            ts("dve", wb[:, k, :], sgv[:, kk_, :], sc_, None, ALU.mult)
                    elif kk_ == 0:
                        act(wb[:, k, :], sgv[:, kk_, :], AF.Copy, scale=gc[:, k:k + 1])
                    else:
                        ts("dve", wb[:, k, :], sgv[:, kk_, :], gc[:, k:k + 1], None, ALU.mult)
        ms("pool", ARpad.rearrange("p c a q t -> p (c a q t)"), 0.0)
        ms("pool", BKpad.rearrange("p c a q t -> p (c a q t)"), 0.0)
        for i in range(2):
            ms("pool", krpad[i].rearrange("p a t -> p (a t)"), 0.0)
            ms("pool", vpad[i].rearrange("p a t -> p (a t)"), 0.0)

        gbank = [0]
        marks = []

        def mark(lbl):
            marks.append((lbl, dict(Sd.cnt)))

        def gb():
            b = gbank[0]
            gbank[0] = (gbank[0] + 1) % 3
            return b

        def rsqrt_col(dst, src, mult, eps):
            n_ = dst.shape[1]
            p0_ = dst.offset // dst.ap[0][0]
            pw = mhalf_c[p0_:p0_ + dst.shape[0], 0:n_]
            ts("dve", dst, src, mult, eps, ALU.mult, ALU.add)
            tt("pool", dst, dst, pw, ALU.pow)

        def scan_pass(t0, C, lv):
            HG = 4 if C == 128 else 8
            if HG == 4:
                Aev_, N0_, P_, PT_, Xf_, Xb_ = Aev, N0b, Pb, PTb, Xf, Xb
            else:
                Aev_ = Aev.rearrange("p h q t -> p (h q t)").rearrange("p (h q t) -> p h q t", h=8, q=4)
                N0_ = N0b.rearrange("p h t -> p (h t)").rearrange("p (h t) -> p h t", h=8)
                P_ = [x.rearrange("p h t -> p (h t)").rearrange("p (h t) -> p h t", h=8) for x in Pb]
                PT_ = [x.rearrange("p h t -> p (h t)").rearrange("p (h t) -> p h t", h=8) for x in PTb]
                Xf_ = gn2; Xb_ = bfv(gn1[:, 0:256])
            HW = HG * 64

            def abank(hl):
                return bank(hl)[0:C, 0:4 * C] if C == 128 else bank(3)[0:C, hl * 4 * C:(hl + 1) * 4 * C]
            n0bank = bank(4)[0:C, 0:HG * C] if C == 128 else bank(3)[0:C, 256:256 + HG * C]
            for g in range(8 // HG):
                heads = [(g * (HG // 2) + jj, par) for jj in range(HG // 2) for par in range(2)]
                j0 = g * (HG // 2); nj = HG // 2
                for hl, (j, par) in enumerate(heads):
                    rhs_ar = ARpad[:, j, par, :, t0:t0 + C]
                    mm(abank(hl)[:, 0:2 * C].rearrange("p (q t) -> p q t", t=C), BKpad[:, j, 0, 0, t0:t0 + C], rhs_ar)
                    mm(abank(hl)[:, 2 * C:4 * C].rearrange("p (q t) -> p q t", t=C), BKpad[:, j, 0, 1, t0:t0 + C], rhs_ar)
                for hl, (j, par) in enumerate(heads):
                    mm(n0bank[:, hl * C:(hl + 1) * C], ARpad[:, j, par, 0, t0:t0 + C], BKpad[:, j, 0, 0, t0:t0 + C])
                if C == 128:
                    for hl in range(HG):
                        tt("dve", Aev_[0:C, hl, :, 0:C], abank(hl).rearrange("p (q t) -> p q t", t=C), mset[0:C, :, 0:C], ALU.mult)
                else:
                    for q in range(4):
                        tt("dve", Aev_[0:C, :, q, 0:C], bank(3)[0:C, 0:HG * 4 * C].rearrange("p (h q t) -> p h q t", h=HG, q=4)[:, :, q, :],
                           apx(mset[0:C, q, 0:C], [[0, HG], [1, C]]), ALU.mult)
                tt("dve", N0_[0:C, :, 0:C], n0bank.rearrange("p (h t) -> p h t", t=C),
                   apx(nmgt[0:C, 0:C], [[0, HG], [1, C]]), ALU.mult)
                for hl, (j, par) in enumerate(heads):
                    h = 2 * j + par
                    mm(bank(5)[0:C, hl * 64:(hl + 1) * 64], ARpad[:, j, par, 0, t0:t0 + C], Hbf[:, j, :], True, False)
                    mm(bank(5)[0:C, hl * 64:(hl + 1) * 64], Aev_[0:C, hl, 2, 0:C], v_tm[0:C, h * 64:(h + 1) * 64], False, True)
                cp("act", Xf_[0:C, 0:HW], bank(5)[0:C, 0:HW])
                cp("dve", Xb_[0:C, 0:HW], bank(5)[0:C, 0:HW])
                PTcur = [Aev_[0:C, hl, 0, 0:C] for hl in range(HG)]
                Pcur = [N0_[0:C, hl, 0:C] for hl in range(HG)]
                hh_ = HG // 2
                subs = [(0, hh_, 5, 6, 7), (hh_, HG, 0, 1, 2)] if C == 128 else [(0, HG, 5, 6, 7)]
                for i in range(lv):
                    for (h0, h1, bru, bp_, bpt) in subs:
                        for hl in range(h0, h1):
                            mm(bank(bru)[0:C, hl * 64:(hl + 1) * 64], PTcur[hl], Xb_[0:C, hl * 64:(hl + 1) * 64])
                        if i < lv - 2:
                            for hl in range(h0, h1):
                                mm(bank(bp_)[0:C, hl * C:(hl + 1) * C], PTcur[hl], Pcur[hl])
                        if i < lv - 1:
                            for hl in range(h0, h1):
                                mm(bank(bpt)[0:C, hl * C:(hl + 1) * C], Pcur[hl], PTcur[hl])
                    pn = P_[i % 2]; ptn = PT_[i % 2]
                    for (h0, h1, bru, bp_, bpt) in subs:
                        cs_ = slice(h0 * 64, h1 * 64)
                        tt("dve", Xf_[0:C, cs_], Xf_[0:C, cs_], bank(bru)[0:C, cs_], ALU.add)
                        if i < lv - 1:
                            cp("act", Xb_[0:C, cs_], Xf_[0:C, cs_])
                            if i < lv - 2:
                                cp("act", pn[0:C, h0:h1, 0:C], bank(bp_)[0:C, h0 * C:h1 * C].rearrange("p (h t) -> p h t", t=C))
                            cp("dve", ptn[0:C, h0:h1, 0:C], bank(bpt)[0:C, h0 * C:h1 * C].rearrange("p (h t) -> p h t", t=C))
                    if i < lv - 1:
                        Pcur = [pn[0:C, hl, 0:C] for hl in range(HG)]
                        PTcur = [ptn[0:C, hl, 0:C] for hl in range(HG)]
                act(Uneg[0:C, g * HW:(g + 1) * HW], Xf_[0:C, 0:HW], AF.Copy, scale=-1.0)
                for hl, (j, par) in enumerate(heads):
                    h = 2 * j + par
                    o_ = bank(6)[0:C, hl * 64:(hl + 1) * 64]
                    mm(o_, ARpad[:, j, par, 1, t0:t0 + C], Hbf[:, j, :], True, False)
                    mm(o_, Aev_[0:C, hl, 3, 0:C], v_tm[0:C, h * 64:(h + 1) * 64], False, False)
                    mm(o_, Aev_[0:C, hl, 1, 0:C], Uneg[0:C, h * 64:(h + 1) * 64], False, True)
                cp("act", ytm[0:C, g * HW:(g + 1) * HW], bank(6)[0:C, 0:HW])
                for jj in range(nj):
                    j = j0 + jj
                    o_ = bank(7)[:, jj * 128:(jj + 1) * 128]
                    mm(o_, khat_tm[0:C, j * 128:(j + 1) * 128], v_tm[0:C, j * 128:(j + 1) * 128], True, False)
                    mm(o_, bhat_tm[0:C, j * 128:(j + 1) * 128], Uneg[0:C, j * 128:(j + 1) * 128], False, True)
                wc = apx(eLinc[:, t0 + C - 1:t0 + C], [[128, nj], [0, 64]], off=j0 * 128)
                tt("pool", tmpH[:, j0:j0 + nj, :], Hst[:, j0:j0 + nj, :], wc, ALU.mult)
                for par in range(2):
                    pr = slice(par * 64, par * 64 + 64)
                    src = bank(7)[pr, 0:nj * 128].rearrange("p (j v) -> p j v", v=128)[:, :, par * 64:par * 64 + 64]
                    tt("dve", Hst[pr, j0:j0 + nj, :], tmpH[pr, j0:j0 + nj, :], src, ALU.add)
            cp("act", Hbf, Hst)

        def tm_prep(khat, bhat, t0, C, want32=True, which=(0, 1, 2)):
            jobs = ((khat, khat_tm, None), (bhat, bhat_tm, None), (zs[:, 8:12, :], v_tm, v_tm32 if want32 else None))
            for src, dst, dst32 in [jobs[w] for w in which]:
                b_ = gb()
                for c in range(4):
                    Sd.op("pe", lambda e, o=bank(b_)[0:C, c * 128:(c + 1) * 128], i=src[:, c, t0:t0 + C]: e.transpose(out=o, in_=i, identity=ident),
                          [bank(b_)[0:C, c * 128:(c + 1) * 128]], [src[:, c, t0:t0 + C], ident])
                if dst32 is not None:
                    cp("act", dst32[0:C, :], bank(b_)[0:C, :])
                    cp("dve", dst[0:C, :], bank(b_)[0:C, :])
                else:
                    cp("act" if dst is khat_tm else "dve", dst[0:C, :], bank(b_)[0:C, :])

        def gn_bonus(C):
            v3 = v_tm32[0:C, :].rearrange("p (h v) -> p h v", v=64)
            tt("pool", v3, v3, apx(cbon[0:C, 0:8], [[1, 8], [0, 64]]), ALU.mult)
            tt("pool", v_tm32[0:C, :], v_tm32[0:C, :], lnb_bc[0:C, :], ALU.add)

        def gn_out(t0, C, mrbank, ysrc=None):
            ysrc = ytm if ysrc is None else ysrc
            y3 = ysrc[0:C, :].rearrange("p (h v) -> p h v", v=64)
            s1 = stat[0:C, 0:8]; s2 = stat[0:C, 8:16]; msq = stat[0:C, 24:32]
            bc8 = lambda s: apx(s, [[1, 8], [0, 64]])
            sq = gn2[0:C, :]
            act(sq, ysrc[0:C, :], AF.Square)
            Sd.op("dve", lambda e: e.tensor_reduce(out=s1, in_=y3, axis=AX.X, op=ALU.add), [s1], [y3])
            ts("dve", s1, s1, 1.0 / 64, None, ALU.mult)
            Sd.op("dve", lambda e: e.tensor_reduce(out=s2, in_=sq.rearrange("p (h v) -> p h v", v=64), axis=AX.X, op=ALU.add), [s2], [sq])
            ts("dve", s2, s2, 1.0 / 64, GN_EPS, ALU.mult, ALU.add)
            tt("dve", msq, s1, s1, ALU.mult)
            tt("dve", s2, s2, msq, ALU.subtract)
            tt("pool", s2, s2, mhalf_c[0:C, 0:8], ALU.pow)
            yc = gn1[0:C, :].rearrange("p (h v) -> p h v", v=64)
            tt("dve", yc, y3, bc8(s1), ALU.subtract)
            tt("dve", gn1[0:C, :], gn1[0:C, :], lnw_bc[0:C, :], ALU.mult)
            tt("dve", yc, yc, bc8(s2), ALU.mult)
            tt("dve", gn1[0:C, :], gn1[0:C, :], v_tm32[0:C, :], ALU.add)
            for c in range(4):
                o_ = mrbank[:, c * 128 + t0:c * 128 + t0 + C]
                i_ = gn1[0:C, c * 128:(c + 1) * 128]
                Sd.op("pe", lambda e, o=o_, i=i_: e.transpose(out=o, in_=i, identity=ident[0:C, 0:C]), [o_], [i_, ident[0:C, 0:C]])

        def attn_qk(t0, C, cur, prev, has_prev):
            if has_prev:
                for hb in range(2):
                    b_ = hb
                    for hh in range(4):
                        hq = hb * 4 + hh
                        c, par = hq % 4, hq // 4
                        mm(bank(b_)[:, hh * C:(hh + 1) * C], krpad[prev][:, par, :], qr_b[:, c, t0:t0 + C])
                    bv_ = bank(b_)[:, 0:4 * C].rearrange("p (h t) -> p h t", t=C)
                    tt("dve", bv_, bv_, apx(negprev[:, 0:C], [[0, 4], [1, C]]), ALU.add)
                    act(PTp[:, hb * 4:hb * 4 + 4, 0:C], bv_, AF.Exp, scale=0.125)
            for hb in range(2):
                b_ = 2 + hb
                for hh in range(4):
                    hq = hb * 4 + hh
                    c, par = hq % 4, hq // 4
                    mm(bank(b_)[0:C, hh * C:(hh + 1) * C], krpad[cur][:, par, t0:t0 + C], qr_b[:, c, t0:t0 + C])
                bv_ = bank(b_)[0:C, 0:4 * C].rearrange("p (h t) -> p h t", t=C)
                tt("dve", bv_, bv_, apx(negcur[0:C, 0:C], [[0, 4], [1, C]]), ALU.add)
                act(PTc[0:C, hb * 4:hb * 4 + 4, 0:C], bv_, AF.Exp, scale=0.125)

        def attn_pv(t0, C, cur, prev, has_prev):
            for c in range(4):
                oo = bank(6)[:, c * 128 + t0:c * 128 + t0 + C]
                dd = bank(7)[:, c * 128 + t0:c * 128 + t0 + C]
                seq = []
                if has_prev:
                    seq += [(vpad[prev][:, par, :], opad_b[:, par, :], PTp[:, par * 4 + c, 0:C]) for par in range(2)]
                seq += [(vpad[cur][t0:t0 + C, par, :] if C == 128 else vpad[cur][0:C, par, :], opad_b[0:C, par, :], PTc[0:C, par * 4 + c, 0:C]) for par in range(2)]
                for i, (vv, on, pt) in enumerate(seq):
                    mm(oo, vv, pt, i == 0, i == len(seq) - 1)
                for i, (vv, on, pt) in enumerate(seq):
                    mm(dd, on, pt, i == 0, i == len(seq) - 1)


        def attn_pass(t0, C, cur, prev, has_prev):
            attn_qk(t0, C, cur, prev, has_prev)
            attn_pv(t0, C, cur, prev, has_prev)

        def load_x(T):
            Sd.dma(xins[T["gi"] % 2], T["xrows"])

        def p1(T):
            xin = xins[T["gi"] % 2]
            mark('P1 %s%d' % (T["kind"], T["ti"]))
            ss = stat[:, 16:17]
            act(xs32, xin, AF.Square, accum=ss)
            rsqrt_col(ss, ss, 1.0 / 1024, NORM_EPS)
            act(xs32, xin, AF.Copy, scale=ss)
            for hb in range(2):
                b_ = gb()
                for kk_ in range(4):
                    k = hb * 4 + kk_
                    Sd.op("pe", lambda e, o=bank(b_)[:, kk_ * 128:(kk_ + 1) * 128], i=xs32[:, k * 128:(k + 1) * 128]: e.transpose(out=o, in_=i, identity=ident),
                          [bank(b_)[:, kk_ * 128:(kk_ + 1) * 128]], [xs32[:, k * 128:(k + 1) * 128], ident])
                cp("dve" if hb == 0 else "act", xT[:, hb * 4:hb * 4 + 4, :], bank(b_).rearrange("p (k t) -> p k t", t=128))

        def body_front_a(T):
            kind, ti, seq, first_of_seq, last_of_seq = T["kind"], T["ti"], T["seq"], T["first"], T["last"]
            prows, yrows, pos0 = T["prows"], T["yrows"], T["pos0"]
            xin = xins[T["gi"] % 2]
            nb, C = (1, 128) if kind == "p" else (nsb, 8)
            lv = 7 if kind == "p" else 3
            zb = zbuf_flat[:, 0:13 * nb * (C + 1)].rearrange("p (c n t) -> p c n t", c=13, n=nb)
            mark('P2')
            if kind == "p":
                if first_of_seq:
                    ms("pool", zb[:, :, 0, 0:1], 0.0)
                else:
                    cp("pool", zb[:, :, 0, 0], zcarry[:, 0:13])
            else:
                Sd.dma(shs, sh0.ap())
                b_ = gb()
                for c in range(13):
                    Sd.op("pe", lambda e, o=bank(b_)[:, c * nsb:(c + 1) * nsb], i=shs[:, c * 128:(c + 1) * 128]: e.transpose(out=o, in_=i, identity=ident[0:nsb, 0:nsb]),
                          [bank(b_)[:, c * nsb:(c + 1) * nsb]], [shs[:, c * 128:(c + 1) * 128], ident[0:nsb, 0:nsb]])
                cp("dve", zb[:, :, :, 0], bank(b_)[:, 0:13 * nsb].rearrange("p (c n) -> p c n", n=nsb))
            zcur = zb[:, :, :, 1:C + 1]
            zprev = zb[:, :, :, 0:C]
            zs4 = zs.rearrange("p c (n t) -> p c n t", n=nb)
            mark('P2')
            for (what, c0, n) in [("z", 12, 1), ("z", 4, 4), ("z", 0, 4), ("z", 8, 4)]:
                b_ = gb()
                for cc in range(n):
                    oc = c0 + cc
                    for k in range(8):
                        mm(bank(b_)[:, cc * 128:(cc + 1) * 128], win_b[:, k, oc * 128:(oc + 1) * 128], xT[:, k, :], k == 0, k == 7)
                src = bank(b_)[:, 0:n * 128]
                cp("act", zb[:, c0:c0 + n, :, 1:C + 1], src.rearrange("p (c n t) -> p c n t", c=n, n=nb))

        def body_front_b(T):
            kind, ti, seq, first_of_seq, last_of_seq = T["kind"], T["ti"], T["seq"], T["first"], T["last"]
            nb, C = (1, 128) if kind == "p" else (nsb, 8)
            zb = zbuf_flat[:, 0:13 * nb * (C + 1)].rearrange("p (c n t) -> p c n t", c=13, n=nb)
            zcur = zb[:, :, :, 1:C + 1]
            zprev = zb[:, :, :, 0:C]
            zs4 = zs.rearrange("p c (n t) -> p c n t", n=nb)
            for (c0, n) in [(12, 1), (4, 4), (0, 4), (8, 4)]:
                eng_ = "pool" if c0 == 0 else "dve"
                tt(eng_, zs4[:, c0:c0 + n], zprev[:, c0:c0 + n], zcur[:, c0:c0 + n], ALU.subtract)
                tt(eng_, zs[:, c0:c0 + n, :], zs[:, c0:c0 + n, :], apx(mu_c[:, c0:c0 + n], [[1, n], [0, 128]]), ALU.mult)
                tt(eng_, zs4[:, c0:c0 + n], zs4[:, c0:c0 + n], zcur[:, c0:c0 + n], ALU.add)
            if kind == "p":
                cp("dve", zcarry[:, 0:13], zb[:, :, 0, C])
                if last_of_seq:
                    b_ = gb()
                    Sd.op("pe", lambda e, o=bank(b_)[0:13, 0:128], i=zcarry[:, 0:13]: e.transpose(out=o, in_=i, identity=ident),
                          [bank(b_)[0:13, 0:128]], [zcarry[:, 0:13], ident])
                    stg = A[0:13, u2b:u2b + 128]
                    cp("act", stg, bank(b_)[0:13, 0:128])
                    Sd.dma(sh_p.ap()[seq:seq + 1, :].rearrange("a (c p) -> (a c) p", p=128), stg, out_onchip=False, in_onchip=True, is_output=True)
            else:
                zl = A[:, u2b:u2b + 13 * nsb]
                cp("dve", zl.rearrange("p (c n) -> p c n", n=nsb), zb[:, :, :, C])
                stg = A[0:nsb, u2b + 256:u2b + 256 + SHIFT_DIM]
                for c0_ in range(0, 13, 4):
                    n_ = min(4, 13 - c0_)
                    b_ = gb()
                    for cc in range(n_):
                        c = c0_ + cc
                        Sd.op("pe", lambda e, o=bank(b_)[0:nsb, cc * 128:(cc + 1) * 128], i=zl[:, c * nsb:(c + 1) * nsb]: e.transpose(out=o, in_=i, identity=ident),
                              [bank(b_)[0:nsb, cc * 128:(cc + 1) * 128]], [zl[:, c * nsb:(c + 1) * nsb], ident])
                    cp("act", stg[:, c0_ * 128:(c0_ + n_) * 128], bank(b_)[0:nsb, 0:n_ * 128])
                Sd.dma(sh_s.ap(), stg, out_onchip=False, in_onchip=True, is_output=True)

        def body_rest(T, nxt):
            kind, ti, seq, first_of_seq, last_of_seq = T["kind"], T["ti"], T["seq"], T["first"], T["last"]
            prows, yrows, pos0 = T["prows"], T["yrows"], T["pos0"]
            xin = xins[T["gi"] % 2]
            nb, C = (1, 128) if kind == "p" else (nsb, 8)
            lv = 7 if kind == "p" else 3
            zb = zbuf_flat[:, 0:13 * nb * (C + 1)].rearrange("p (c n t) -> p c n t", c=13, n=nb)
            zs4 = zs.rearrange("p c (n t) -> p c n t", n=nb)
            Sd.dma(pin, prows)
            Sd.dma(cs, rope_d.ap()[:, :, pos0:pos0 + 128].rearrange("a p t -> p a t"))
            if nxt is not None:
                load_x(nxt)
            if kind == "p":
                mark('P4')
                tm_prep(None, None, 0, 128, which=(2,))
                W = [w.rearrange("p (c t) -> p c t", t=128) for w in Wk]
                r3, kx3, v3 = zs[:, 0:4, :], zs[:, 4:8, :], zs[:, 8:12, :]
                act(L12[0:64, :], zs[0:64, 12, :], AF.Tanh)
                cp("act", L12[64:128, :], zs[64:128, 12, :])
                bw = gb(); ba = gb()
                for c in range(4):
                    mm(bank(bw)[:, c * 128:(c + 1) * 128], lora_b[:, 0, c * 128:(c + 1) * 128], L12)
                for c in range(4):
                    mm(bank(ba)[:, c * 128:(c + 1) * 128], lora_b[:, 1, c * 128:(c + 1) * 128], L12)
                sigw, a_, Linc, Lexc = W[0], W[1], Wk[2], Wk[3]
                for c in range(4):
                    act(sigw[:, c, :], bank(bw)[:, c * 128:(c + 1) * 128], AF.Tanh, bias=hw0_c[:, c:c + 1], scale=0.5)
                for c in range(4):
                    act(a_[:, c, :], bank(ba)[:, c * 128:(c + 1) * 128], AF.Tanh, bias=ha0_c[:, c:c + 1], scale=0.5)
                ts("dve", Wk[0], Wk[0], 0.5, 0.5, ALU.mult, ALU.add)
                ts("pool", Wk[1], Wk[1], 0.5, 0.5, ALU.mult, ALU.add)
                rsm = rsp if kind == "p" else rss
                Sd.op("dve", lambda e: e.tensor_tensor_scan(out=Linc, data0=rsm, data1=Wk[0], initial=0.0, op0=ALU.mult, op1=ALU.add), [Linc], [rsm, Wk[0]])
                tt("pool", Lexc, Linc, Wk[0], ALU.subtract)
                eLexc, emLinc, edk = Wk[4], Wk[5], Wk[6]
                act(eLinc, Linc, AF.Exp, scale=-C0)
                act(eLexc, Lexc, AF.Exp, scale=-C0)
                act(emLinc, Linc, AF.Exp, scale=C0)
                L4 = Linc.rearrange("p (c n t) -> p c n t", c=4, n=nb)
                ltot = apx(Linc[:, C - 1:C], [[128, 4], [C, nb], [0, C]])
                tt("dve", edk.rearrange("p (c n t) -> p c n t", c=4, n=nb), ltot, L4, ALU.subtract)
                act(edk, edk, AF.Exp, scale=-C0)
                for (what, c0, n) in [("gr", 13, 4), ("q", 17, 4)]:
                    b_ = gb()
                    for cc in range(n):
                        oc = c0 + cc
                        for k in range(8):
                            mm(bank(b_)[:, cc * 128:(cc + 1) * 128], win_b[:, k, oc * 128:(oc + 1) * 128], xT[:, k, :], k == 0, k == 7)
                    src = bank(b_)[:, 0:n * 128]
                    if what == "z":
                        cp("act", zb[:, c0:c0 + n, :, 1:C + 1], src.rearrange("p (c n t) -> p c n t", c=n, n=nb))
                        eng_ = "pool" if c0 == 0 else "dve"
                        tt(eng_, zs4[:, c0:c0 + n], zprev[:, c0:c0 + n], zcur[:, c0:c0 + n], ALU.subtract)
                        tt(eng_, zs[:, c0:c0 + n, :], zs[:, c0:c0 + n, :], apx(mu_c[:, c0:c0 + n], [[1, n], [0, 128]]), ALU.mult)
                        tt(eng_, zs4[:, c0:c0 + n], zs4[:, c0:c0 + n], zcur[:, c0:c0 + n], ALU.add)
                    elif what in ("gr", "ga"):
                        tmpg = Wk[2] if what == "gr" else Wk[3]
                        dstg = sgr if what == "gr" else sga
                        act(tmpg, src, AF.Tanh, scale=0.5)
                        stt(dstg.rearrange("p c t -> p (c t)"), tmpg, 1.0, src, ALU.add, ALU.mult)
                    elif what == "q":
                        cp("act", qf, src.rearrange("p (c t) -> p c t", t=128))
                    else:
                        cp("act", kvf, src.rearrange("p (c t) -> p c t", t=128))
                cur = ti % 2 if kind == "p" else 0
                prev = 1 - cur
                cosb = apx(cs[:, 0, :], [[0, 4], [1, 128]]); sinb = apx(cs[:, 1, :], [[0, 4], [1, 128]])
                bq_ = gb()
                for c in range(4):
                    mm(bank(bq_)[:, c * 128:(c + 1) * 128], rot, qf[:, c, :])
                tmpq = tmpq_b.rearrange("p (c t) -> p c t", t=128)
                tt("dve", tmpq, bank(bq_).rearrange("p (c t) -> p c t", t=128), sinb, ALU.mult)
                tt("pool", qf, qf, cosb, ALU.mult)
                tt("dve", qr_b, qf, tmpq, ALU.add)
                kkx = W[7]
                tt("dve", kkx, kx3, apx(kk_c, [[1, 4], [0, 128]]), ALU.mult)
                act(sqb, Wk[7], AF.Square)
                bq = gb()
                for c in range(4):
                    mm(bank(bq)[:, c * 128:(c + 1) * 128], bones_b, sqb[:, c * 128:(c + 1) * 128])
                rn = Wk[8]
                ts("dve", rn, bank(bq), 1e-18, None, ALU.max)
                act(rn, rn, AF.Ln)
                act(rn, rn, AF.Exp, scale=-0.5)
                tt("dve", Wk[7], Wk[7], rn, ALU.mult)
                t1 = W[8]
                for c in range(4):
                    ts("pool", t1[:, c, :], a_[:, c, :], ka_c[:, c:c + 1], omka_c[:, c:c + 1], ALU.mult, ALU.add)
                kf = W[9]
                tt("dve", kf, kx3, t1, ALU.mult)
                kka = W[8]
                tt("pool", kka, kkx, a_, ALU.mult)
                for (what, c0, n) in [("kv", 21, 2), ("ga", 23, 4)]:
                    b_ = gb()
                    for cc in range(n):
                        oc = c0 + cc
                        for k in range(8):
                            mm(bank(b_)[:, cc * 128:(cc + 1) * 128], win_b[:, k, oc * 128:(oc + 1) * 128], xT[:, k, :], k == 0, k == 7)
                    src = bank(b_)[:, 0:n * 128]
                    if what == "z":
                        cp("act", zb[:, c0:c0 + n, :, 1:C + 1], src.rearrange("p (c n t) -> p c n t", c=n, n=nb))
                        eng_ = "pool" if c0 == 0 else "dve"
                        tt(eng_, zs4[:, c0:c0 + n], zprev[:, c0:c0 + n], zcur[:, c0:c0 + n], ALU.subtract)
                        tt(eng_, zs[:, c0:c0 + n, :], zs[:, c0:c0 + n, :], apx(mu_c[:, c0:c0 + n], [[1, n], [0, 128]]), ALU.mult)
                        tt(eng_, zs4[:, c0:c0 + n], zs4[:, c0:c0 + n], zcur[:, c0:c0 + n], ALU.add)
                    elif what in ("gr", "ga"):
                        tmpg = Wk[2] if what == "gr" else Wk[3]
                        dstg = sgr if what == "gr" else sga
                        act(tmpg, src, AF.Tanh, scale=0.5)
                        stt(dstg.rearrange("p c t -> p (c t)"), tmpg, 1.0, src, ALU.add, ALU.mult)
                    elif what == "q":
                        cp("act", qf, src.rearrange("p (c t) -> p c t", t=128))
                    else:
                        cp("act", kvf, src.rearrange("p (c t) -> p c t", t=128))
                mark('P7')
                bk_ = gb()
                mm(bank(bk_)[:, 0:128], rot, kvf[:, 0, :])
                tmpk = tmpk_b
                tt("dve", tmpk, bank(bk_)[:, 0:128], cs[:, 1, :], ALU.mult)
                tt("pool", kr32, kvf[:, 0, :], cs[:, 0, :], ALU.mult)
                tt("dve", kr32, kr32, tmpk, ALU.add)
                dump("kr32", kr32); dump("qr", qr_b.rearrange("p c t -> p (c t)"))
                bt = gb()
                Sd.op("pe", lambda e: e.transpose(out=bank(bt)[:, 0:128], in_=kvf[:, 1, :], identity=ident), [bank(bt)[:, 0:128]], [kvf[:, 1, :], ident])
                Sd.op("pe", lambda e: e.transpose(out=bank(bt)[:, 128:256], in_=kr32, identity=ident), [bank(bt)[:, 128:256]], [kr32, ident])
                cp("act", vat32, bank(bt)[:, 0:128])
                cp("dve", kat32, bank(bt)[:, 128:256])
                for par in range(2):
                        pr = slice(par * 64, par * 64 + 64)
                        cp("act", krpad[cur][pr, par, :], kr32[pr, :])
                        cp("act", vpad[cur][:, par, par * 64:par * 64 + 64], vat32[:, par * 64:par * 64 + 64])
                attn_qk(0, 128, cur, prev, not first_of_seq)
                rkr = sqb.rearrange("p (c t) -> p c t", t=128)
                tt("dve", W[1], r3, kf, ALU.mult)
                tt("dve", rkr, W[1], apx(rk_c, [[1, 4], [0, 128]]), ALU.mult)
                for par in range(2):
                    pr = slice(par * 64, par * 64 + 64)
                    tt("dve", ARpad[pr, :, par, 0, :], kkx[pr], Wk[4].rearrange("p (c t) -> p c t", t=128)[pr], ALU.mult)
                    tt("pool", ARpad[pr, :, par, 1, :], r3[pr], eLinc.rearrange("p (c t) -> p c t", t=128)[pr], ALU.mult)
                tt("dve", BKpad[:, :, 0, 0, :], kka, Wk[5].rearrange("p (c t) -> p c t", t=128), ALU.mult)
                tt("pool", BKpad[:, :, 0, 1, :], kf, Wk[5].rearrange("p (c t) -> p c t", t=128), ALU.mult)
                khat, bhat = W[2], W[3]
                tt("dve", khat, kf, W[6], ALU.mult)
                tt("pool", bhat, kka, W[6], ALU.mult)
                dump("zs", zs.rearrange("p c t -> p (c t)")); dump("eLinc", eLinc); dump("kk", Wk[7]); dump("kf", Wk[9]); dump("khat", Wk[2])

                attn_pv(0, 128, cur, prev, not first_of_seq)
                if last_of_seq:
                    Sd.dma(k_p.ap()[seq], kat32, out_onchip=False, in_onchip=True, is_output=True)
                    Sd.dma(v_p.ap()[seq], vat32, out_onchip=False, in_onchip=True, is_output=True)
                for c in range(4):
                    act(rden[:, c * 128:(c + 1) * 128], bank(7)[:, c * 128:(c + 1) * 128], AF.Ln, bias=es_c[:, c:c + 1], scale=1.0)
                act(rden, rden, AF.Exp, scale=-1.0)
                tt("dve", g2, rden, sga.rearrange("p c t -> p (c t)"), ALU.mult)
                tt("dve", mixT[:, 4:8, :], bank(6).rearrange("p (c t) -> p c t", t=128), g2.rearrange("p (c t) -> p c t", t=128), ALU.mult)
                dump("mixT", mixT.rearrange("p c t -> p (c t)"))

            else:
                for (what, c0, n) in [("gr", 13, 4), ("q", 17, 4), ("kv", 21, 2), ("ga", 23, 4)]:
                    b_ = gb()
                    for cc in range(n):
                        oc = c0 + cc
                        for k in range(8):
                            mm(bank(b_)[:, cc * 128:(cc + 1) * 128], win_b[:, k, oc * 128:(oc + 1) * 128], xT[:, k, :], k == 0, k == 7)
                    src = bank(b_)[:, 0:n * 128]
                    if what == "z":
                        cp("act", zb[:, c0:c0 + n, :, 1:C + 1], src.rearrange("p (c n t) -> p c n t", c=n, n=nb))
                        eng_ = "pool" if c0 == 0 else "dve"
                        tt(eng_, zs4[:, c0:c0 + n], zprev[:, c0:c0 + n], zcur[:, c0:c0 + n], ALU.subtract)
                        tt(eng_, zs[:, c0:c0 + n, :], zs[:, c0:c0 + n, :], apx(mu_c[:, c0:c0 + n], [[1, n], [0, 128]]), ALU.mult)
                        tt(eng_, zs4[:, c0:c0 + n], zs4[:, c0:c0 + n], zcur[:, c0:c0 + n], ALU.add)
                    elif what in ("gr", "ga"):
                        tmpg = Wk[2] if what == "gr" else Wk[3]
                        dstg = sgr if what == "gr" else sga
                        act(tmpg, src, AF.Tanh, scale=0.5)
                        stt(dstg.rearrange("p c t -> p (c t)"), tmpg, 1.0, src, ALU.add, ALU.mult)
                    elif what == "q":
                        cp("act", qf, src.rearrange("p (c t) -> p c t", t=128))
                    else:
                        cp("act", kvf, src.rearrange("p (c t) -> p c t", t=128))
                mark('P7')
                cur = ti % 2 if kind == "p" else 0
                prev = 1 - cur
                cosb = apx(cs[:, 0, :], [[0, 4], [1, 128]]); sinb = apx(cs[:, 1, :], [[0, 4], [1, 128]])
                bq_ = gb()
                for c in range(4):
                    mm(bank(bq_)[:, c * 128:(c + 1) * 128], rot, qf[:, c, :])
                bk_ = gb()
                mm(bank(bk_)[:, 0:128], rot, kvf[:, 0, :])
                tmpq = tmpq_b.rearrange("p (c t) -> p c t", t=128)
                tt("dve", tmpq, bank(bq_).rearrange("p (c t) -> p c t", t=128), sinb, ALU.mult)
                tt("pool", qf, qf, cosb, ALU.mult)
                tt("dve", qr_b, qf, tmpq, ALU.add)
                tmpk = tmpk_b
                tt("dve", tmpk, bank(bk_)[:, 0:128], cs[:, 1, :], ALU.mult)
                tt("pool", kr32, kvf[:, 0, :], cs[:, 0, :], ALU.mult)
                tt("dve", kr32, kr32, tmpk, ALU.add)
                dump("kr32", kr32); dump("qr", qr_b.rearrange("p c t -> p (c t)"))
                bt = gb()
                Sd.op("pe", lambda e: e.transpose(out=bank(bt)[:, 0:128], in_=kvf[:, 1, :], identity=ident), [bank(bt)[:, 0:128]], [kvf[:, 1, :], ident])
                Sd.op("pe", lambda e: e.transpose(out=bank(bt)[:, 128:256], in_=kr32, identity=ident), [bank(bt)[:, 128:256]], [kr32, ident])
                cp("act", vat32, bank(bt)[:, 0:128])
                cp("dve", kat32, bank(bt)[:, 128:256])
                for par in range(2):
                    pr = slice(par * 64, par * 64 + 64)
                    cp("pool", krpad[0][pr, par, :], kr32[pr, :])
                for b in range(nsb):
                    Sd.dma(k_s.ap()[b, 0:120, :], ck.ap()[b, 8:128, :], out_onchip=False, in_onchip=False, is_output=True)
                    Sd.dma(v_s.ap()[b, 0:120, :], cv.ap()[b, 8:128, :], out_onchip=False, in_onchip=False, is_output=True)
                    Sd.dma(k_s.ap()[b, 120:128, :], kat32[b * 8:(b + 1) * 8, :], out_onchip=False, in_onchip=True, is_output=True)
                    Sd.dma(v_s.ap()[b, 120:128, :], vat32[b * 8:(b + 1) * 8, :], out_onchip=False, in_onchip=True, is_output=True)
                    kvb = [(Wk[8][:, 0:128], Wk[8][:, 128:256]), (Wk[8][:, 256:384], Wk[8][:, 384:512])]
                    if b == 0:
                        Sd.dma(kvb[0][0], ck.ap()[0]); Sd.dma(kvb[0][1], cv.ap()[0])
                    if b + 1 < nsb:
                        Sd.dma(kvb[(b + 1) % 2][0], ck.ap()[b + 1]); Sd.dma(kvb[(b + 1) % 2][1], cv.ap()[b + 1])
                    kc, vc = kvb[b % 2]
                    bt2 = gb()
                    Sd.op("pe", lambda e, o=bank(bt2)[:, 0:128], i=kc: e.transpose(out=o, in_=i, identity=ident), [bank(bt2)[:, 0:128]], [kc, ident])
                    Sd.op("pe", lambda e, o=bank(bt2)[0:8, 128:256], i=kvf[:, 1, b * 8:(b + 1) * 8]: e.transpose(out=o, in_=i, identity=ident),
                          [bank(bt2)[0:8, 128:256]], [kvf[:, 1, b * 8:(b + 1) * 8], ident])
                    for par in range(2):
                        pr = slice(par * 64, par * 64 + 64)
                        cp("act" if par == 0 else "dve", krpad[1][pr, par, :], bank(bt2)[pr, 0:128])
                        cp("pool", vpad[1][:, par, par * 64:par * 64 + 64], vc[:, par * 64:par * 64 + 64])
                        cp("act" if par == 0 else "dve", vpad[0][0:8, par, par * 64:par * 64 + 64], bank(bt2)[0:8, 128 + par * 64:128 + par * 64 + 64])
                    attn_pass(b * 8, 8, 0, 1, True)
                for c in range(4):
                    act(rden[:, c * 128:(c + 1) * 128], bank(7)[:, c * 128:(c + 1) * 128], AF.Ln, bias=es_c[:, c:c + 1], scale=1.0)
                act(rden, rden, AF.Exp, scale=-1.0)
                tt("dve", g2, rden, sga.rearrange("p c t -> p (c t)"), ALU.mult)
                tt("dve", mixT[:, 4:8, :], bank(6).rearrange("p (c t) -> p c t", t=128), g2.rearrange("p (c t) -> p c t", t=128), ALU.mult)
                dump("mixT", mixT.rearrange("p c t -> p (c t)"))

                mark('P4')
                W = [w.rearrange("p (c t) -> p c t", t=128) for w in Wk]
                r3, kx3, v3 = zs[:, 0:4, :], zs[:, 4:8, :], zs[:, 8:12, :]
                act(L12[0:64, :], zs[0:64, 12, :], AF.Tanh)
                cp("act", L12[64:128, :], zs[64:128, 12, :])
                bw = gb(); ba = gb()
                for c in range(4):
                    mm(bank(bw)[:, c * 128:(c + 1) * 128], lora_b[:, 0, c * 128:(c + 1) * 128], L12)
                for c in range(4):
                    mm(bank(ba)[:, c * 128:(c + 1) * 128], lora_b[:, 1, c * 128:(c + 1) * 128], L12)
                sigw, a_, Linc, Lexc = W[0], W[1], Wk[2], Wk[3]
                for c in range(4):
                    act(sigw[:, c, :], bank(bw)[:, c * 128:(c + 1) * 128], AF.Tanh, bias=hw0_c[:, c:c + 1], scale=0.5)
                for c in range(4):
                    act(a_[:, c, :], bank(ba)[:, c * 128:(c + 1) * 128], AF.Tanh, bias=ha0_c[:, c:c + 1], scale=0.5)
                ts("dve", Wk[0], Wk[0], 0.5, 0.5, ALU.mult, ALU.add)
                ts("pool", Wk[1], Wk[1], 0.5, 0.5, ALU.mult, ALU.add)
                rsm = rsp if kind == "p" else rss
                Sd.op("dve", lambda e: e.tensor_tensor_scan(out=Linc, data0=rsm, data1=Wk[0], initial=0.0, op0=ALU.mult, op1=ALU.add), [Linc], [rsm, Wk[0]])
                tt("pool", Lexc, Linc, Wk[0], ALU.subtract)
                eLexc, emLinc, edk = Wk[4], Wk[5], Wk[6]
                act(eLinc, Linc, AF.Exp, scale=-C0)
                act(eLexc, Lexc, AF.Exp, scale=-C0)
                act(emLinc, Linc, AF.Exp, scale=C0)
                L4 = Linc.rearrange("p (c n t) -> p c n t", c=4, n=nb)
                ltot = apx(Linc[:, C - 1:C], [[128, 4], [C, nb], [0, C]])
                tt("dve", edk.rearrange("p (c n t) -> p c n t", c=4, n=nb), ltot, L4, ALU.subtract)
                act(edk, edk, AF.Exp, scale=-C0)
                kkx = W[7]
                tt("dve", kkx, kx3, apx(kk_c, [[1, 4], [0, 128]]), ALU.mult)
                act(sqb, Wk[7], AF.Square)
                bq = gb()
                for c in range(4):
                    mm(bank(bq)[:, c * 128:(c + 1) * 128], bones_b, sqb[:, c * 128:(c + 1) * 128])
                rn = Wk[8]
                ts("dve", rn, bank(bq), 1e-18, None, ALU.max)
                act(rn, rn, AF.Ln)
                act(rn, rn, AF.Exp, scale=-0.5)
                tt("dve", Wk[7], Wk[7], rn, ALU.mult)
                t1 = W[8]
                for c in range(4):
                    ts("pool", t1[:, c, :], a_[:, c, :], ka_c[:, c:c + 1], omka_c[:, c:c + 1], ALU.mult, ALU.add)
                kf = W[9]
                tt("dve", kf, kx3, t1, ALU.mult)
                kka = W[8]
                tt("pool", kka, kkx, a_, ALU.mult)
                rkr = sqb.rearrange("p (c t) -> p c t", t=128)
                tt("dve", W[1], r3, kf, ALU.mult)
                tt("dve", rkr, W[1], apx(rk_c, [[1, 4], [0, 128]]), ALU.mult)
                for par in range(2):
                    pr = slice(par * 64, par * 64 + 64)
                    tt("dve", ARpad[pr, :, par, 0, :], kkx[pr], Wk[4].rearrange("p (c t) -> p c t", t=128)[pr], ALU.mult)
                    tt("pool", ARpad[pr, :, par, 1, :], r3[pr], eLinc.rearrange("p (c t) -> p c t", t=128)[pr], ALU.mult)
                tt("dve", BKpad[:, :, 0, 0, :], kka, Wk[5].rearrange("p (c t) -> p c t", t=128), ALU.mult)
                tt("pool", BKpad[:, :, 0, 1, :], kf, Wk[5].rearrange("p (c t) -> p c t", t=128), ALU.mult)
                khat, bhat = W[2], W[3]
                tt("dve", khat, kf, W[6], ALU.mult)
                tt("pool", bhat, kka, W[6], ALU.mult)
                dump("zs", zs.rearrange("p c t -> p (c t)")); dump("eLinc", eLinc); dump("kk", Wk[7]); dump("kf", Wk[9]); dump("khat", Wk[2])

            if nxt is not None:
                p1(nxt)
            mark('P5')
            mrb = bank(4)
            for pb in range(nb):
                t0 = pb * C
                if kind == "p":
                    if first_of_seq:
                        ms("pool", Hst.rearrange("p j v -> p (j v)"), 0.0)
                        ms("pool", Hbf.rearrange("p j v -> p (j v)"), 0.0)
                else:
                    wsb = [ws_st, Wk[0][0:64, :]]
                    if pb == 0:
                        Sd.dma(wsb[0].rearrange("p (h k) -> p h k", k=64), wkv0.ap()[0].rearrange("h v k -> v h k"))
                    if pb + 1 < nb:
                        Sd.dma(wsb[(pb + 1) % 2].rearrange("p (h k) -> p h k", k=64), wkv0.ap()[pb + 1].rearrange("h v k -> v h k"))
                    ws = wsb[pb % 2]
                    b_ = gb()
                    for j in range(4):
                        Sd.op("pe", lambda e, o=bank(b_)[:, j * 64:(j + 1) * 64], i=ws[:, j * 128:(j + 1) * 128]: e.transpose(out=o, in_=i, identity=ident[0:64, 0:64]),
                              [bank(b_)[:, j * 64:(j + 1) * 64]], [ws[:, j * 128:(j + 1) * 128], ident[0:64, 0:64]])
                    cp("act", Hst, bank(b_)[:, 0:256].rearrange("p (j v) -> p j v", v=64))
                    cp("dve", Hbf, bank(b_)[:, 0:256].rearrange("p (j v) -> p j v", v=64))
                if kind == "p":
                    b_ = gb()
                    for c in range(4):
                        mm(bank(b_)[0:C, 2 * c:2 * c + 2], rkr[:, c, t0:t0 + C], hsel_b)
                    cp("act", cbon[0:C, :], bank(b_)[0:C, 0:8])
                    gn_bonus(C)
                elif pb == 0:
                    b_ = gb()
                    for c in range(4):
                        mm(bank(b_)[:, 2 * c:2 * c + 2], rkr[:, c, :], hsel_b)
                    cp("act", cbon, bank(b_)[:, 0:8])
                    b_ = gb()
                    for c in range(4):
                        Sd.op("pe", lambda e, o=bank(b_)[:, c * 128:(c + 1) * 128], i=zs[:, 8 + c, :]: e.transpose(out=o, in_=i, identity=ident),
                              [bank(b_)[:, c * 128:(c + 1) * 128]], [zs[:, 8 + c, :], ident])
                    cp("act", v_tm32, bank(b_))
                if kind == "p":
                    tm_prep(khat, bhat, t0, C, which=(0, 1))
                else:
                    tm_prep(khat, bhat, t0, C, want32=False)
                    if pb == 0:
                        gn_bonus(128)
                scan_pass(t0, C, lv)
                if nxt is not None and pb == nb - 1:
                    body_front_a(nxt)
                if pb == 0:
                    dump("ytm", ytm); dump("Hst", Hst.rearrange("p j v -> p (j v)"))
                if kind == "p":
                    gn_out(t0, C, mrb)
                    if nxt is not None:
                        body_front_b(nxt)
                else:
                    Sd.dma(yall[t0:t0 + C, :], ytm[0:C, :], out_onchip=True, in_onchip=True)
                    if pb == nb - 1:
                        gn_out(0, 128, mrb, ysrc=yall)
                if kind == "s" or last_of_seq:
                    b_ = gb()
                    for j in range(4):
                        Sd.op("pe", lambda e, o=bank(b_)[0:64, j * 128:(j + 1) * 128], i=Hst[:, j, :]: e.transpose(out=o, in_=i, identity=ident),
                              [bank(b_)[0:64, j * 128:(j + 1) * 128]], [Hst[:, j, :], ident])
                    wo = wo_st
                    cp("act", wo, bank(b_)[0:64, :])
                    dst = (wkv_s.ap()[pb] if kind == "s" else wkv_p.ap()[seq]).rearrange("h v k -> v h k")
                    Sd.dma(dst, wo.rearrange("p (h k) -> p h k", k=64), out_onchip=False, in_onchip=True, is_output=True)
            tt("dve", mixT[:, 0:4, :], mrb.rearrange("p (c t) -> p c t", t=128), sgr, ALU.mult)

            mark('P8')
            for n in range(2):
                b_ = gb()
                for k in range(8):
                    mm(bank(b_), mixT[:, k, :], wout_b[:, k, n * 512:(n + 1) * 512], k == 0, k == 7)
                tt("dve", xin[:, n * 512:(n + 1) * 512], bank(b_), xin[:, n * 512:(n + 1) * 512], ALU.add)
            ss2 = stat[:, 17:18]
            act(hn32, xin, AF.Square, accum=ss2)
            rsqrt_col(ss2, ss2, 1.0 / 1024, NORM_EPS)
            hs2 = stat[:, 19:20]
            ts("dve", hs2, ss2, 0.5, None, ALU.mult)
            for hb in range(2):
                b_ = gb()
                for kk_ in range(4):
                    k = hb * 4 + kk_
                    Sd.op("pe", lambda e, o=bank(b_)[:, kk_ * 128:(kk_ + 1) * 128], i=xin[:, k * 128:(k + 1) * 128]: e.transpose(out=o, in_=i, identity=ident),
                          [bank(b_)[:, kk_ * 128:(kk_ + 1) * 128]], [xin[:, k * 128:(k + 1) * 128], ident])
                cp("dve" if hb == 0 else "act", hnT[:, hb * 4:hb * 4 + 4, :], bank(b_).rearrange("p (k t) -> p k t", t=128))
            b_ = gb()
            for k in range(2):
                Sd.op("pe", lambda e, o=bank(b_)[:, k * 128:(k + 1) * 128], i=pin[:, k * 128:(k + 1) * 128]: e.transpose(out=o, in_=i, identity=ident),
                      [bank(b_)[:, k * 128:(k + 1) * 128]], [pin[:, k * 128:(k + 1) * 128], ident])
            cp("act", pT, bank(b_)[:, 0:256].rearrange("p (k t) -> p k t", t=128))
            for n in range(2):
                bg = gb()
                for k in range(8):
                    mm(bank(bg), hnT[:, k, :], wpg_b[:, k, n * 512:(n + 1) * 512], k == 0, k == 7)
                gate = hn32[:, n * 512:(n + 1) * 512]
                act(gate, bank(bg), AF.Tanh, scale=hs2)
                bp = gb()
                for k in range(2):
                    mm(bank(bp), pT[:, k, :], wpp_b[:, k, n * 512:(n + 1) * 512], k == 0, k == 1)
                stt(gate, gate, 1.0, bank(bp), ALU.add, ALU.mult)
                stt(xin[:, n * 512:(n + 1) * 512], gate, 0.5, xin[:, n * 512:(n + 1) * 512], ALU.mult, ALU.add)
            ss3 = stat[:, 18:19]
            act(hn32, xin, AF.Square, accum=ss3)
            rsqrt_col(ss3, ss3, 1.0 / 1024, NORM_EPS)
            stt(yout, xin, ss3, gfin_bc, ALU.mult, ALU.mult)
            Sd.dma(yrows, yout, out_onchip=False, in_onchip=True, is_output=True)

        tiles = []
        for seq in range(nseq):
            for ti in range(NT):
                r0 = seq * S + ti * 128
                tiles.append(dict(kind="p", ti=ti, seq=seq, first=ti == 0, last=ti == NT - 1, xrows=xp.ap()[r0:r0 + 128, :],
                                  prows=pp.ap()[r0:r0 + 128, :], yrows=y_p.ap()[r0:r0 + 128, :], pos0=ti * 128))
        if nsb > 0 and not _SKIP_S:
            tiles.append(dict(kind="s", ti=0, seq=0, first=True, last=True, xrows=xsm.ap(), prows=psm.ap(), yrows=y_s.ap(), pos0=S))
        for gi, T in enumerate(tiles):
            T["gi"] = gi
        load_x(tiles[0])
        p1(tiles[0])
        body_front_a(tiles[0])
        body_front_b(tiles[0])
        for gi, T in enumerate(tiles):
            body_rest(T, tiles[gi + 1] if gi + 1 < len(tiles) else None)
        mark('END')
        Sd.finish()
        if _os.environ.get("KMARKS"):
            import json
            json.dump(marks, open(_os.environ["KMARKS"], "w"))
        print("instr counts", Sd.cnt, "waits", sum(1 for e in ENGS for it in Sd.ops[e] if it[0] == "wait"))
    return nc


def _host_consts(S):
    c = np.zeros((128, NCONST), np.float32)
    i = np.arange(128)
    s_, t_ = np.meshgrid(i, i, indexing="ij")
    c[:, 0:128] = np.eye(128)
    lt = (s_ < t_).astype(np.float32); le = (s_ <= t_).astype(np.float32); gt = (s_ > t_).astype(np.float32)
    c[:, 128:256] = -lt; c[:, 256:384] = le; c[:, 384:512] = lt; c[:, 512:640] = le
    c[:, 640:768] = -gt; c[:, 768:896] = gt
    c[:, 896:1024] = (s_ // 64 == t_ // 64)
    rot = np.zeros((128, 128), np.float32)
    for d in range(128):
        j = d % 64
        if j < 8:
            rot[d + 8, d] = 1
        elif j < 16:
            rot[d - 8, d] = 1
    c[:, 1024:1152] = rot
    rsp = np.ones((4, 128), np.float32); rsp[:, 0] = 0
    c[:, 1152:1664] = rsp.reshape(-1)[None]
    rss = np.ones(512, np.float32); rss[::8] = 0
    c[:, 1664:2176] = rss[None]
    c[:, 2176] = (i < 64); c[:, 2177] = (i >= 64)
    op = np.zeros((2, 128), np.float32); op[0, 0:64] = 1; op[1, 64:128] = 1
    c[:, 2178:2434] = op.reshape(-1)[None]
    npos = S + 128
    pos = np.concatenate([np.arange(S), np.tile(PAST_LEN + np.arange(8), 16)]).astype(np.float32)
    inv = (np.float32(500000.0) ** (-np.arange(8, dtype=np.float32) / np.float32(8))).astype(np.float32)
    ang = pos[:, None] * inv[None, :]
    co, si = np.cos(ang).astype(np.float32), np.sin(ang).astype(np.float32)
    rope = np.zeros((2, 128, npos), np.float32)
    rope[0] = 1.0
    for p in range(128):
        j = p % 64
        if j < 16:
            rope[0, p] = co[:, j % 8]
            rope[1, p] = -si[:, j % 8] if j < 8 else si[:, j % 8]
    return c, rope


_CACHE = {}
import os as _os
_PH = int(_os.environ.get("KPH", "9"))
_SKIP_S = bool(int(_os.environ.get("KSKIPS", "0")))


def _prep_weights(inp, S):
    n2o = np.array([(c + 4 * par) * 64 + d for c in range(4) for par in range(2) for d in range(64)])
    w_in = np.ascontiguousarray(inp["w_in"][0]).copy()
    o2 = SHIFT_DIM + 512
    o5 = o2 + 512 + 256
    w_in[:, o2:o2 + 512] = inp["w_in"][0][:, o2 + n2o]
    w_in[:, o5:o5 + 512] = inp["w_in"][0][:, o5 + n2o]
    w_out = np.ascontiguousarray(inp["w_out"][0]).copy()
    w_out[512:1024] = inp["w_out"][0][512 + n2o]
    pv = np.zeros((128, NV), np.float32)
    col = lambda v, n: np.asarray(v, np.float32).reshape(n, 128).T
    pv[:, 0:13] = col(inp["mu_shift"][0], 13)
    pv[:, 13:17] = col(inp["k_k"][0], 4); pv[:, 17:21] = col(inp["k_a"][0], 4); pv[:, 21:25] = col(inp["r_k"][0], 4)
    pv[:, 25:29] = col(inp["w0"][0], 4); pv[:, 29:33] = col(inp["a0"][0], 4)
    pv[:, 33:41] = col(inp["g_norm"][0], 8); pv[:, 41:49] = col(inp["g_ple"][0], 8)
    sk = np.asarray(inp["sinks"][0], np.float32)
    for c in range(4):
        for p in range(128):
            pv[p, 49 + c] = sk[c + 4 * (p // 64)]
    consts, rope = _host_consts(S)
    return dict(w_in=w_in, w_out=w_out, w_pg=np.ascontiguousarray(inp["w_ple_gate"][0]), w_pp=np.ascontiguousarray(inp["w_ple_proj"][0]),
                w2=np.ascontiguousarray(inp["w2"][0]), a2=np.ascontiguousarray(inp["a2"][0]), pvec=pv,
                lnw=np.ascontiguousarray(inp["ln_w"]).reshape(1, 512), lnb=np.ascontiguousarray(inp["ln_b"]).reshape(1, 512),
                gfin=np.ascontiguousarray(inp["g_final"]).reshape(1, 1024), consts=consts, rope=rope)


def run_cores(inp, n_cores, nseq, S, nsb, dbg=None):
    key = (nseq, S, nsb, dbg is not None)
    if key not in _CACHE:
        _CACHE[key] = build(nseq, S, nsb, dbg)
    nc = _CACHE[key]
    shared = _prep_weights(inp, S)
    f = lambda a: np.ascontiguousarray(a, dtype=np.float32)
    in_maps = []
    for c in range(n_cores):
        bs = slice(c * nseq, (c + 1) * nseq)
        ss = slice(c * nsb, (c + 1) * nsb)
        m = dict(shared)
        m["xp"] = f(inp["x_prompt"][bs]).reshape(nseq * S, 1024)
        m["pp"] = f(inp["p_prompt"][0, bs]).reshape(nseq * S, 256)
        m["xsm"] = f(inp["x_sample"][ss]).reshape(nsb * 8, 1024)
        m["psm"] = f(inp["p_sample"][0, ss]).reshape(nsb * 8, 256)
        m["wkv0"] = f(inp["state_rwkv_wkv"][0, ss])
        m["sh0"] = f(inp["state_rwkv_shift"][0, ss])
        m["ck"] = f(inp["cache_swa_k"][0, ss]).reshape(nsb, 128, 128)
        m["cv"] = f(inp["cache_swa_v"][0, ss]).reshape(nsb, 128, 128)
        in_maps.append(m)
    res = run_bass_kernel_spmd(nc, in_maps, core_ids=list(range(n_cores)))
    return res.results


def kernel(**inp):
    n = 8
    B, S = inp["x_prompt"].shape[0], inp["x_prompt"].shape[1]
    DB = inp["x_sample"].shape[0]
    nseq, nsb = B // n, DB // n
    r = run_cores(inp, n, nseq, S, nsb)
    cat = lambda k: np.concatenate([x[k] for x in r], axis=0)
    y_p = cat("y_p").reshape(B, S, 1024)
    y_s = cat("y_s").reshape(DB, 8, 1024)
    return (y_p, y_s,
            cat("wkv_p").reshape(1, B, 8, 64, 64), cat("sh_p").reshape(1, B, SHIFT_DIM),
            cat("k_p").reshape(1, B, 128, 2, 64), cat("v_p").reshape(1, B, 128, 2, 64),
            cat("wkv_s").reshape(1, DB, 8, 64, 64), cat("sh_s").reshape(1, DB, SHIFT_DIM),
            cat("k_s").reshape(1, DB, 128, 2, 64), cat("v_s").reshape(1, DB, 128, 2, 64))
```

```python
import contextlib
import numpy as np
import concourse.bass as bass
import concourse.mybir as mybir
from concourse.bass_utils import run_bass_kernel_spmd

F32 = mybir.dt.float32
BF16 = mybir.dt.bfloat16
AF = mybir.ActivationFunctionType
ALU = mybir.AluOpType
AX = mybir.AxisListType

ENGS = ("pe", "act", "dve", "pool", "sp")
D_MODEL = 1024
SHIFT_DIM = 1664
IN_DIM = 3456
PAST_LEN = 16384
C0 = float(np.exp(-0.5))
NORM_EPS = 1e-6
GN_EPS = 64e-5
NCONST = 2440
NV = 53


def _esize(dt):
    return 2 if dt == BF16 else 4


class Sched:
    def __init__(self, nc, n_dma_sems=48):
        self.nc = nc
        self.ops = {e: [] for e in ENGS}
        self.cnt = {e: 0 for e in ENGS}
        self.waited = {e: {} for e in ENGS}
        self.recs = {}
        self.n_dma_sems = n_dma_sems
        self.dma_vals = [0] * n_dma_sems
        self.dma_rr = 0
        self.out_dma = []

    @staticmethod
    def fp(ap):
        es = _esize(ap.dtype)
        pat = ap.ap
        pstep, pn = pat[0]
        off = ap.offset
        p0 = off // pstep
        col = off % pstep
        lo = col
        hi = col
        for s, c in pat[1:]:
            if s >= 0:
                hi += s * (c - 1)
            else:
                lo += s * (c - 1)
        return (ap.name, p0, p0 + pn, lo * es, (hi + 1) * es)

    @staticmethod
    def _ovl(a, b):
        return a[1] < b[2] and b[1] < a[2] and a[3] < b[4] and b[3] < a[4]

    @staticmethod
    def _covers(a, b):
        return a[1] <= b[1] and a[2] >= b[2] and a[3] <= b[3] and a[4] >= b[4]

    @staticmethod
    def _bank(f):
        return (f[0], 0, 128, (f[3] // 2048) * 2048, ((f[4] + 2047) // 2048) * 2048)

    def _deps(self, eng, reads, writes):
        deps = {}

        def add(r):
            if deps.get(r[1], 0) < r[2]:
                deps[r[1]] = r[2]
        for f in reads:
            ps = f[0] == "ps"
            fb = self._bank(f) if ps else None
            for r in self.recs.get(f[0], ()):
                if r[4] and self._ovl(f, r[0]):
                    add(r)
                elif ps and r[3] != eng and self._ovl(fb, r[5]):
                    add(r)
        for f in writes:
            ps = f[0] == "ps"
            fb = self._bank(f) if ps else None
            for r in self.recs.get(f[0], ()):
                if r[3] == eng and eng == "pe":
                    continue
                if self._ovl(f, r[0]) or (ps and r[3] != eng and self._ovl(fb, r[5])):
                    add(r)
        return deps

    def _record(self, eng, sk, val, reads, writes):
        for f in writes:
            lst = self.recs.setdefault(f[0], [])
            lst[:] = [r for r in lst if not self._covers(f, r[0])]
            lst.append((f, sk, val, eng, True, self._bank(f) if f[0] == "ps" else None))
        for f in reads:
            lst = self.recs.setdefault(f[0], [])
            if f[0] == "ps":
                fb = self._bank(f)
                lst[:] = [r for r in lst if r[4] or r[3] != eng or r[5] != fb]
                lst.append((f, sk, val, eng, False, fb))
                continue
            if eng != "dma":
                lst[:] = [r for r in lst if r[4] or r[3] != eng or r[0] != f]
            lst.append((f, sk, val, eng, False, None))

    def _emit_waits(self, eng, deps):
        for sk, val in deps.items():
            if self.waited[eng].get(sk, 0) >= val:
                continue
            self.waited[eng][sk] = val
            self.ops[eng].append(("wait", sk, val))

    def op(self, eng, fn, outs, ins):
        reads = [self.fp(a) for a in ins]
        writes = [self.fp(a) for a in outs]
        deps = self._deps(eng, reads, writes)
        self._emit_waits(eng, deps)
        self.cnt[eng] += 1
        self.ops[eng].append(("op", fn))
        self._record(eng, ("e", eng), self.cnt[eng], reads, writes)

    def dma(self, out, in_, q="sp", out_onchip=True, in_onchip=False, is_output=False, **kw):
        reads = [self.fp(in_)] if in_onchip else []
        writes = [self.fp(out)] if out_onchip else []
        deps = self._deps("dma", reads, writes)
        s = self.dma_rr
        self.dma_rr = (self.dma_rr + 1) % self.n_dma_sems
        if self.dma_vals[s] > 0:
            deps[("d", s)] = max(deps.get(("d", s), 0), self.dma_vals[s])
        self._emit_waits(q, deps)
        self.dma_vals[s] += 16
        val = self.dma_vals[s]
        self.ops[q].append(("dma", out, in_, s, kw))
        self._record("dma", ("d", s), val, reads, writes)
        if is_output:
            self.out_dma.append((s, val))

    def finish(self):
        nc = self.nc
        with contextlib.ExitStack() as st:
            esem = {e: st.enter_context(nc.semaphore("sem_" + e)) for e in ENGS if e != "sp"}
            dsem = [st.enter_context(nc.semaphore("dsem%d" % i)) for i in range(self.n_dma_sems)]
            for s, val in self.out_dma:
                if self.waited["sp"].get(("d", s), 0) < val:
                    self.waited["sp"][("d", s)] = val
                    self.ops["sp"].append(("wait", ("d", s), val))
            block = st.enter_context(nc.Block())

            def semof(sk):
                return esem[sk[1]] if sk[0] == "e" else dsem[sk[1]]

            def run(eng_name, e):
                for item in self.ops[eng_name]:
                    if item[0] == "wait":
                        e.wait_ge(semof(item[1]), item[2])
                    elif item[0] == "op":
                        item[1](e).then_inc(esem[eng_name], 1)
                    else:
                        _, out, in_, s, kw = item
                        e.dma_start(out=out, in_=in_, **kw).then_inc(dsem[s], 16)

            @block.tensor
            def _(e):
                run("pe", e)

            @block.scalar
            def _(e):
                run("act", e)

            @block.vector
            def _(e):
                run("dve", e)

            @block.gpsimd
            def _(e):
                run("pool", e)

            @block.sync
            def _(e):
                run("sp", e)


class Arena:
    def __init__(self, A, base, limit):
        self.A, self.off, self.limit, self.hi = A, base, limit, base

    def take(self, n):
        ap = self.A[:, self.off:self.off + n]
        self.off += n
        assert self.off <= self.limit, ("arena overflow", self.off, self.limit)
        return ap

    def sub(self, n):
        base = self.off
        self.off += n
        assert self.off <= self.limit, ("arena overflow", self.off, self.limit)
        return base, base + n


def bfv(ap):
    return ap.bitcast(BF16)


def build(nseq, S, nsb, dbg=None):
    NT = S // 128
    NPOS = S + 128
    nc = bass.Bass("TRN2", target_bir_lowering=False, dynamic_dma_scratch_size=256)

    def din(name, shape):
        return nc.dram_tensor(name, shape, F32, kind="ExternalInput")

    def dout(name, shape):
        return nc.dram_tensor(name, shape, F32, kind="ExternalOutput")

    xp = din("xp", [nseq * S, 1024]); pp = din("pp", [nseq * S, 256])
    xsm = din("xsm", [nsb * 8, 1024]); psm = din("psm", [nsb * 8, 256])
    wkv0 = din("wkv0", [nsb, 8, 64, 64]); sh0 = din("sh0", [nsb, SHIFT_DIM])
    ck = din("ck", [nsb, 128, 128]); cv = din("cv", [nsb, 128, 128])
    w_in = din("w_in", [1024, IN_DIM]); w_out = din("w_out", [1024, 1024])
    w_pg = din("w_pg", [1024, 1024]); w_pp = din("w_pp", [256, 1024])
    w2 = din("w2", [64, 512]); a2 = din("a2", [64, 512])
    pvec_d = din("pvec", [128, NV]); lnw_d = din("lnw", [1, 512]); lnb_d = din("lnb", [1, 512])
    gfin_d = din("gfin", [1, 1024]); consts_d = din("consts", [128, NCONST])
    rope_d = din("rope", [2, 128, NPOS])

    y_p = dout("y_p", [nseq * S, 1024]); y_s = dout("y_s", [nsb * 8, 1024])
    wkv_p = dout("wkv_p", [nseq, 8, 64, 64]); sh_p = dout("sh_p", [nseq, SHIFT_DIM])
    k_p = dout("k_p", [nseq, 128, 128]); v_p = dout("v_p", [nseq, 128, 128])
    wkv_s = dout("wkv_s", [nsb, 8, 64, 64]); sh_s = dout("sh_s", [nsb, SHIFT_DIM])
    k_s = dout("k_s", [nsb, 128, 128]); v_s = dout("v_s", [nsb, 128, 128])
    dbg_d = dout("dbg", [128, 8192]) if dbg is not None else None
    dbg_off = [0]

    AW = 49088
    with contextlib.ExitStack() as st:
        A = st.enter_context(nc.sbuf_tensor("arena", [128, AW], F32))
        PS = st.enter_context(nc.psum_tensor("ps", [128, 4096], F32))
        Sd = Sched(nc)
        ar = Arena(A, 0, AW)

        def tt(eng, out, in0, in1, op):
            Sd.op(eng, lambda e: e.tensor_tensor(out=out, in0=in0, in1=in1, op=op), [out], [in0, in1])

        def ts(eng, out, in0, s1, s2, op0, op1=None):
            ins = [in0] + [s for s in (s1, s2) if s is not None and not isinstance(s, (int, float))]
            if op1 is None:
                Sd.op(eng, lambda e: e.tensor_scalar(out=out, in0=in0, scalar1=s1, scalar2=None, op0=op0), [out], ins)
            else:
                Sd.op(eng, lambda e: e.tensor_scalar(out=out, in0=in0, scalar1=s1, scalar2=s2, op0=op0, op1=op1), [out], ins)

        def stt(out, in0, scalar, in1, op0, op1):
            ins = [in0, in1] + ([scalar] if not isinstance(scalar, (int, float)) else [])
            Sd.op("dve", lambda e: e.scalar_tensor_tensor(out=out, in0=in0, scalar=scalar, in1=in1, op0=op0, op1=op1), [out], ins)

        def act(out, in_, func, bias=None, scale=None, accum=None):
            ins = [in_]
            kw = {}
            if bias is not None:
                kw["bias"] = bias
                if not isinstance(bias, (int, float)):
                    ins.append(bias)
            if scale is not None:
                kw["scale"] = scale
                if not isinstance(scale, (int, float)):
                    ins.append(scale)
            outs = [out]
            if accum is not None:
                kw["accum_out"] = accum
                outs.append(accum)
            Sd.op("act", lambda e: e.activation(out=out, in_=in_, func=func, **kw), outs, ins)

        def cp(eng, out, in_):
            if eng == "act":
                act(out, in_, AF.Copy)
            else:
                Sd.op(eng, lambda e: e.tensor_copy(out=out, in_=in_), [out], [in_])

        def mm(out, lhsT, rhs, start=True, stop=True):
            Sd.op("pe", lambda e: e.matmul(out, lhsT=lhsT, rhs=rhs, start=start, stop=stop), [out], [lhsT, rhs])

        def ms(eng, out, val):
            Sd.op(eng, lambda e: e.memset(out, val), [out], [])

        def apx(base, tail, off=0, parts=None):
            p = list(base.ap[0])
            if parts is not None:
                p[1] = parts
            return bass.AP(base.tensor, base.offset + off, [p] + [list(t) for t in tail])

        def bank(i, lo=0, hi=512):
            return PS[:, i * 512 + lo:i * 512 + hi]

        def dump(name, ap, parts=128):
            if dbg is None or name not in dbg or dbg[name] is not None:
                return
            n = 1
            for s in ap.shape[1:]:
                n *= s
            if ap.dtype == BF16:
                tmp = dbgtmp[0:parts, 0:n]
                cp("dve", tmp, ap.rearrange("p a b -> p (a b)") if len(ap.shape) == 3 else ap)
                src = tmp
            else:
                src = ap
            dbg[name] = (dbg_off[0], parts, tuple(ap.shape[1:]))
            dst = dbg_d.ap()[0:parts, dbg_off[0]:dbg_off[0] + n]
            if len(ap.shape) == 3 and ap.dtype != BF16:
                dst = dst.rearrange("p (a b) -> p a b", b=ap.shape[2])
            Sd.dma(dst, src, out_onchip=False, in_onchip=True, is_output=True)
            dbg_off[0] += n

        win_b = bfv(ar.take(8 * IN_DIM // 2)).rearrange("p (k n) -> p k n", n=IN_DIM)
        wout_b = bfv(ar.take(4096)).rearrange("p (k n) -> p k n", n=1024)
        wpg_b = bfv(ar.take(4096)).rearrange("p (k n) -> p k n", n=1024)
        wpp_b = bfv(ar.take(1024)).rearrange("p (k n) -> p k n", n=1024)
        lora_b = bfv(ar.take(512)).rearrange("p (j n) -> p j n", n=512)
        consts = ar.take(NCONST)
        ident = consts[:, 0:128]
        mset = consts[:, 128:640].rearrange("p (a b) -> p a b", b=128)
        m_le = consts[:, 256:384]
        nmgt = consts[:, 640:768]
        m_gt = consts[:, 768:896]
        rot = consts[:, 1024:1152]
        rsp = consts[:, 1152:1664]
        rss = consts[:, 1664:2176]
        pvec = ar.take(NV + 27)
        mu_c = pvec[:, 0:13]; kk_c = pvec[:, 13:17]; ka_c = pvec[:, 17:21]; rk_c = pvec[:, 21:25]
        w0_c = pvec[:, 25:29]; a0_c = pvec[:, 29:33]; gn_c = pvec[:, 33:41]; gp_c = pvec[:, 41:49]
        sk_c = pvec[:, 49:53]; omka_c = pvec[:, 53:57]; es_c = pvec[:, 57:61]; hw0_c = pvec[:, 61:65]; ha0_c = pvec[:, 65:69]; mhalf_c = pvec[:, 69:77]
        lnw_bc = ar.take(512); lnb_bc = ar.take(512); gfin_bc = ar.take(1024)
        cb = bfv(ar.take(256))
        bones_b = cb[:, 0:128]; hsel_b = cb[:, 128:130]; opad_b = cb[:, 136:392].rearrange("p (a b) -> p a b", b=128)
        stat = ar.take(64)
        dbgtmp = ar.take(1024) if dbg is not None else None

        xin = ar.take(1024); xin2 = ar.take(1024); xins = [xin, xin2]; pin = ar.take(256); cs = ar.take(256).rearrange("p (a b) -> p a b", b=128)
        sgr = ar.take(512).rearrange("p (a b) -> p a b", b=128)
        sga = ar.take(512).rearrange("p (a b) -> p a b", b=128)
        qf = ar.take(512).rearrange("p (a b) -> p a b", b=128)
        kvf = ar.take(256).rearrange("p (a b) -> p a b", b=128)
        xT = bfv(ar.take(512)).rearrange("p (k t) -> p k t", t=128)
        mixT = bfv(ar.take(512)).rearrange("p (k t) -> p k t", t=128)
        qr_b = bfv(ar.take(256)).rearrange("p (k t) -> p k t", t=128)
        kr32 = ar.take(128)
        krpad = [bfv(ar.take(128)).rearrange("p (a t) -> p a t", t=128) for _ in range(2)]
        vpad = [bfv(ar.take(128)).rearrange("p (a t) -> p a t", t=128) for _ in range(2)]
        vat32 = ar.take(128); kat32 = ar.take(128)
        Hst = ar.take(256).rearrange("p (j v) -> p j v", v=64)
        Hbf = bfv(ar.take(128)).rearrange("p (j v) -> p j v", v=64)
        tmpH = ar.take(256).rearrange("p (j v) -> p j v", v=64)
        ARpad = bfv(ar.take(1024)).rearrange("p (c a q t) -> p c a q t", c=4, a=2, q=2)
        BKpad = bfv(ar.take(1024)).rearrange("p (c a q t) -> p c a q t", c=4, a=2, q=2)
        khat_tm = bfv(ar.take(256)); bhat_tm = bfv(ar.take(256)); v_tm = bfv(ar.take(256)); v_tm32 = ar.take(512)
        cbon = ar.take(8)
        zcarry = ar.take(16)
        eLinc = ar.take(512)
        ytm = ar.take(512); Xb = bfv(ar.take(128))

        u1b, u1e = ar.sub(1872 + 1024)
        zbuf_flat = A[:, u1b:u1b + 1872]
        zs = A[:, u1b + 1872:u1b + 1872 + 1664].rearrange("p (c t) -> p c t", t=128)
        ar.sub(1664 - 1024)
        a1 = Arena(A, u1b, u1e)
        Aev = bfv(a1.take(1024)).rearrange("p (h q t) -> p h q t", h=4, q=4)
        N0b = bfv(a1.take(256)).rearrange("p (h t) -> p h t", t=128)
        Pb = [bfv(a1.take(256)).rearrange("p (h t) -> p h t", t=128) for _ in range(2)]
        PTb = [bfv(a1.take(256)).rearrange("p (h t) -> p h t", t=128) for _ in range(2)]
        Xf = a1.take(256); Uneg = bfv(a1.take(256))

        u2b, u2e = ar.sub(512 * 10 + 64 + 256)
        assert u2b == u1e + 640
        a2_ = Arena(A, u2b, u2e)
        Wk = [a2_.take(512) for _ in range(10)]
        L12 = bfv(a2_.take(64)); sqb = bfv(a2_.take(256))
        xs32 = A[:, u2b:u2b + 1024]
        PTp = bfv(A[:, u1b:u1b + 512]).rearrange("p (h t) -> p h t", t=128)
        PTc = bfv(A[:, u1b + 512:u1b + 1024]).rearrange("p (h t) -> p h t", t=128)
        rden = A[:, u1b + 1024:u1b + 1536]; g2 = rden
        tmpq_b = A[:, u1b + 1024:u1b + 1536]; tmpk_b = A[:, u1b + 1536:u1b + 1664]
        gn1 = Wk[4]; gn2 = Wk[5]
        yout = A[:, u2b + 4 * 512:u2b + 6 * 512]
        hn32 = A[:, u2b + 6 * 512:u2b + 8 * 512]
        hnT = bfv(Wk[8]).rearrange("p (k t) -> p k t", t=128)
        pT = bfv(Wk[9][:, 0:128]).rearrange("p (k t) -> p k t", t=128)
        shs = A[0:nsb, u2b + 1024:u2b + 1024 + SHIFT_DIM] if nsb > 0 else None
        ws_st = Wk[6][0:64, :]; wo_st = Wk[7][0:64, :]; yall = Wk[8]
        kc_st = Wk[8][:, 0:128]; vc_st = Wk[8][:, 128:256]
        stage = [A[:, u1b:u1b + IN_DIM], A[:, u1b + IN_DIM:u1b + 2 * IN_DIM]]
        assert u1b + 2 * IN_DIM <= ar.off
        print("arena words used", ar.off, "of", AW)

        Sd.dma(consts, consts_d.ap())
        Sd.dma(pvec[:, 0:NV], pvec_d.ap())
        Sd.dma(lnw_bc, bass.AP(lnw_d, 0, [[0, 128], [1, 512]]))
        Sd.dma(lnb_bc, bass.AP(lnb_d, 0, [[0, 128], [1, 512]]))
        Sd.dma(gfin_bc, bass.AP(gfin_d, 0, [[0, 128], [1, 1024]]))
        cp("dve", bones_b, consts[:, 896:1024])
        cp("dve", hsel_b, consts[:, 2176:2178])
        cp("dve", cb[:, 136:392], consts[:, 2178:2434])
        negprev = consts[:, 896:1024]; negcur = consts[:, 2178:2306]
        ts("dve", negprev, m_le, -30000.0, None, ALU.mult)
        ts("dve", negcur, m_gt, -30000.0, None, ALU.mult)
        ts("dve", omka_c, ka_c, -1.0, 1.0, ALU.mult, ALU.add)
        act(es_c, sk_c, AF.Exp)
        ts("dve", hw0_c, w0_c, 0.5, None, ALU.mult)
        ts("dve", ha0_c, a0_c, 0.5, None, ALU.mult)
        ms("dve", mhalf_c, -0.5)
        lst = stage[0][:, 0:1024].rearrange("p (j n) -> p j n", n=512)
        ms("pool", stage[0][:, 0:1024], 0.0)
        Sd.dma(lst[0:64, 0, :], w2.ap())
        Sd.dma(lst[64:128, 1, :], a2.ap())
        cp("dve", lora_b, lst)
        si = 0
        for k in range(8):
            sg = stage[si % 2]; si += 1
            Sd.dma(sg, w_in.ap()[k * 128:(k + 1) * 128, :])
            if k % 2 == 0:
                act(win_b[:, k, :], sg, AF.Copy, scale=gn_c[:, k:k + 1])
            else:
                ts("dve", win_b[:, k, :], sg, gn_c[:, k:k + 1], None, ALU.mult)
        for (wd, wb, gc, nk) in ((w_out, wout_b, None, 8), (w_pg, wpg_b, gp_c, 8), (w_pp, wpp_b, None, 2)):
            for k0 in range(0, nk, 2):
                sg = stage[si % 2]; si += 1
                sgv = sg[:, 0:2048].rearrange("p (k n) -> p k n", n=1024)
                Sd.dma(sgv, wd.ap()[k0 * 128:(k0 + 2) * 128, :].rearrange("(k p) n -> p k n", p=128))
                for kk_ in range(2):
                    k = k0 + kk_
                    if gc is None:
                        sc_ = 0.5 if wd is w_out else 1.0
                        if kk_ == 0:
                            act(wb[:, k, :], sgv[:, kk_, :], AF.Copy, scale=sc_)
                        else:
                            ts("dve", wb[:, k, :], sgv[:, kk_, :], sc_, None, ALU.mult)
                    elif kk_ == 0:
                        act(wb[:, k, :], sgv[:, kk_, :], AF.Copy, scale=gc[:, k:k + 1])
                    else:
                        ts("dve", wb[:, k, :], sgv[:, kk_, :], gc[:, k:k + 1], None, ALU.mult)
        ms("pool", ARpad.rearrange("p c a q t -> p (c a q t)"), 0.0)
        ms("pool", BKpad.rearrange("p c a q t -> p (c a q t)"), 0.0)
        for i in range(2):
            ms("pool", krpad[i].rearrange("p a t -> p (a t)"), 0.0)
            ms("pool", vpad[i].rearrange("p a t -> p (a t)"), 0.0)

        gbank = [0]
        marks = []

        def mark(lbl):
            marks.append((lbl, dict(Sd.cnt)))

        def gb():
            b = gbank[0]
            gbank[0] = (gbank[0] + 1) % 3
            return b

        def rsqrt_col(dst, src, mult, eps):
            n_ = dst.shape[1]
            p0_ = dst.offset // dst.ap[0][0]
            pw = mhalf_c[p0_:p0_ + dst.shape[0], 0:n_]
            ts("dve", dst, src, mult, eps, ALU.mult, ALU.add)
            tt("pool", dst, dst, pw, ALU.pow)

        def scan_pass(t0, C, lv):
            HG = 4 if C == 128 else 8
            if HG == 4:
                Aev_, N0_, P_, PT_, Xf_, Xb_ = Aev, N0b, Pb, PTb, Xf, Xb
            else:
                Aev_ = Aev.rearrange("p h q t -> p (h q t)").rearrange("p (h q t) -> p h q t", h=8, q=4)
                N0_ = N0b.rearrange("p h t -> p (h t)").rearrange("p (h t) -> p h t", h=8)
                P_ = [x.rearrange("p h t -> p (h t)").rearrange("p (h t) -> p h t", h=8) for x in Pb]
                PT_ = [x.rearrange("p h t -> p (h t)").rearrange("p (h t) -> p h t", h=8) for x in PTb]
                Xf_ = gn2; Xb_ = bfv(gn1[:, 0:256])
            HW = HG * 64

            def abank(hl):
                return bank(hl)[0:C, 0:4 * C] if C == 128 else bank(3)[0:C, hl * 4 * C:(hl + 1) * 4 * C]
            n0bank = bank(4)[0:C, 0:HG * C] if C == 128 else bank(3)[0:C, 256:256 + HG * C]
            for g in range(8 // HG):
                heads = [(g * (HG // 2) + jj, par) for jj in range(HG // 2) for par in range(2)]
                j0 = g * (HG // 2); nj = HG // 2
                for hl, (j, par) in enumerate(heads):
                    rhs_ar = ARpad[:, j, par, :, t0:t0 + C]
                    mm(abank(hl)[:, 0:2 * C].rearrange("p (q t) -> p q t", t=C), BKpad[:, j, 0, 0, t0:t0 + C], rhs_ar)
                    mm(abank(hl)[:, 2 * C:4 * C].rearrange("p (q t) -> p q t", t=C), BKpad[:, j, 0, 1, t0:t0 + C], rhs_ar)
                for hl, (j, par) in enumerate(heads):
                    mm(n0bank[:, hl * C:(hl + 1) * C], ARpad[:, j, par, 0, t0:t0 + C], BKpad[:, j, 0, 0, t0:t0 + C])
                if C == 128:
                    for hl in range(HG):
                        tt("dve", Aev_[0:C, hl, :, 0:C], abank(hl).rearrange("p (q t) -> p q t", t=C), mset[0:C, :, 0:C], ALU.mult)
                else:
                    for q in range(4):
                        tt("dve", Aev_[0:C, :, q, 0:C], bank(3)[0:C, 0:HG * 4 * C].rearrange("p (h q t) -> p h q t", h=HG, q=4)[:, :, q, :],
                           apx(mset[0:C, q, 0:C], [[0, HG], [1, C]]), ALU.mult)
                tt("dve", N0_[0:C, :, 0:C], n0bank.rearrange("p (h t) -> p h t", t=C),
                   apx(nmgt[0:C, 0:C], [[0, HG], [1, C]]), ALU.mult)
                for hl, (j, par) in enumerate(heads):
                    h = 2 * j + par
                    mm(bank(5)[0:C, hl * 64:(hl + 1) * 64], ARpad[:, j, par, 0, t0:t0 + C], Hbf[:, j, :], True, False)
                    mm(bank(5)[0:C, hl * 64:(hl + 1) * 64], Aev_[0:C, hl, 2, 0:C], v_tm[0:C, h * 64:(h + 1) * 64], False, True)
                cp("act", Xf_[0:C, 0:HW], bank(5)[0:C, 0:HW])
                cp("dve", Xb_[0:C, 0:HW], bank(5)[0:C, 0:HW])
                PTcur = [Aev_[0:C, hl, 0, 0:C] for hl in range(HG)]
                Pcur = [N0_[0:C, hl, 0:C] for hl in range(HG)]
                hh_ = HG // 2
                subs = [(0, hh_, 5, 6, 7), (hh_, HG, 0, 1, 2)] if C == 128 else [(0, HG, 5, 6, 7)]
                for i in range(lv):
                    for (h0, h1, bru, bp_, bpt) in subs:
                        for hl in range(h0, h1):
                            mm(bank(bru)[0:C, hl * 64:(hl + 1) * 64], PTcur[hl], Xb_[0:C, hl * 64:(hl + 1) * 64])
                        if i < lv - 2:
                            for hl in range(h0, h1):
                                mm(bank(bp_)[0:C, hl * C:(hl + 1) * C], PTcur[hl], Pcur[hl])
                        if i < lv - 1:
                            for hl in range(h0, h1):
                                mm(bank(bpt)[0:C, hl * C:(hl + 1) * C], Pcur[hl], PTcur[hl])
                    pn = P_[i % 2]; ptn = PT_[i % 2]
                    for (h0, h1, bru, bp_, bpt) in subs:
                        cs_ = slice(h0 * 64, h1 * 64)
                        tt("dve", Xf_[0:C, cs_], Xf_[0:C, cs_], bank(bru)[0:C, cs_], ALU.add)
                        if i < lv - 1:
                            cp("act", Xb_[0:C, cs_], Xf_[0:C, cs_])
                            if i < lv - 2:
                                cp("act", pn[0:C, h0:h1, 0:C], bank(bp_)[0:C, h0 * C:h1 * C].rearrange("p (h t) -> p h t", t=C))
                            cp("dve", ptn[0:C, h0:h1, 0:C], bank(bpt)[0:C, h0 * C:h1 * C].rearrange("p (h t) -> p h t", t=C))
                    if i < lv - 1:
                        Pcur = [pn[0:C, hl, 0:C] for hl in range(HG)]
                        PTcur = [ptn[0:C, hl, 0:C] for hl in range(HG)]
                act(Uneg[0:C, g * HW:(g + 1) * HW], Xf_[0:C, 0:HW], AF.Copy, scale=-1.0)
                for hl, (j, par) in enumerate(heads):
                    h = 2 * j + par
                    o_ = bank(6)[0:C, hl * 64:(hl + 1) * 64]
                    mm(o_, ARpad[:, j, par, 1, t0:t0 + C], Hbf[:, j, :], True, False)
                    mm(o_, Aev_[0:C, hl, 3, 0:C], v_tm[0:C, h * 64:(h + 1) * 64], False, False)
                    mm(o_, Aev_[0:C, hl, 1, 0:C], Uneg[0:C, h * 64:(h + 1) * 64], False, True)
                cp("act", ytm[0:C, g * HW:(g + 1) * HW], bank(6)[0:C, 0:HW])
                for jj in range(nj):
                    j = j0 + jj
                    o_ = bank(7)[:, jj * 128:(jj + 1) * 128]
                    mm(o_, khat_tm[0:C, j * 128:(j + 1) * 128], v_tm[0:C, j * 128:(j + 1) * 128], True, False)
                    mm(o_, bhat_tm[0:C, j * 128:(j + 1) * 128], Uneg[0:C, j * 128:(j + 1) * 128], False, True)
                wc = apx(eLinc[:, t0 + C - 1:t0 + C], [[128, nj], [0, 64]], off=j0 * 128)
                tt("pool", tmpH[:, j0:j0 + nj, :], Hst[:, j0:j0 + nj, :], wc, ALU.mult)
                for par in range(2):
                    pr = slice(par * 64, par * 64 + 64)
                    src = bank(7)[pr, 0:nj * 128].rearrange("p (j v) -> p j v", v=128)[:, :, par * 64:par * 64 + 64]
                    tt("dve", Hst[pr, j0:j0 + nj, :], tmpH[pr, j0:j0 + nj, :], src, ALU.add)
            cp("act", Hbf, Hst)

        def tm_prep(khat, bhat, t0, C, want32=True, which=(0, 1, 2)):
            jobs = ((khat, khat_tm, None), (bhat, bhat_tm, None), (zs[:, 8:12, :], v_tm, v_tm32 if want32 else None))
            for src, dst, dst32 in [jobs[w] for w in which]:
                b_ = gb()
                for c in range(4):
                    Sd.op("pe", lambda e, o=bank(b_)[0:C, c * 128:(c + 1) * 128], i=src[:, c, t0:t0 + C]: e.transpose(out=o, in_=i, identity=ident),
                          [bank(b_)[0:C, c * 128:(c + 1) * 128]], [src[:, c, t0:t0 + C], ident])
                if dst32 is not None:
                    cp("act", dst32[0:C, :], bank(b_)[0:C, :])
                    cp("dve", dst[0:C, :], bank(b_)[0:C, :])
                else:
                    cp("act" if dst is khat_tm else "dve", dst[0:C, :], bank(b_)[0:C, :])

        def gn_bonus(C):
            v3 = v_tm32[0:C, :].rearrange("p (h v) -> p h v", v=64)
            tt("pool", v3, v3, apx(cbon[0:C, 0:8], [[1, 8], [0, 64]]), ALU.mult)
            tt("pool", v_tm32[0:C, :], v_tm32[0:C, :], lnb_bc[0:C, :], ALU.add)

        def gn_out(t0, C, mrbank, ysrc=None):
            ysrc = ytm if ysrc is None else ysrc
            y3 = ysrc[0:C, :].rearrange("p (h v) -> p h v", v=64)
            s1 = stat[0:C, 0:8]; s2 = stat[0:C, 8:16]; msq = stat[0:C, 24:32]
            bc8 = lambda s: apx(s, [[1, 8], [0, 64]])
            sq = gn2[0:C, :]
            act(sq, ysrc[0:C, :], AF.Square)
            Sd.op("dve", lambda e: e.tensor_reduce(out=s1, in_=y3, axis=AX.X, op=ALU.add), [s1], [y3])
            ts("dve", s1, s1, 1.0 / 64, None, ALU.mult)
            Sd.op("dve", lambda e: e.tensor_reduce(out=s2, in_=sq.rearrange("p (h v) -> p h v", v=64), axis=AX.X, op=ALU.add), [s2], [sq])
            ts("dve", s2, s2, 1.0 / 64, GN_EPS, ALU.mult, ALU.add)
            tt("dve", msq, s1, s1, ALU.mult)
            tt("dve", s2, s2, msq, ALU.subtract)
            tt("pool", s2, s2, mhalf_c[0:C, 0:8], ALU.pow)
            yc = gn1[0:C, :].rearrange("p (h v) -> p h v", v=64)
            tt("dve", yc, y3, bc8(s1), ALU.subtract)
            tt("dve", gn1[0:C, :], gn1[0:C, :], lnw_bc[0:C, :], ALU.mult)
            tt("dve", yc, yc, bc8(s2), ALU.mult)
            tt("dve", gn1[0:C, :], gn1[0:C, :], v_tm32[0:C, :], ALU.add)
            for c in range(4):
                o_ = mrbank[:, c * 128 + t0:c * 128 + t0 + C]
                i_ = gn1[0:C, c * 128:(c + 1) * 128]
                Sd.op("pe", lambda e, o=o_, i=i_: e.transpose(out=o, in_=i, identity=ident[0:C, 0:C]), [o_], [i_, ident[0:C, 0:C]])

        def attn_qk(t0, C, cur, prev, has_prev):
            if has_prev:
                for hb in range(2):
                    b_ = hb
                    for hh in range(4):
                        hq = hb * 4 + hh
                        c, par = hq % 4, hq // 4
                        mm(bank(b_)[:, hh * C:(hh + 1) * C], krpad[prev][:, par, :], qr_b[:, c, t0:t0 + C])
                    bv_ = bank(b_)[:, 0:4 * C].rearrange("p (h t) -> p h t", t=C)
                    tt("dve", bv_, bv_, apx(negprev[:, 0:C], [[0, 4], [1, C]]), ALU.add)
                    act(PTp[:, hb * 4:hb * 4 + 4, 0:C], bv_, AF.Exp, scale=0.125)
            for hb in range(2):
                b_ = 2 + hb
                for hh in range(4):
                    hq = hb * 4 + hh
                    c, par = hq % 4, hq // 4
                    mm(bank(b_)[0:C, hh * C:(hh + 1) * C], krpad[cur][:, par, t0:t0 + C], qr_b[:, c, t0:t0 + C])
                bv_ = bank(b_)[0:C, 0:4 * C].rearrange("p (h t) -> p h t", t=C)
                tt("dve", bv_, bv_, apx(negcur[0:C, 0:C], [[0, 4], [1, C]]), ALU.add)
                act(PTc[0:C, hb * 4:hb * 4 + 4, 0:C], bv_, AF.Exp, scale=0.125)

        def attn_pv(t0, C, cur, prev, has_prev):
            for c in range(4):
                oo = bank(6)[:, c * 128 + t0:c * 128 + t0 + C]
                dd = bank(7)[:, c * 128 + t0:c * 128 + t0 + C]
                seq = []
                if has_prev:
                    seq += [(vpad[prev][:, par, :], opad_b[:, par, :], PTp[:, par * 4 + c, 0:C]) for par in range(2)]
                seq += [(vpad[cur][t0:t0 + C, par, :] if C == 128 else vpad[cur][0:C, par, :], opad_b[0:C, par, :], PTc[0:C, par * 4 + c, 0:C]) for par in range(2)]
                for i, (vv, on, pt) in enumerate(seq):
                    mm(oo, vv, pt, i == 0, i == len(seq) - 1)
                for i, (vv, on, pt) in enumerate(seq):
                    mm(dd, on, pt, i == 0, i == len(seq) - 1)


        def attn_pass(t0, C, cur, prev, has_prev):
            attn_qk(t0, C, cur, prev, has_prev)
            attn_pv(t0, C, cur, prev, has_prev)

        def load_x(T):
            Sd.dma(xins[T["gi"] % 2], T["xrows"])

        def p1(T):
            xin = xins[T["gi"] % 2]
            mark('P1 %s%d' % (T["kind"], T["ti"]))
            ss = stat[:, 16:17]
            act(xs32, xin, AF.Square, accum=ss)
            rsqrt_col(ss, ss, 1.0 / 1024, NORM_EPS)
            act(xs32, xin, AF.Copy, scale=ss)
            for hb in range(2):
                b_ = gb()
                for kk_ in range(4):
                    k = hb * 4 + kk_
                    Sd.op("pe", lambda e, o=bank(b_)[:, kk_ * 128:(kk_ + 1) * 128], i=xs32[:, k * 128:(k + 1) * 128]: e.transpose(out=o, in_=i, identity=ident),
                          [bank(b_)[:, kk_ * 128:(kk_ + 1) * 128]], [xs32[:, k * 128:(k + 1) * 128], ident])
                cp("dve" if hb == 0 else "act", xT[:, hb * 4:hb * 4 + 4, :], bank(b_).rearrange("p (k t) -> p k t", t=128))

        def body_front(T):
            kind, ti, seq, first_of_seq, last_of_seq = T["kind"], T["ti"], T["seq"], T["first"], T["last"]
            prows, yrows, pos0 = T["prows"], T["yrows"], T["pos0"]
            xin = xins[T["gi"] % 2]
            nb, C = (1, 128) if kind == "p" else (nsb, 8)
            lv = 7 if kind == "p" else 3
            zb = zbuf_flat[:, 0:13 * nb * (C + 1)].rearrange("p (c n t) -> p c n t", c=13, n=nb)
            mark('P2')
            if kind == "p":
                if first_of_seq:
                    ms("pool", zb[:, :, 0, 0:1], 0.0)
                else:
                    cp("pool", zb[:, :, 0, 0], zcarry[:, 0:13])
            else:
                Sd.dma(shs, sh0.ap())
                b_ = gb()
                for c in range(13):
                    Sd.op("pe", lambda e, o=bank(b_)[:, c * nsb:(c + 1) * nsb], i=shs[:, c * 128:(c + 1) * 128]: e.transpose(out=o, in_=i, identity=ident[0:nsb, 0:nsb]),
                          [bank(b_)[:, c * nsb:(c + 1) * nsb]], [shs[:, c * 128:(c + 1) * 128], ident[0:nsb, 0:nsb]])
                cp("dve", zb[:, :, :, 0], bank(b_)[:, 0:13 * nsb].rearrange("p (c n) -> p c n", n=nsb))
            zcur = zb[:, :, :, 1:C + 1]
            zprev = zb[:, :, :, 0:C]
            zs4 = zs.rearrange("p c (n t) -> p c n t", n=nb)
            mark('P2')
            for (what, c0, n) in [("z", 12, 1), ("z", 4, 4), ("z", 0, 4), ("z", 8, 4)]:
                b_ = gb()
                for cc in range(n):
                    oc = c0 + cc
                    for k in range(8):
                        mm(bank(b_)[:, cc * 128:(cc + 1) * 128], win_b[:, k, oc * 128:(oc + 1) * 128], xT[:, k, :], k == 0, k == 7)
                src = bank(b_)[:, 0:n * 128]
                if what == "z":
                    cp("act", zb[:, c0:c0 + n, :, 1:C + 1], src.rearrange("p (c n t) -> p c n t", c=n, n=nb))
                    eng_ = "pool" if c0 == 0 else "dve"
                    tt(eng_, zs4[:, c0:c0 + n], zprev[:, c0:c0 + n], zcur[:, c0:c0 + n], ALU.subtract)
                    tt(eng_, zs[:, c0:c0 + n, :], zs[:, c0:c0 + n, :], apx(mu_c[:, c0:c0 + n], [[1, n], [0, 128]]), ALU.mult)
                    tt(eng_, zs4[:, c0:c0 + n], zs4[:, c0:c0 + n], zcur[:, c0:c0 + n], ALU.add)
                elif what in ("gr", "ga"):
                    tmpg = Wk[2] if what == "gr" else Wk[3]
                    dstg = sgr if what == "gr" else sga
                    act(tmpg, src, AF.Tanh, scale=0.5)
                    stt(dstg.rearrange("p c t -> p (c t)"), tmpg, 1.0, src, ALU.add, ALU.mult)
                elif what == "q":
                    cp("act", qf, src.rearrange("p (c t) -> p c t", t=128))
                else:
                    cp("act", kvf, src.rearrange("p (c t) -> p c t", t=128))
            if kind == "p":
                cp("dve", zcarry[:, 0:13], zb[:, :, 0, C])
                if last_of_seq:
                    b_ = gb()
                    Sd.op("pe", lambda e, o=bank(b_)[0:13, 0:128], i=zcarry[:, 0:13]: e.transpose(out=o, in_=i, identity=ident),
                          [bank(b_)[0:13, 0:128]], [zcarry[:, 0:13], ident])
                    stg = A[0:13, u2b:u2b + 128]
                    cp("act", stg, bank(b_)[0:13, 0:128])
                    Sd.dma(sh_p.ap()[seq:seq + 1, :].rearrange("a (c p) -> (a c) p", p=128), stg, out_onchip=False, in_onchip=True, is_output=True)
            else:
                zl = A[:, u2b:u2b + 13 * nsb]
                cp("dve", zl.rearrange("p (c n) -> p c n", n=nsb), zb[:, :, :, C])
                stg = A[0:nsb, u2b + 256:u2b + 256 + SHIFT_DIM]
                for c0_ in range(0, 13, 4):
                    n_ = min(4, 13 - c0_)
                    b_ = gb()
                    for cc in range(n_):
                        c = c0_ + cc
                        Sd.op("pe", lambda e, o=bank(b_)[0:nsb, cc * 128:(cc + 1) * 128], i=zl[:, c * nsb:(c + 1) * nsb]: e.transpose(out=o, in_=i, identity=ident),
                              [bank(b_)[0:nsb, cc * 128:(cc + 1) * 128]], [zl[:, c * nsb:(c + 1) * nsb], ident])
                    cp("act", stg[:, c0_ * 128:(c0_ + n_) * 128], bank(b_)[0:nsb, 0:n_ * 128])
                Sd.dma(sh_s.ap(), stg, out_onchip=False, in_onchip=True, is_output=True)

        def body_rest(T, nxt):
            kind, ti, seq, first_of_seq, last_of_seq = T["kind"], T["ti"], T["seq"], T["first"], T["last"]
            prows, yrows, pos0 = T["prows"], T["yrows"], T["pos0"]
            xin = xins[T["gi"] % 2]
            nb, C = (1, 128) if kind == "p" else (nsb, 8)
            lv = 7 if kind == "p" else 3
            zb = zbuf_flat[:, 0:13 * nb * (C + 1)].rearrange("p (c n t) -> p c n t", c=13, n=nb)
            zs4 = zs.rearrange("p c (n t) -> p c n t", n=nb)
            Sd.dma(pin, prows)
            Sd.dma(cs, rope_d.ap()[:, :, pos0:pos0 + 128].rearrange("a p t -> p a t"))
            if nxt is not None:
                load_x(nxt)
            if kind == "p":
                mark('P4')
                tm_prep(None, None, 0, 128, which=(2,))
                W = [w.rearrange("p (c t) -> p c t", t=128) for w in Wk]
                r3, kx3, v3 = zs[:, 0:4, :], zs[:, 4:8, :], zs[:, 8:12, :]
                act(L12[0:64, :], zs[0:64, 12, :], AF.Tanh)
                cp("act", L12[64:128, :], zs[64:128, 12, :])
                bw = gb(); ba = gb()
                for c in range(4):
                    mm(bank(bw)[:, c * 128:(c + 1) * 128], lora_b[:, 0, c * 128:(c + 1) * 128], L12)
                for c in range(4):
                    mm(bank(ba)[:, c * 128:(c + 1) * 128], lora_b[:, 1, c * 128:(c + 1) * 128], L12)
                sigw, a_, Linc, Lexc = W[0], W[1], Wk[2], Wk[3]
                for c in range(4):
                    act(sigw[:, c, :], bank(bw)[:, c * 128:(c + 1) * 128], AF.Tanh, bias=hw0_c[:, c:c + 1], scale=0.5)
                for c in range(4):
                    act(a_[:, c, :], bank(ba)[:, c * 128:(c + 1) * 128], AF.Tanh, bias=ha0_c[:, c:c + 1], scale=0.5)
                ts("dve", Wk[0], Wk[0], 0.5, 0.5, ALU.mult, ALU.add)
                ts("pool", Wk[1], Wk[1], 0.5, 0.5, ALU.mult, ALU.add)
                rsm = rsp if kind == "p" else rss
                Sd.op("dve", lambda e: e.tensor_tensor_scan(out=Linc, data0=rsm, data1=Wk[0], initial=0.0, op0=ALU.mult, op1=ALU.add), [Linc], [rsm, Wk[0]])
                tt("pool", Lexc, Linc, Wk[0], ALU.subtract)
                eLexc, emLinc, edk = Wk[4], Wk[5], Wk[6]
                act(eLinc, Linc, AF.Exp, scale=-C0)
                act(eLexc, Lexc, AF.Exp, scale=-C0)
                act(emLinc, Linc, AF.Exp, scale=C0)
                L4 = Linc.rearrange("p (c n t) -> p c n t", c=4, n=nb)
                ltot = apx(Linc[:, C - 1:C], [[128, 4], [C, nb], [0, C]])
                tt("dve", edk.rearrange("p (c n t) -> p c n t", c=4, n=nb), ltot, L4, ALU.subtract)
                act(edk, edk, AF.Exp, scale=-C0)
                for (what, c0, n) in [("gr", 13, 4), ("q", 17, 4)]:
                    b_ = gb()
                    for cc in range(n):
                        oc = c0 + cc
                        for k in range(8):
                            mm(bank(b_)[:, cc * 128:(cc + 1) * 128], win_b[:, k, oc * 128:(oc + 1) * 128], xT[:, k, :], k == 0, k == 7)
                    src = bank(b_)[:, 0:n * 128]
                    if what == "z":
                        cp("act", zb[:, c0:c0 + n, :, 1:C + 1], src.rearrange("p (c n t) -> p c n t", c=n, n=nb))
                        eng_ = "pool" if c0 == 0 else "dve"
                        tt(eng_, zs4[:, c0:c0 + n], zprev[:, c0:c0 + n], zcur[:, c0:c0 + n], ALU.subtract)
                        tt(eng_, zs[:, c0:c0 + n, :], zs[:, c0:c0 + n, :], apx(mu_c[:, c0:c0 + n], [[1, n], [0, 128]]), ALU.mult)
                        tt(eng_, zs4[:, c0:c0 + n], zs4[:, c0:c0 + n], zcur[:, c0:c0 + n], ALU.add)
                    elif what in ("gr", "ga"):
                        tmpg = Wk[2] if what == "gr" else Wk[3]
                        dstg = sgr if what == "gr" else sga
                        act(tmpg, src, AF.Tanh, scale=0.5)
                        stt(dstg.rearrange("p c t -> p (c t)"), tmpg, 1.0, src, ALU.add, ALU.mult)
                    elif what == "q":
                        cp("act", qf, src.rearrange("p (c t) -> p c t", t=128))
                    else:
                        cp("act", kvf, src.rearrange("p (c t) -> p c t", t=128))
                cur = ti % 2 if kind == "p" else 0
                prev = 1 - cur
                cosb = apx(cs[:, 0, :], [[0, 4], [1, 128]]); sinb = apx(cs[:, 1, :], [[0, 4], [1, 128]])
                bq_ = gb()
                for c in range(4):
                    mm(bank(bq_)[:, c * 128:(c + 1) * 128], rot, qf[:, c, :])
                tmpq = tmpq_b.rearrange("p (c t) -> p c t", t=128)
                tt("dve", tmpq, bank(bq_).rearrange("p (c t) -> p c t", t=128), sinb, ALU.mult)
                tt("pool", qf, qf, cosb, ALU.mult)
                tt("dve", qr_b, qf, tmpq, ALU.add)
                for (what, c0, n) in [("kv", 21, 2), ("ga", 23, 4)]:
                    b_ = gb()
                    for cc in range(n):
                        oc = c0 + cc
                        for k in range(8):
                            mm(bank(b_)[:, cc * 128:(cc + 1) * 128], win_b[:, k, oc * 128:(oc + 1) * 128], xT[:, k, :], k == 0, k == 7)
                    src = bank(b_)[:, 0:n * 128]
                    if what == "z":
                        cp("act", zb[:, c0:c0 + n, :, 1:C + 1], src.rearrange("p (c n t) -> p c n t", c=n, n=nb))
                        eng_ = "pool" if c0 == 0 else "dve"
                        tt(eng_, zs4[:, c0:c0 + n], zprev[:, c0:c0 + n], zcur[:, c0:c0 + n], ALU.subtract)
                        tt(eng_, zs[:, c0:c0 + n, :], zs[:, c0:c0 + n, :], apx(mu_c[:, c0:c0 + n], [[1, n], [0, 128]]), ALU.mult)
                        tt(eng_, zs4[:, c0:c0 + n], zs4[:, c0:c0 + n], zcur[:, c0:c0 + n], ALU.add)
                    elif what in ("gr", "ga"):
                        tmpg = Wk[2] if what == "gr" else Wk[3]
                        dstg = sgr if what == "gr" else sga
                        act(tmpg, src, AF.Tanh, scale=0.5)
                        stt(dstg.rearrange("p c t -> p (c t)"), tmpg, 1.0, src, ALU.add, ALU.mult)
                    elif what == "q":
                        cp("act", qf, src.rearrange("p (c t) -> p c t", t=128))
                    else:
                        cp("act", kvf, src.rearrange("p (c t) -> p c t", t=128))
                kkx = W[7]
                tt("dve", kkx, kx3, apx(kk_c, [[1, 4], [0, 128]]), ALU.mult)
                act(sqb, Wk[7], AF.Square)
                bq = gb()
                for c in range(4):
                    mm(bank(bq)[:, c * 128:(c + 1) * 128], bones_b, sqb[:, c * 128:(c + 1) * 128])
                rn = Wk[8]
                ts("dve", rn, bank(bq), 1e-18, None, ALU.max)
                act(rn, rn, AF.Ln)
                act(rn, rn, AF.Exp, scale=-0.5)
                tt("dve", Wk[7], Wk[7], rn, ALU.mult)
                t1 = W[8]
                for c in range(4):
                    ts("pool", t1[:, c, :], a_[:, c, :], ka_c[:, c:c + 1], omka_c[:, c:c + 1], ALU.mult, ALU.add)
                kf = W[9]
                tt("dve", kf, kx3, t1, ALU.mult)
                kka = W[8]
                tt("pool", kka, kkx, a_, ALU.mult)
                mark('P7')
                bk_ = gb()
                mm(bank(bk_)[:, 0:128], rot, kvf[:, 0, :])
                tmpk = tmpk_b
                tt("dve", tmpk, bank(bk_)[:, 0:128], cs[:, 1, :], ALU.mult)
                tt("pool", kr32, kvf[:, 0, :], cs[:, 0, :], ALU.mult)
                tt("dve", kr32, kr32, tmpk, ALU.add)
                dump("kr32", kr32); dump("qr", qr_b.rearrange("p c t -> p (c t)"))
                bt = gb()
                Sd.op("pe", lambda e: e.transpose(out=bank(bt)[:, 0:128], in_=kvf[:, 1, :], identity=ident), [bank(bt)[:, 0:128]], [kvf[:, 1, :], ident])
                Sd.op("pe", lambda e: e.transpose(out=bank(bt)[:, 128:256], in_=kr32, identity=ident), [bank(bt)[:, 128:256]], [kr32, ident])
                cp("act", vat32, bank(bt)[:, 0:128])
                cp("dve", kat32, bank(bt)[:, 128:256])
                for par in range(2):
                        pr = slice(par * 64, par * 64 + 64)
                        cp("act", krpad[cur][pr, par, :], kr32[pr, :])
                        cp("act", vpad[cur][:, par, par * 64:par * 64 + 64], vat32[:, par * 64:par * 64 + 64])
                attn_qk(0, 128, cur, prev, not first_of_seq)
                rkr = sqb.rearrange("p (c t) -> p c t", t=128)
                tt("dve", W[1], r3, kf, ALU.mult)
                tt("dve", rkr, W[1], apx(rk_c, [[1, 4], [0, 128]]), ALU.mult)
                for par in range(2):
                    pr = slice(par * 64, par * 64 + 64)
                    tt("dve", ARpad[pr, :, par, 0, :], kkx[pr], Wk[4].rearrange("p (c t) -> p c t", t=128)[pr], ALU.mult)
                    tt("pool", ARpad[pr, :, par, 1, :], r3[pr], eLinc.rearrange("p (c t) -> p c t", t=128)[pr], ALU.mult)
                tt("dve", BKpad[:, :, 0, 0, :], kka, Wk[5].rearrange("p (c t) -> p c t", t=128), ALU.mult)
                tt("pool", BKpad[:, :, 0, 1, :], kf, Wk[5].rearrange("p (c t) -> p c t", t=128), ALU.mult)
                khat, bhat = W[2], W[3]
                tt("dve", khat, kf, W[6], ALU.mult)
                tt("pool", bhat, kka, W[6], ALU.mult)
                dump("zs", zs.rearrange("p c t -> p (c t)")); dump("eLinc", eLinc); dump("kk", Wk[7]); dump("kf", Wk[9]); dump("khat", Wk[2])

                attn_pv(0, 128, cur, prev, not first_of_seq)
                if last_of_seq:
                    Sd.dma(k_p.ap()[seq], kat32, out_onchip=False, in_onchip=True, is_output=True)
                    Sd.dma(v_p.ap()[seq], vat32, out_onchip=False, in_onchip=True, is_output=True)
                for c in range(4):
                    act(rden[:, c * 128:(c + 1) * 128], bank(7)[:, c * 128:(c + 1) * 128], AF.Ln, bias=es_c[:, c:c + 1], scale=1.0)
                act(rden, rden, AF.Exp, scale=-1.0)
                tt("dve", g2, rden, sga.rearrange("p c t -> p (c t)"), ALU.mult)
                tt("dve", mixT[:, 4:8, :], bank(6).rearrange("p (c t) -> p c t", t=128), g2.rearrange("p (c t) -> p c t", t=128), ALU.mult)
                dump("mixT", mixT.rearrange("p c t -> p (c t)"))

            else:
                for (what, c0, n) in [("gr", 13, 4), ("q", 17, 4), ("kv", 21, 2), ("ga", 23, 4)]:
                    b_ = gb()
                    for cc in range(n):
                        oc = c0 + cc
                        for k in range(8):
                            mm(bank(b_)[:, cc * 128:(cc + 1) * 128], win_b[:, k, oc * 128:(oc + 1) * 128], xT[:, k, :], k == 0, k == 7)
                    src = bank(b_)[:, 0:n * 128]
                    if what == "z":
                        cp("act", zb[:, c0:c0 + n, :, 1:C + 1], src.rearrange("p (c n t) -> p c n t", c=n, n=nb))
                        eng_ = "pool" if c0 == 0 else "dve"
                        tt(eng_, zs4[:, c0:c0 + n], zprev[:, c0:c0 + n], zcur[:, c0:c0 + n], ALU.subtract)
                        tt(eng_, zs[:, c0:c0 + n, :], zs[:, c0:c0 + n, :], apx(mu_c[:, c0:c0 + n], [[1, n], [0, 128]]), ALU.mult)
                        tt(eng_, zs4[:, c0:c0 + n], zs4[:, c0:c0 + n], zcur[:, c0:c0 + n], ALU.add)
                    elif what in ("gr", "ga"):
                        tmpg = Wk[2] if what == "gr" else Wk[3]
                        dstg = sgr if what == "gr" else sga
                        act(tmpg, src, AF.Tanh, scale=0.5)
                        stt(dstg.rearrange("p c t -> p (c t)"), tmpg, 1.0, src, ALU.add, ALU.mult)
                    elif what == "q":
                        cp("act", qf, src.rearrange("p (c t) -> p c t", t=128))
                    else:
                        cp("act", kvf, src.rearrange("p (c t) -> p c t", t=128))
                mark('P7')
                cur = ti % 2 if kind == "p" else 0
                prev = 1 - cur
                cosb = apx(cs[:, 0, :], [[0, 4], [1, 128]]); sinb = apx(cs[:, 1, :], [[0, 4], [1, 128]])
                bq_ = gb()
                for c in range(4):
                    mm(bank(bq_)[:, c * 128:(c + 1) * 128], rot, qf[:, c, :])
                bk_ = gb()
                mm(bank(bk_)[:, 0:128], rot, kvf[:, 0, :])
                tmpq = tmpq_b.rearrange("p (c t) -> p c t", t=128)
                tt("dve", tmpq, bank(bq_).rearrange("p (c t) -> p c t", t=128), sinb, ALU.mult)
                tt("pool", qf, qf, cosb, ALU.mult)
                tt("dve", qr_b, qf, tmpq, ALU.add)
                tmpk = tmpk_b
                tt("dve", tmpk, bank(bk_)[:, 0:128], cs[:, 1, :], ALU.mult)
                tt("pool", kr32, kvf[:, 0, :], cs[:, 0, :], ALU.mult)
                tt("dve", kr32, kr32, tmpk, ALU.add)
                dump("kr32", kr32); dump("qr", qr_b.rearrange("p c t -> p (c t)"))
                bt = gb()
                Sd.op("pe", lambda e: e.transpose(out=bank(bt)[:, 0:128], in_=kvf[:, 1, :], identity=ident), [bank(bt)[:, 0:128]], [kvf[:, 1, :], ident])
                Sd.op("pe", lambda e: e.transpose(out=bank(bt)[:, 128:256], in_=kr32, identity=ident), [bank(bt)[:, 128:256]], [kr32, ident])
                cp("act", vat32, bank(bt)[:, 0:128])
                cp("dve", kat32, bank(bt)[:, 128:256])
                for par in range(2):
                    pr = slice(par * 64, par * 64 + 64)
                    cp("pool", krpad[0][pr, par, :], kr32[pr, :])
                for b in range(nsb):
                    Sd.dma(k_s.ap()[b, 0:120, :], ck.ap()[b, 8:128, :], out_onchip=False, in_onchip=False, is_output=True)
                    Sd.dma(v_s.ap()[b, 0:120, :], cv.ap()[b, 8:128, :], out_onchip=False, in_onchip=False, is_output=True)
                    Sd.dma(k_s.ap()[b, 120:128, :], kat32[b * 8:(b + 1) * 8, :], out_onchip=False, in_onchip=True, is_output=True)
                    Sd.dma(v_s.ap()[b, 120:128, :], vat32[b * 8:(b + 1) * 8, :], out_onchip=False, in_onchip=True, is_output=True)
                    kvb = [(Wk[8][:, 0:128], Wk[8][:, 128:256]), (Wk[8][:, 256:384], Wk[8][:, 384:512])]
                    if b == 0:
                        Sd.dma(kvb[0][0], ck.ap()[0]); Sd.dma(kvb[0][1], cv.ap()[0])
                    if b + 1 < nsb:
                        Sd.dma(kvb[(b + 1) % 2][0], ck.ap()[b + 1]); Sd.dma(kvb[(b + 1) % 2][1], cv.ap()[b + 1])
                    kc, vc = kvb[b % 2]
                    bt2 = gb()
                    Sd.op("pe", lambda e, o=bank(bt2)[:, 0:128], i=kc: e.transpose(out=o, in_=i, identity=ident), [bank(bt2)[:, 0:128]], [kc, ident])
                    Sd.op("pe", lambda e, o=bank(bt2)[0:8, 128:256], i=kvf[:, 1, b * 8:(b + 1) * 8]: e.transpose(out=o, in_=i, identity=ident),
                          [bank(bt2)[0:8, 128:256]], [kvf[:, 1, b * 8:(b + 1) * 8], ident])
                    for par in range(2):
                        pr = slice(par * 64, par * 64 + 64)
                        cp("act" if par == 0 else "dve", krpad[1][pr, par, :], bank(bt2)[pr, 0:128])
                        cp("pool", vpad[1][:, par, par * 64:par * 64 + 64], vc[:, par * 64:par * 64 + 64])
                        cp("act" if par == 0 else "dve", vpad[0][0:8, par, par * 64:par * 64 + 64], bank(bt2)[0:8, 128 + par * 64:128 + par * 64 + 64])
                    attn_pass(b * 8, 8, 0, 1, True)
                for c in range(4):
                    act(rden[:, c * 128:(c + 1) * 128], bank(7)[:, c * 128:(c + 1) * 128], AF.Ln, bias=es_c[:, c:c + 1], scale=1.0)
                act(rden, rden, AF.Exp, scale=-1.0)
                tt("dve", g2, rden, sga.rearrange("p c t -> p (c t)"), ALU.mult)
                tt("dve", mixT[:, 4:8, :], bank(6).rearrange("p (c t) -> p c t", t=128), g2.rearrange("p (c t) -> p c t", t=128), ALU.mult)
                dump("mixT", mixT.rearrange("p c t -> p (c t)"))

                mark('P4')
                W = [w.rearrange("p (c t) -> p c t", t=128) for w in Wk]
                r3, kx3, v3 = zs[:, 0:4, :], zs[:, 4:8, :], zs[:, 8:12, :]
                act(L12[0:64, :], zs[0:64, 12, :], AF.Tanh)
                cp("act", L12[64:128, :], zs[64:128, 12, :])
                bw = gb(); ba = gb()
                for c in range(4):
                    mm(bank(bw)[:, c * 128:(c + 1) * 128], lora_b[:, 0, c * 128:(c + 1) * 128], L12)
                for c in range(4):
                    mm(bank(ba)[:, c * 128:(c + 1) * 128], lora_b[:, 1, c * 128:(c + 1) * 128], L12)
                sigw, a_, Linc, Lexc = W[0], W[1], Wk[2], Wk[3]
                for c in range(4):
                    act(sigw[:, c, :], bank(bw)[:, c * 128:(c + 1) * 128], AF.Tanh, bias=hw0_c[:, c:c + 1], scale=0.5)
                for c in range(4):
                    act(a_[:, c, :], bank(ba)[:, c * 128:(c + 1) * 128], AF.Tanh, bias=ha0_c[:, c:c + 1], scale=0.5)
                ts("dve", Wk[0], Wk[0], 0.5, 0.5, ALU.mult, ALU.add)
                ts("pool", Wk[1], Wk[1], 0.5, 0.5, ALU.mult, ALU.add)
                rsm = rsp if kind == "p" else rss
                Sd.op("dve", lambda e: e.tensor_tensor_scan(out=Linc, data0=rsm, data1=Wk[0], initial=0.0, op0=ALU.mult, op1=ALU.add), [Linc], [rsm, Wk[0]])
                tt("pool", Lexc, Linc, Wk[0], ALU.subtract)
                eLexc, emLinc, edk = Wk[4], Wk[5], Wk[6]
                act(eLinc, Linc, AF.Exp, scale=-C0)
                act(eLexc, Lexc, AF.Exp, scale=-C0)
                act(emLinc, Linc, AF.Exp, scale=C0)
                L4 = Linc.rearrange("p (c n t) -> p c n t", c=4, n=nb)
                ltot = apx(Linc[:, C - 1:C], [[128, 4], [C, nb], [0, C]])
                tt("dve", edk.rearrange("p (c n t) -> p c n t", c=4, n=nb), ltot, L4, ALU.subtract)
                act(edk, edk, AF.Exp, scale=-C0)
                kkx = W[7]
                tt("dve", kkx, kx3, apx(kk_c, [[1, 4], [0, 128]]), ALU.mult)
                act(sqb, Wk[7], AF.Square)
                bq = gb()
                for c in range(4):
                    mm(bank(bq)[:, c * 128:(c + 1) * 128], bones_b, sqb[:, c * 128:(c + 1) * 128])
                rn = Wk[8]
                ts("dve", rn, bank(bq), 1e-18, None, ALU.max)
                act(rn, rn, AF.Ln)
                act(rn, rn, AF.Exp, scale=-0.5)
                tt("dve", Wk[7], Wk[7], rn, ALU.mult)
                t1 = W[8]
                for c in range(4):
                    ts("pool", t1[:, c, :], a_[:, c, :], ka_c[:, c:c + 1], omka_c[:, c:c + 1], ALU.mult, ALU.add)
                kf = W[9]
                tt("dve", kf, kx3, t1, ALU.mult)
                kka = W[8]
                tt("pool", kka, kkx, a_, ALU.mult)
                rkr = sqb.rearrange("p (c t) -> p c t", t=128)
                tt("dve", W[1], r3, kf, ALU.mult)
                tt("dve", rkr, W[1], apx(rk_c, [[1, 4], [0, 128]]), ALU.mult)
                for par in range(2):
                    pr = slice(par * 64, par * 64 + 64)
                    tt("dve", ARpad[pr, :, par, 0, :], kkx[pr], Wk[4].rearrange("p (c t) -> p c t", t=128)[pr], ALU.mult)
                    tt("pool", ARpad[pr, :, par, 1, :], r3[pr], eLinc.rearrange("p (c t) -> p c t", t=128)[pr], ALU.mult)
                tt("dve", BKpad[:, :, 0, 0, :], kka, Wk[5].rearrange("p (c t) -> p c t", t=128), ALU.mult)
                tt("pool", BKpad[:, :, 0, 1, :], kf, Wk[5].rearrange("p (c t) -> p c t", t=128), ALU.mult)
                khat, bhat = W[2], W[3]
                tt("dve", khat, kf, W[6], ALU.mult)
                tt("pool", bhat, kka, W[6], ALU.mult)
                dump("zs", zs.rearrange("p c t -> p (c t)")); dump("eLinc", eLinc); dump("kk", Wk[7]); dump("kf", Wk[9]); dump("khat", Wk[2])

            if nxt is not None:
                p1(nxt)
            mark('P5')
            mrb = bank(4)
            for pb in range(nb):
                t0 = pb * C
                if kind == "p":
                    if first_of_seq:
                        ms("pool", Hst.rearrange("p j v -> p (j v)"), 0.0)
                        ms("pool", Hbf.rearrange("p j v -> p (j v)"), 0.0)
                else:
                    wsb = [ws_st, Wk[0][0:64, :]]
                    if pb == 0:
                        Sd.dma(wsb[0].rearrange("p (h k) -> p h k", k=64), wkv0.ap()[0].rearrange("h v k -> v h k"))
                    if pb + 1 < nb:
                        Sd.dma(wsb[(pb + 1) % 2].rearrange("p (h k) -> p h k", k=64), wkv0.ap()[pb + 1].rearrange("h v k -> v h k"))
                    ws = wsb[pb % 2]
                    b_ = gb()
                    for j in range(4):
                        Sd.op("pe", lambda e, o=bank(b_)[:, j * 64:(j + 1) * 64], i=ws[:, j * 128:(j + 1) * 128]: e.transpose(out=o, in_=i, identity=ident[0:64, 0:64]),
                              [bank(b_)[:, j * 64:(j + 1) * 64]], [ws[:, j * 128:(j + 1) * 128], ident[0:64, 0:64]])
                    cp("act", Hst, bank(b_)[:, 0:256].rearrange("p (j v) -> p j v", v=64))
                    cp("dve", Hbf, bank(b_)[:, 0:256].rearrange("p (j v) -> p j v", v=64))
                if kind == "p":
                    b_ = gb()
                    for c in range(4):
                        mm(bank(b_)[0:C, 2 * c:2 * c + 2], rkr[:, c, t0:t0 + C], hsel_b)
                    cp("act", cbon[0:C, :], bank(b_)[0:C, 0:8])
                    gn_bonus(C)
                elif pb == 0:
                    b_ = gb()
                    for c in range(4):
                        mm(bank(b_)[:, 2 * c:2 * c + 2], rkr[:, c, :], hsel_b)
                    cp("act", cbon, bank(b_)[:, 0:8])
                    b_ = gb()
                    for c in range(4):
                        Sd.op("pe", lambda e, o=bank(b_)[:, c * 128:(c + 1) * 128], i=zs[:, 8 + c, :]: e.transpose(out=o, in_=i, identity=ident),
                              [bank(b_)[:, c * 128:(c + 1) * 128]], [zs[:, 8 + c, :], ident])
                    cp("act", v_tm32, bank(b_))
                if kind == "p":
                    tm_prep(khat, bhat, t0, C, which=(0, 1))
                else:
                    tm_prep(khat, bhat, t0, C, want32=False)
                    if pb == 0:
                        gn_bonus(128)
                scan_pass(t0, C, lv)
                if nxt is not None and pb == nb - 1:
                    body_front(nxt)
                if pb == 0:
                    dump("ytm", ytm); dump("Hst", Hst.rearrange("p j v -> p (j v)"))
                if kind == "p":
                    gn_out(t0, C, mrb)
                else:
                    Sd.dma(yall[t0:t0 + C, :], ytm[0:C, :], out_onchip=True, in_onchip=True)
                    if pb == nb - 1:
                        gn_out(0, 128, mrb, ysrc=yall)
                if kind == "s" or last_of_seq:
                    b_ = gb()
                    for j in range(4):
                        Sd.op("pe", lambda e, o=bank(b_)[0:64, j * 128:(j + 1) * 128], i=Hst[:, j, :]: e.transpose(out=o, in_=i, identity=ident),
                              [bank(b_)[0:64, j * 128:(j + 1) * 128]], [Hst[:, j, :], ident])
                    wo = wo_st
                    cp("act", wo, bank(b_)[0:64, :])
                    dst = (wkv_s.ap()[pb] if kind == "s" else wkv_p.ap()[seq]).rearrange("h v k -> v h k")
                    Sd.dma(dst, wo.rearrange("p (h k) -> p h k", k=64), out_onchip=False, in_onchip=True, is_output=True)
            tt("dve", mixT[:, 0:4, :], mrb.rearrange("p (c t) -> p c t", t=128), sgr, ALU.mult)

            mark('P8')
            for n in range(2):
                b_ = gb()
                for k in range(8):
                    mm(bank(b_), mixT[:, k, :], wout_b[:, k, n * 512:(n + 1) * 512], k == 0, k == 7)
                tt("dve", xin[:, n * 512:(n + 1) * 512], bank(b_), xin[:, n * 512:(n + 1) * 512], ALU.add)
            ss2 = stat[:, 17:18]
            act(hn32, xin, AF.Square, accum=ss2)
            rsqrt_col(ss2, ss2, 1.0 / 1024, NORM_EPS)
            hs2 = stat[:, 19:20]
            ts("dve", hs2, ss2, 0.5, None, ALU.mult)
            for hb in range(2):
                b_ = gb()
                for kk_ in range(4):
                    k = hb * 4 + kk_
                    Sd.op("pe", lambda e, o=bank(b_)[:, kk_ * 128:(kk_ + 1) * 128], i=xin[:, k * 128:(k + 1) * 128]: e.transpose(out=o, in_=i, identity=ident),
                          [bank(b_)[:, kk_ * 128:(kk_ + 1) * 128]], [xin[:, k * 128:(k + 1) * 128], ident])
                cp("dve" if hb == 0 else "act", hnT[:, hb * 4:hb * 4 + 4, :], bank(b_).rearrange("p (k t) -> p k t", t=128))
            b_ = gb()
            for k in range(2):
                Sd.op("pe", lambda e, o=bank(b_)[:, k * 128:(k + 1) * 128], i=pin[:, k * 128:(k + 1) * 128]: e.transpose(out=o, in_=i, identity=ident),
                      [bank(b_)[:, k * 128:(k + 1) * 128]], [pin[:, k * 128:(k + 1) * 128], ident])
            cp("act", pT, bank(b_)[:, 0:256].rearrange("p (k t) -> p k t", t=128))
            for n in range(2):
                bg = gb()
                for k in range(8):
                    mm(bank(bg), hnT[:, k, :], wpg_b[:, k, n * 512:(n + 1) * 512], k == 0, k == 7)
                gate = hn32[:, n * 512:(n + 1) * 512]
                act(gate, bank(bg), AF.Tanh, scale=hs2)
                bp = gb()
                for k in range(2):
                    mm(bank(bp), pT[:, k, :], wpp_b[:, k, n * 512:(n + 1) * 512], k == 0, k == 1)
                stt(gate, gate, 1.0, bank(bp), ALU.add, ALU.mult)
                stt(xin[:, n * 512:(n + 1) * 512], gate, 0.5, xin[:, n * 512:(n + 1) * 512], ALU.mult, ALU.add)
            ss3 = stat[:, 18:19]
            act(hn32, xin, AF.Square, accum=ss3)
            rsqrt_col(ss3, ss3, 1.0 / 1024, NORM_EPS)
            stt(yout, xin, ss3, gfin_bc, ALU.mult, ALU.mult)
            Sd.dma(yrows, yout, out_onchip=False, in_onchip=True, is_output=True)

        tiles = []
        for seq in range(nseq):
            for ti in range(NT):
                r0 = seq * S + ti * 128
                tiles.append(dict(kind="p", ti=ti, seq=seq, first=ti == 0, last=ti == NT - 1, xrows=xp.ap()[r0:r0 + 128, :],
                                  prows=pp.ap()[r0:r0 + 128, :], yrows=y_p.ap()[r0:r0 + 128, :], pos0=ti * 128))
        if nsb > 0 and not _SKIP_S:
            tiles.append(dict(kind="s", ti=0, seq=0, first=True, last=True, xrows=xsm.ap(), prows=psm.ap(), yrows=y_s.ap(), pos0=S))
        for gi, T in enumerate(tiles):
            T["gi"] = gi
        load_x(tiles[0])
        p1(tiles[0])
        body_front(tiles[0])
        for gi, T in enumerate(tiles):
            body_rest(T, tiles[gi + 1] if gi + 1 < len(tiles) else None)
        mark('END')
        Sd.finish()
        if _os.environ.get("KMARKS"):
            import json
            json.dump(marks, open(_os.environ["KMARKS"], "w"))
        print("instr counts", Sd.cnt, "waits", sum(1 for e in ENGS for it in Sd.ops[e] if it[0] == "wait"))
    return nc


def _host_consts(S):
    c = np.zeros((128, NCONST), np.float32)
    i = np.arange(128)
    s_, t_ = np.meshgrid(i, i, indexing="ij")
    c[:, 0:128] = np.eye(128)
    lt = (s_ < t_).astype(np.float32); le = (s_ <= t_).astype(np.float32); gt = (s_ > t_).astype(np.float32)
    c[:, 128:256] = -lt; c[:, 256:384] = le; c[:, 384:512] = lt; c[:, 512:640] = le
    c[:, 640:768] = -gt; c[:, 768:896] = gt
    c[:, 896:1024] = (s_ // 64 == t_ // 64)
    rot = np.zeros((128, 128), np.float32)
    for d in range(128):
        j = d % 64
        if j < 8:
            rot[d + 8, d] = 1
        elif j < 16:
            rot[d - 8, d] = 1
    c[:, 1024:1152] = rot
    rsp = np.ones((4, 128), np.float32); rsp[:, 0] = 0
    c[:, 1152:1664] = rsp.reshape(-1)[None]
    rss = np.ones(512, np.float32); rss[::8] = 0
    c[:, 1664:2176] = rss[None]
    c[:, 2176] = (i < 64); c[:, 2177] = (i >= 64)
    op = np.zeros((2, 128), np.float32); op[0, 0:64] = 1; op[1, 64:128] = 1
    c[:, 2178:2434] = op.reshape(-1)[None]
    npos = S + 128
    pos = np.concatenate([np.arange(S), np.tile(PAST_LEN + np.arange(8), 16)]).astype(np.float32)
    inv = (np.float32(500000.0) ** (-np.arange(8, dtype=np.float32) / np.float32(8))).astype(np.float32)
    ang = pos[:, None] * inv[None, :]
    co, si = np.cos(ang).astype(np.float32), np.sin(ang).astype(np.float32)
    rope = np.zeros((2, 128, npos), np.float32)
    rope[0] = 1.0
    for p in range(128):
        j = p % 64
        if j < 16:
            rope[0, p] = co[:, j % 8]
            rope[1, p] = -si[:, j % 8] if j < 8 else si[:, j % 8]
    return c, rope


_CACHE = {}
import os as _os
_PH = int(_os.environ.get("KPH", "9"))
_SKIP_S = bool(int(_os.environ.get("KSKIPS", "0")))


def _prep_weights(inp, S):
    n2o = np.array([(c + 4 * par) * 64 + d for c in range(4) for par in range(2) for d in range(64)])
    w_in = np.ascontiguousarray(inp["w_in"][0]).copy()
    o2 = SHIFT_DIM + 512
    o5 = o2 + 512 + 256
    w_in[:, o2:o2 + 512] = inp["w_in"][0][:, o2 + n2o]
    w_in[:, o5:o5 + 512] = inp["w_in"][0][:, o5 + n2o]
    w_out = np.ascontiguousarray(inp["w_out"][0]).copy()
    w_out[512:1024] = inp["w_out"][0][512 + n2o]
    pv = np.zeros((128, NV), np.float32)
    col = lambda v, n: np.asarray(v, np.float32).reshape(n, 128).T
    pv[:, 0:13] = col(inp["mu_shift"][0], 13)
    pv[:, 13:17] = col(inp["k_k"][0], 4); pv[:, 17:21] = col(inp["k_a"][0], 4); pv[:, 21:25] = col(inp["r_k"][0], 4)
    pv[:, 25:29] = col(inp["w0"][0], 4); pv[:, 29:33] = col(inp["a0"][0], 4)
    pv[:, 33:41] = col(inp["g_norm"][0], 8); pv[:, 41:49] = col(inp["g_ple"][0], 8)
    sk = np.asarray(inp["sinks"][0], np.float32)
    for c in range(4):
        for p in range(128):
            pv[p, 49 + c] = sk[c + 4 * (p // 64)]
    consts, rope = _host_consts(S)
    return dict(w_in=w_in, w_out=w_out, w_pg=np.ascontiguousarray(inp["w_ple_gate"][0]), w_pp=np.ascontiguousarray(inp["w_ple_proj"][0]),
                w2=np.ascontiguousarray(inp["w2"][0]), a2=np.ascontiguousarray(inp["a2"][0]), pvec=pv,
                lnw=np.ascontiguousarray(inp["ln_w"]).reshape(1, 512), lnb=np.ascontiguousarray(inp["ln_b"]).reshape(1, 512),
                gfin=np.ascontiguousarray(inp["g_final"]).reshape(1, 1024), consts=consts, rope=rope)


def run_cores(inp, n_cores, nseq, S, nsb, dbg=None):
    key = (nseq, S, nsb, dbg is not None)
    if key not in _CACHE:
        _CACHE[key] = build(nseq, S, nsb, dbg)
    nc = _CACHE[key]
    shared = _prep_weights(inp, S)
    f = lambda a: np.ascontiguousarray(a, dtype=np.float32)
    in_maps = []
    for c in range(n_cores):
        bs = slice(c * nseq, (c + 1) * nseq)
        ss = slice(c * nsb, (c + 1) * nsb)
        m = dict(shared)
        m["xp"] = f(inp["x_prompt"][bs]).reshape(nseq * S, 1024)
        m["pp"] = f(inp["p_prompt"][0, bs]).reshape(nseq * S, 256)
        m["xsm"] = f(inp["x_sample"][ss]).reshape(nsb * 8, 1024)
        m["psm"] = f(inp["p_sample"][0, ss]).reshape(nsb * 8, 256)
        m["wkv0"] = f(inp["state_rwkv_wkv"][0, ss])
        m["sh0"] = f(inp["state_rwkv_shift"][0, ss])
        m["ck"] = f(inp["cache_swa_k"][0, ss]).reshape(nsb, 128, 128)
        m["cv"] = f(inp["cache_swa_v"][0, ss]).reshape(nsb, 128, 128)
        in_maps.append(m)
    res = run_bass_kernel_spmd(nc, in_maps, core_ids=list(range(n_cores)))
    return res.results


def kernel(**inp):
    n = 8
    B, S = inp["x_prompt"].shape[0], inp["x_prompt"].shape[1]
    DB = inp["x_sample"].shape[0]
    nseq, nsb = B // n, DB // n
    r = run_cores(inp, n, nseq, S, nsb)
    cat = lambda k: np.concatenate([x[k] for x in r], axis=0)
    y_p = cat("y_p").reshape(B, S, 1024)
    y_s = cat("y_s").reshape(DB, 8, 1024)
    return (y_p, y_s,
            cat("wkv_p").reshape(1, B, 8, 64, 64), cat("sh_p").reshape(1, B, SHIFT_DIM),
            cat("k_p").reshape(1, B, 128, 2, 64), cat("v_p").reshape(1, B, 128, 2, 64),
            cat("wkv_s").reshape(1, DB, 8, 64, 64), cat("sh_s").reshape(1, DB, SHIFT_DIM),
            cat("k_s").reshape(1, DB, 128, 2, 64), cat("v_s").reshape(1, DB, 128, 2, 64))
```

```python
import contextlib
import numpy as np
import concourse.bass as bass
import concourse.mybir as mybir
from concourse.bass_utils import run_bass_kernel_spmd

F32 = mybir.dt.float32
BF16 = mybir.dt.bfloat16
AF = mybir.ActivationFunctionType
ALU = mybir.AluOpType
AX = mybir.AxisListType

ENGS = ("pe", "act", "dve", "pool", "sp")
D_MODEL = 1024
SHIFT_DIM = 1664
IN_DIM = 3456
PAST_LEN = 16384
C0 = float(np.exp(-0.5))
NORM_EPS = 1e-6
GN_EPS = 64e-5
NCONST = 2440
NV = 53


def _esize(dt):
    return 2 if dt == BF16 else 4


class Sched:
    def __init__(self, nc, n_dma_sems=48):
        self.nc = nc
        self.ops = {e: [] for e in ENGS}
        self.cnt = {e: 0 for e in ENGS}
        self.waited = {e: {} for e in ENGS}
        self.recs = {}
        self.n_dma_sems = n_dma_sems
        self.dma_vals = [0] * n_dma_sems
        self.dma_rr = 0
        self.out_dma = []

    @staticmethod
    def fp(ap):
        es = _esize(ap.dtype)
        pat = ap.ap
        pstep, pn = pat[0]
        off = ap.offset
        p0 = off // pstep
        col = off % pstep
        lo = col
        hi = col
        for s, c in pat[1:]:
            if s >= 0:
                hi += s * (c - 1)
            else:
                lo += s * (c - 1)
        return (ap.name, p0, p0 + pn, lo * es, (hi + 1) * es)

    @staticmethod
    def _ovl(a, b):
        return a[1] < b[2] and b[1] < a[2] and a[3] < b[4] and b[3] < a[4]

    @staticmethod
    def _covers(a, b):
        return a[1] <= b[1] and a[2] >= b[2] and a[3] <= b[3] and a[4] >= b[4]

    @staticmethod
    def _bank(f):
        return (f[0], 0, 128, (f[3] // 2048) * 2048, ((f[4] + 2047) // 2048) * 2048)

    def _deps(self, eng, reads, writes):
        deps = {}

        def add(r):
            if deps.get(r[1], 0) < r[2]:
                deps[r[1]] = r[2]
        for f in reads:
            ps = f[0] == "ps"
            fb = self._bank(f) if ps else None
            for r in self.recs.get(f[0], ()):
                if r[4] and self._ovl(f, r[0]):
                    add(r)
                elif ps and r[3] != eng and self._ovl(fb, r[5]):
                    add(r)
        for f in writes:
            ps = f[0] == "ps"
            fb = self._bank(f) if ps else None
            for r in self.recs.get(f[0], ()):
                if r[3] == eng and eng == "pe":
                    continue
                if self._ovl(f, r[0]) or (ps and r[3] != eng and self._ovl(fb, r[5])):
                    add(r)
        return deps

    def _record(self, eng, sk, val, reads, writes):
        for f in writes:
            lst = self.recs.setdefault(f[0], [])
            lst[:] = [r for r in lst if not self._covers(f, r[0])]
            lst.append((f, sk, val, eng, True, self._bank(f) if f[0] == "ps" else None))
        for f in reads:
            lst = self.recs.setdefault(f[0], [])
            if f[0] == "ps":
                fb = self._bank(f)
                lst[:] = [r for r in lst if r[4] or r[3] != eng or r[5] != fb]
                lst.append((f, sk, val, eng, False, fb))
                continue
            if eng != "dma":
                lst[:] = [r for r in lst if r[4] or r[3] != eng or r[0] != f]
            lst.append((f, sk, val, eng, False, None))

    def _emit_waits(self, eng, deps):
        for sk, val in deps.items():
            if self.waited[eng].get(sk, 0) >= val:
                continue
            self.waited[eng][sk] = val
            self.ops[eng].append(("wait", sk, val))

    def op(self, eng, fn, outs, ins):
        reads = [self.fp(a) for a in ins]
        writes = [self.fp(a) for a in outs]
        deps = self._deps(eng, reads, writes)
        self._emit_waits(eng, deps)
        self.cnt[eng] += 1
        self.ops[eng].append(("op", fn))
        self._record(eng, ("e", eng), self.cnt[eng], reads, writes)

    def dma(self, out, in_, q="sp", out_onchip=True, in_onchip=False, is_output=False, **kw):
        reads = [self.fp(in_)] if in_onchip else []
        writes = [self.fp(out)] if out_onchip else []
        deps = self._deps("dma", reads, writes)
        s = self.dma_rr
        self.dma_rr = (self.dma_rr + 1) % self.n_dma_sems
        if self.dma_vals[s] > 0:
            deps[("d", s)] = max(deps.get(("d", s), 0), self.dma_vals[s])
        self._emit_waits(q, deps)
        self.dma_vals[s] += 16
        val = self.dma_vals[s]
        self.ops[q].append(("dma", out, in_, s, kw))
        self._record("dma", ("d", s), val, reads, writes)
        if is_output:
            self.out_dma.append((s, val))

    def finish(self):
        nc = self.nc
        with contextlib.ExitStack() as st:
            esem = {e: st.enter_context(nc.semaphore("sem_" + e)) for e in ENGS if e != "sp"}
            dsem = [st.enter_context(nc.semaphore("dsem%d" % i)) for i in range(self.n_dma_sems)]
            for s, val in self.out_dma:
                if self.waited["sp"].get(("d", s), 0) < val:
                    self.waited["sp"][("d", s)] = val
                    self.ops["sp"].append(("wait", ("d", s), val))
            block = st.enter_context(nc.Block())

            def semof(sk):
                return esem[sk[1]] if sk[0] == "e" else dsem[sk[1]]

            def run(eng_name, e):
                for item in self.ops[eng_name]:
                    if item[0] == "wait":
                        e.wait_ge(semof(item[1]), item[2])
                    elif item[0] == "op":
                        item[1](e).then_inc(esem[eng_name], 1)
                    else:
                        _, out, in_, s, kw = item
                        e.dma_start(out=out, in_=in_, **kw).then_inc(dsem[s], 16)

            @block.tensor
            def _(e):
                run("pe", e)

            @block.scalar
            def _(e):
                run("act", e)

            @block.vector
            def _(e):
                run("dve", e)

            @block.gpsimd
            def _(e):
                run("pool", e)

            @block.sync
            def _(e):
                run("sp", e)


class Arena:
    def __init__(self, A, base, limit):
        self.A, self.off, self.limit, self.hi = A, base, limit, base

    def take(self, n):
        ap = self.A[:, self.off:self.off + n]
        self.off += n
        assert self.off <= self.limit, ("arena overflow", self.off, self.limit)
        return ap

    def sub(self, n):
        base = self.off
        self.off += n
        assert self.off <= self.limit, ("arena overflow", self.off, self.limit)
        return base, base + n


def bfv(ap):
    return ap.bitcast(BF16)


def build(nseq, S, nsb, dbg=None):
    NT = S // 128
    NPOS = S + 128
    nc = bass.Bass("TRN2", target_bir_lowering=False, dynamic_dma_scratch_size=256)

    def din(name, shape):
        return nc.dram_tensor(name, shape, F32, kind="ExternalInput")

    def dout(name, shape):
        return nc.dram_tensor(name, shape, F32, kind="ExternalOutput")

    xp = din("xp", [nseq * S, 1024]); pp = din("pp", [nseq * S, 256])
    xsm = din("xsm", [nsb * 8, 1024]); psm = din("psm", [nsb * 8, 256])
    wkv0 = din("wkv0", [nsb, 8, 64, 64]); sh0 = din("sh0", [nsb, SHIFT_DIM])
    ck = din("ck", [nsb, 128, 128]); cv = din("cv", [nsb, 128, 128])
    w_in = din("w_in", [1024, IN_DIM]); w_out = din("w_out", [1024, 1024])
    w_pg = din("w_pg", [1024, 1024]); w_pp = din("w_pp", [256, 1024])
    w2 = din("w2", [64, 512]); a2 = din("a2", [64, 512])
    pvec_d = din("pvec", [128, NV]); lnw_d = din("lnw", [1, 512]); lnb_d = din("lnb", [1, 512])
    gfin_d = din("gfin", [1, 1024]); consts_d = din("consts", [128, NCONST])
    rope_d = din("rope", [2, 128, NPOS])

    y_p = dout("y_p", [nseq * S, 1024]); y_s = dout("y_s", [nsb * 8, 1024])
    wkv_p = dout("wkv_p", [nseq, 8, 64, 64]); sh_p = dout("sh_p", [nseq, SHIFT_DIM])
    k_p = dout("k_p", [nseq, 128, 128]); v_p = dout("v_p", [nseq, 128, 128])
    wkv_s = dout("wkv_s", [nsb, 8, 64, 64]); sh_s = dout("sh_s", [nsb, SHIFT_DIM])
    k_s = dout("k_s", [nsb, 128, 128]); v_s = dout("v_s", [nsb, 128, 128])
    dbg_d = dout("dbg", [128, 8192]) if dbg is not None else None
    dbg_off = [0]

    AW = 49088
    with contextlib.ExitStack() as st:
        A = st.enter_context(nc.sbuf_tensor("arena", [128, AW], F32))
        PS = st.enter_context(nc.psum_tensor("ps", [128, 4096], F32))
        Sd = Sched(nc)
        ar = Arena(A, 0, AW)

        def tt(eng, out, in0, in1, op):
            Sd.op(eng, lambda e: e.tensor_tensor(out=out, in0=in0, in1=in1, op=op), [out], [in0, in1])

        def ts(eng, out, in0, s1, s2, op0, op1=None):
            ins = [in0] + [s for s in (s1, s2) if s is not None and not isinstance(s, (int, float))]
            if op1 is None:
                Sd.op(eng, lambda e: e.tensor_scalar(out=out, in0=in0, scalar1=s1, scalar2=None, op0=op0), [out], ins)
            else:
                Sd.op(eng, lambda e: e.tensor_scalar(out=out, in0=in0, scalar1=s1, scalar2=s2, op0=op0, op1=op1), [out], ins)

        def stt(out, in0, scalar, in1, op0, op1):
            ins = [in0, in1] + ([scalar] if not isinstance(scalar, (int, float)) else [])
            Sd.op("dve", lambda e: e.scalar_tensor_tensor(out=out, in0=in0, scalar=scalar, in1=in1, op0=op0, op1=op1), [out], ins)

        def act(out, in_, func, bias=None, scale=None, accum=None):
            ins = [in_]
            kw = {}
            if bias is not None:
                kw["bias"] = bias
                if not isinstance(bias, (int, float)):
                    ins.append(bias)
            if scale is not None:
                kw["scale"] = scale
                if not isinstance(scale, (int, float)):
                    ins.append(scale)
            outs = [out]
            if accum is not None:
                kw["accum_out"] = accum
                outs.append(accum)
            Sd.op("act", lambda e: e.activation(out=out, in_=in_, func=func, **kw), outs, ins)

        def cp(eng, out, in_):
            if eng == "act":
                act(out, in_, AF.Copy)
            else:
                Sd.op(eng, lambda e: e.tensor_copy(out=out, in_=in_), [out], [in_])

        def mm(out, lhsT, rhs, start=True, stop=True):
            Sd.op("pe", lambda e: e.matmul(out, lhsT=lhsT, rhs=rhs, start=start, stop=stop), [out], [lhsT, rhs])

        def ms(eng, out, val):
            Sd.op(eng, lambda e: e.memset(out, val), [out], [])

        def apx(base, tail, off=0, parts=None):
            p = list(base.ap[0])
            if parts is not None:
                p[1] = parts
            return bass.AP(base.tensor, base.offset + off, [p] + [list(t) for t in tail])

        def bank(i, lo=0, hi=512):
            return PS[:, i * 512 + lo:i * 512 + hi]

        def dump(name, ap, parts=128):
            if dbg is None or name not in dbg or dbg[name] is not None:
                return
            n = 1
            for s in ap.shape[1:]:
                n *= s
            if ap.dtype == BF16:
                tmp = dbgtmp[0:parts, 0:n]
                cp("dve", tmp, ap.rearrange("p a b -> p (a b)") if len(ap.shape) == 3 else ap)
                src = tmp
            else:
                src = ap
            dbg[name] = (dbg_off[0], parts, tuple(ap.shape[1:]))
            dst = dbg_d.ap()[0:parts, dbg_off[0]:dbg_off[0] + n]
            if len(ap.shape) == 3 and ap.dtype != BF16:
                dst = dst.rearrange("p (a b) -> p a b", b=ap.shape[2])
            Sd.dma(dst, src, out_onchip=False, in_onchip=True, is_output=True)
            dbg_off[0] += n

        win_b = bfv(ar.take(8 * IN_DIM // 2)).rearrange("p (k n) -> p k n", n=IN_DIM)
        wout_b = bfv(ar.take(4096)).rearrange("p (k n) -> p k n", n=1024)
        wpg_b = bfv(ar.take(4096)).rearrange("p (k n) -> p k n", n=1024)
        wpp_b = bfv(ar.take(1024)).rearrange("p (k n) -> p k n", n=1024)
        lora_b = bfv(ar.take(512)).rearrange("p (j n) -> p j n", n=512)
        consts = ar.take(NCONST)
        ident = consts[:, 0:128]
        mset = consts[:, 128:640].rearrange("p (a b) -> p a b", b=128)
        m_le = consts[:, 256:384]
        nmgt = consts[:, 640:768]
        m_gt = consts[:, 768:896]
        rot = consts[:, 1024:1152]
        rsp = consts[:, 1152:1664]
        rss = consts[:, 1664:2176]
        pvec = ar.take(NV + 27)
        mu_c = pvec[:, 0:13]; kk_c = pvec[:, 13:17]; ka_c = pvec[:, 17:21]; rk_c = pvec[:, 21:25]
        w0_c = pvec[:, 25:29]; a0_c = pvec[:, 29:33]; gn_c = pvec[:, 33:41]; gp_c = pvec[:, 41:49]
        sk_c = pvec[:, 49:53]; omka_c = pvec[:, 53:57]; es_c = pvec[:, 57:61]; hw0_c = pvec[:, 61:65]; ha0_c = pvec[:, 65:69]; mhalf_c = pvec[:, 69:77]
        lnw_bc = ar.take(512); lnb_bc = ar.take(512); gfin_bc = ar.take(1024)
        cb = bfv(ar.take(256))
        bones_b = cb[:, 0:128]; hsel_b = cb[:, 128:130]; opad_b = cb[:, 136:392].rearrange("p (a b) -> p a b", b=128)
        stat = ar.take(64)
        dbgtmp = ar.take(1024) if dbg is not None else None

        xin = ar.take(1024); xin2 = ar.take(1024); xins = [xin, xin2]; pin = ar.take(256); cs = ar.take(256).rearrange("p (a b) -> p a b", b=128)
        sgr = ar.take(512).rearrange("p (a b) -> p a b", b=128)
        sga = ar.take(512).rearrange("p (a b) -> p a b", b=128)
        qf = ar.take(512).rearrange("p (a b) -> p a b", b=128)
        kvf = ar.take(256).rearrange("p (a b) -> p a b", b=128)
        xT = bfv(ar.take(512)).rearrange("p (k t) -> p k t", t=128)
        mixT = bfv(ar.take(512)).rearrange("p (k t) -> p k t", t=128)
        qr_b = bfv(ar.take(256)).rearrange("p (k t) -> p k t", t=128)
        kr32 = ar.take(128)
        krpad = [bfv(ar.take(128)).rearrange("p (a t) -> p a t", t=128) for _ in range(2)]
        vpad = [bfv(ar.take(128)).rearrange("p (a t) -> p a t", t=128) for _ in range(2)]
        vat32 = ar.take(128); kat32 = ar.take(128)
        Hst = ar.take(256).rearrange("p (j v) -> p j v", v=64)
        Hbf = bfv(ar.take(128)).rearrange("p (j v) -> p j v", v=64)
        tmpH = ar.take(256).rearrange("p (j v) -> p j v", v=64)
        ARpad = bfv(ar.take(1024)).rearrange("p (c a q t) -> p c a q t", c=4, a=2, q=2)
        BKpad = bfv(ar.take(1024)).rearrange("p (c a q t) -> p c a q t", c=4, a=2, q=2)
        khat_tm = bfv(ar.take(256)); bhat_tm = bfv(ar.take(256)); v_tm = bfv(ar.take(256)); v_tm32 = ar.take(512)
        cbon = ar.take(8)
        zcarry = ar.take(16)
        eLinc = ar.take(512)
        ytm = ar.take(512); Xb = bfv(ar.take(128))

        u1b, u1e = ar.sub(1872 + 1024)
        zbuf_flat = A[:, u1b:u1b + 1872]
        zs = A[:, u1b + 1872:u1b + 1872 + 1664].rearrange("p (c t) -> p c t", t=128)
        ar.sub(1664 - 1024)
        a1 = Arena(A, u1b, u1e)
        Aev = bfv(a1.take(1024)).rearrange("p (h q t) -> p h q t", h=4, q=4)
        N0b = bfv(a1.take(256)).rearrange("p (h t) -> p h t", t=128)
        Pb = [bfv(a1.take(256)).rearrange("p (h t) -> p h t", t=128) for _ in range(2)]
        PTb = [bfv(a1.take(256)).rearrange("p (h t) -> p h t", t=128) for _ in range(2)]
        Xf = a1.take(256); Uneg = bfv(a1.take(256))

        u2b, u2e = ar.sub(512 * 10 + 64 + 256)
        assert u2b == u1e + 640
        a2_ = Arena(A, u2b, u2e)
        Wk = [a2_.take(512) for _ in range(10)]
        L12 = bfv(a2_.take(64)); sqb = bfv(a2_.take(256))
        xs32 = A[:, u2b:u2b + 1024]
        PTp = bfv(A[:, u1b:u1b + 512]).rearrange("p (h t) -> p h t", t=128)
        PTc = bfv(A[:, u1b + 512:u1b + 1024]).rearrange("p (h t) -> p h t", t=128)
        rden = A[:, u1b + 1024:u1b + 1536]; g2 = rden
        tmpq_b = A[:, u1b + 1024:u1b + 1536]; tmpk_b = A[:, u1b + 1536:u1b + 1664]
        gn1 = Wk[4]; gn2 = Wk[5]
        yout = A[:, u2b + 4 * 512:u2b + 6 * 512]
        hn32 = A[:, u2b + 6 * 512:u2b + 8 * 512]
        hnT = bfv(Wk[8]).rearrange("p (k t) -> p k t", t=128)
        pT = bfv(Wk[9][:, 0:128]).rearrange("p (k t) -> p k t", t=128)
        shs = A[0:nsb, u2b + 1024:u2b + 1024 + SHIFT_DIM] if nsb > 0 else None
        ws_st = Wk[6][0:64, :]; wo_st = Wk[7][0:64, :]; yall = Wk[8]
        kc_st = Wk[8][:, 0:128]; vc_st = Wk[8][:, 128:256]
        stage = [A[:, u1b:u1b + IN_DIM], A[:, u1b + IN_DIM:u1b + 2 * IN_DIM]]
        assert u1b + 2 * IN_DIM <= ar.off
        print("arena words used", ar.off, "of", AW)

        Sd.dma(consts, consts_d.ap())
        Sd.dma(pvec[:, 0:NV], pvec_d.ap())
        Sd.dma(lnw_bc, bass.AP(lnw_d, 0, [[0, 128], [1, 512]]))
        Sd.dma(lnb_bc, bass.AP(lnb_d, 0, [[0, 128], [1, 512]]))
        Sd.dma(gfin_bc, bass.AP(gfin_d, 0, [[0, 128], [1, 1024]]))
        cp("dve", bones_b, consts[:, 896:1024])
        cp("dve", hsel_b, consts[:, 2176:2178])
        cp("dve", cb[:, 136:392], consts[:, 2178:2434])
        negprev = consts[:, 896:1024]; negcur = consts[:, 2178:2306]
        ts("dve", negprev, m_le, -30000.0, None, ALU.mult)
        ts("dve", negcur, m_gt, -30000.0, None, ALU.mult)
        ts("dve", omka_c, ka_c, -1.0, 1.0, ALU.mult, ALU.add)
        act(es_c, sk_c, AF.Exp)
        ts("dve", hw0_c, w0_c, 0.5, None, ALU.mult)
        ts("dve", ha0_c, a0_c, 0.5, None, ALU.mult)
        ms("dve", mhalf_c, -0.5)
        lst = stage[0][:, 0:1024].rearrange("p (j n) -> p j n", n=512)
        ms("pool", stage[0][:, 0:1024], 0.0)
        Sd.dma(lst[0:64, 0, :], w2.ap())
        Sd.dma(lst[64:128, 1, :], a2.ap())
        cp("dve", lora_b, lst)
        si = 0
        for k in range(8):
            sg = stage[si % 2]; si += 1
            Sd.dma(sg, w_in.ap()[k * 128:(k + 1) * 128, :])
            if k % 2 == 0:
                act(win_b[:, k, :], sg, AF.Copy, scale=gn_c[:, k:k + 1])
            else:
                ts("dve", win_b[:, k, :], sg, gn_c[:, k:k + 1], None, ALU.mult)
        for (wd, wb, gc, nk) in ((w_out, wout_b, None, 8), (w_pg, wpg_b, gp_c, 8), (w_pp, wpp_b, None, 2)):
            for k0 in range(0, nk, 2):
                sg = stage[si % 2]; si += 1
                sgv = sg[:, 0:2048].rearrange("p (k n) -> p k n", n=1024)
                Sd.dma(sgv, wd.ap()[k0 * 128:(k0 + 2) * 128, :].rearrange("(k p) n -> p k n", p=128))
                for kk_ in range(2):
                    k = k0 + kk_
                    if gc is None:
                        sc_ = 0.5 if wd is w_out else 1.0
                        if kk_ == 0:
                            act(wb[:, k, :], sgv[:, kk_, :], AF.Copy, scale=sc_)
                        else:
                            ts("dve", wb[:, k, :], sgv[:, kk_, :], sc_, None, ALU.mult)
                    elif kk_ == 0:
                        act(wb[:, k, :], sgv[:, kk_, :], AF.Copy, scale=gc[:, k:k + 1])
                    else:
                        ts("dve", wb[:, k, :], sgv[:, kk_, :], gc[:, k:k + 1], None, ALU.mult)
        ms("pool", ARpad.rearrange("p c a q t -> p (c a q t)"), 0.0)
        ms("pool", BKpad.rearrange("p c a q t -> p (c a q t)"), 0.0)
        for i in range(2):
            ms("pool", krpad[i].rearrange("p a t -> p (a t)"), 0.0)
            ms("pool", vpad[i].rearrange("p a t -> p (a t)"), 0.0)

        gbank = [0]
        marks = []

        def mark(lbl):
            marks.append((lbl, dict(Sd.cnt)))

        def gb():
            b = gbank[0]
            gbank[0] = (gbank[0] + 1) % 3
            return b

        def rsqrt_col(dst, src, mult, eps):
            n_ = dst.shape[1]
            p0_ = dst.offset // dst.ap[0][0]
            pw = mhalf_c[p0_:p0_ + dst.shape[0], 0:n_]
            ts("dve", dst, src, mult, eps, ALU.mult, ALU.add)
            tt("pool", dst, dst, pw, ALU.pow)

        def scan_pass(t0, C, lv):
            HG = 4 if C == 128 else 8
            if HG == 4:
                Aev_, N0_, P_, PT_, Xf_, Xb_ = Aev, N0b, Pb, PTb, Xf, Xb
            else:
                Aev_ = Aev.rearrange("p h q t -> p (h q t)").rearrange("p (h q t) -> p h q t", h=8, q=4)
                N0_ = N0b.rearrange("p h t -> p (h t)").rearrange("p (h t) -> p h t", h=8)
                P_ = [x.rearrange("p h t -> p (h t)").rearrange("p (h t) -> p h t", h=8) for x in Pb]
                PT_ = [x.rearrange("p h t -> p (h t)").rearrange("p (h t) -> p h t", h=8) for x in PTb]
                Xf_ = gn2; Xb_ = bfv(gn1[:, 0:256])
            HW = HG * 64

            def abank(hl):
                return bank(hl)[0:C, 0:4 * C] if C == 128 else bank(3)[0:C, hl * 4 * C:(hl + 1) * 4 * C]
            n0bank = bank(4)[0:C, 0:HG * C] if C == 128 else bank(3)[0:C, 256:256 + HG * C]
            for g in range(8 // HG):
                heads = [(g * (HG // 2) + jj, par) for jj in range(HG // 2) for par in range(2)]
                j0 = g * (HG // 2); nj = HG // 2
                for hl, (j, par) in enumerate(heads):
                    rhs_ar = ARpad[:, j, par, :, t0:t0 + C]
                    mm(abank(hl)[:, 0:2 * C].rearrange("p (q t) -> p q t", t=C), BKpad[:, j, 0, 0, t0:t0 + C], rhs_ar)
                    mm(abank(hl)[:, 2 * C:4 * C].rearrange("p (q t) -> p q t", t=C), BKpad[:, j, 0, 1, t0:t0 + C], rhs_ar)
                for hl, (j, par) in enumerate(heads):
                    mm(n0bank[:, hl * C:(hl + 1) * C], ARpad[:, j, par, 0, t0:t0 + C], BKpad[:, j, 0, 0, t0:t0 + C])
                if C == 128:
                    for hl in range(HG):
                        tt("dve", Aev_[0:C, hl, :, 0:C], abank(hl).rearrange("p (q t) -> p q t", t=C), mset[0:C, :, 0:C], ALU.mult)
                else:
                    for q in range(4):
                        tt("dve", Aev_[0:C, :, q, 0:C], bank(3)[0:C, 0:HG * 4 * C].rearrange("p (h q t) -> p h q t", h=HG, q=4)[:, :, q, :],
                           apx(mset[0:C, q, 0:C], [[0, HG], [1, C]]), ALU.mult)
                tt("dve", N0_[0:C, :, 0:C], n0bank.rearrange("p (h t) -> p h t", t=C),
                   apx(nmgt[0:C, 0:C], [[0, HG], [1, C]]), ALU.mult)
                for hl, (j, par) in enumerate(heads):
                    h = 2 * j + par
                    mm(bank(5)[0:C, hl * 64:(hl + 1) * 64], ARpad[:, j, par, 0, t0:t0 + C], Hbf[:, j, :], True, False)
                    mm(bank(5)[0:C, hl * 64:(hl + 1) * 64], Aev_[0:C, hl, 2, 0:C], v_tm[0:C, h * 64:(h + 1) * 64], False, True)
                cp("act", Xf_[0:C, 0:HW], bank(5)[0:C, 0:HW])
                cp("dve", Xb_[0:C, 0:HW], bank(5)[0:C, 0:HW])
                PTcur = [Aev_[0:C, hl, 0, 0:C] for hl in range(HG)]
                Pcur = [N0_[0:C, hl, 0:C] for hl in range(HG)]
                hh_ = HG // 2
                subs = [(0, hh_, 5, 6, 7), (hh_, HG, 0, 1, 2)] if C == 128 else [(0, HG, 5, 6, 7)]
                for i in range(lv):
                    for (h0, h1, bru, bp_, bpt) in subs:
                        for hl in range(h0, h1):
                            mm(bank(bru)[0:C, hl * 64:(hl + 1) * 64], PTcur[hl], Xb_[0:C, hl * 64:(hl + 1) * 64])
                        if i < lv - 2:
                            for hl in range(h0, h1):
                                mm(bank(bp_)[0:C, hl * C:(hl + 1) * C], PTcur[hl], Pcur[hl])
                        if i < lv - 1:
                            for hl in range(h0, h1):
                                mm(bank(bpt)[0:C, hl * C:(hl + 1) * C], Pcur[hl], PTcur[hl])
                    pn = P_[i % 2]; ptn = PT_[i % 2]
                    for (h0, h1, bru, bp_, bpt) in subs:
                        cs_ = slice(h0 * 64, h1 * 64)
                        tt("dve", Xf_[0:C, cs_], Xf_[0:C, cs_], bank(bru)[0:C, cs_], ALU.add)
                        if i < lv - 1:
                            cp("act", Xb_[0:C, cs_], Xf_[0:C, cs_])
                            if i < lv - 2:
                                cp("act", pn[0:C, h0:h1, 0:C], bank(bp_)[0:C, h0 * C:h1 * C].rearrange("p (h t) -> p h t", t=C))
                            cp("dve", ptn[0:C, h0:h1, 0:C], bank(bpt)[0:C, h0 * C:h1 * C].rearrange("p (h t) -> p h t", t=C))
                    if i < lv - 1:
                        Pcur = [pn[0:C, hl, 0:C] for hl in range(HG)]
                        PTcur = [ptn[0:C, hl, 0:C] for hl in range(HG)]
                act(Uneg[0:C, g * HW:(g + 1) * HW], Xf_[0:C, 0:HW], AF.Copy, scale=-1.0)
                for hl, (j, par) in enumerate(heads):
                    h = 2 * j + par
                    o_ = bank(6)[0:C, hl * 64:(hl + 1) * 64]
                    mm(o_, ARpad[:, j, par, 1, t0:t0 + C], Hbf[:, j, :], True, False)
                    mm(o_, Aev_[0:C, hl, 3, 0:C], v_tm[0:C, h * 64:(h + 1) * 64], False, False)
                    mm(o_, Aev_[0:C, hl, 1, 0:C], Uneg[0:C, h * 64:(h + 1) * 64], False, True)
                cp("act", ytm[0:C, g * HW:(g + 1) * HW], bank(6)[0:C, 0:HW])
                for jj in range(nj):
                    j = j0 + jj
                    o_ = bank(7)[:, jj * 128:(jj + 1) * 128]
                    mm(o_, khat_tm[0:C, j * 128:(j + 1) * 128], v_tm[0:C, j * 128:(j + 1) * 128], True, False)
                    mm(o_, bhat_tm[0:C, j * 128:(j + 1) * 128], Uneg[0:C, j * 128:(j + 1) * 128], False, True)
                wc = apx(eLinc[:, t0 + C - 1:t0 + C], [[128, nj], [0, 64]], off=j0 * 128)
                tt("pool", tmpH[:, j0:j0 + nj, :], Hst[:, j0:j0 + nj, :], wc, ALU.mult)
                for par in range(2):
                    pr = slice(par * 64, par * 64 + 64)
                    src = bank(7)[pr, 0:nj * 128].rearrange("p (j v) -> p j v", v=128)[:, :, par * 64:par * 64 + 64]
                    tt("dve", Hst[pr, j0:j0 + nj, :], tmpH[pr, j0:j0 + nj, :], src, ALU.add)
            cp("act", Hbf, Hst)

        def tm_prep(khat, bhat, t0, C, want32=True, which=(0, 1, 2)):
            jobs = ((khat, khat_tm, None), (bhat, bhat_tm, None), (zs[:, 8:12, :], v_tm, v_tm32 if want32 else None))
            for src, dst, dst32 in [jobs[w] for w in which]:
                b_ = gb()
                for c in range(4):
                    Sd.op("pe", lambda e, o=bank(b_)[0:C, c * 128:(c + 1) * 128], i=src[:, c, t0:t0 + C]: e.transpose(out=o, in_=i, identity=ident),
                          [bank(b_)[0:C, c * 128:(c + 1) * 128]], [src[:, c, t0:t0 + C], ident])
                if dst32 is not None:
                    cp("act", dst32[0:C, :], bank(b_)[0:C, :])
                    cp("dve", dst[0:C, :], bank(b_)[0:C, :])
                else:
                    cp("act" if dst is khat_tm else "dve", dst[0:C, :], bank(b_)[0:C, :])

        def gn_bonus(C):
            v3 = v_tm32[0:C, :].rearrange("p (h v) -> p h v", v=64)
            tt("pool", v3, v3, apx(cbon[0:C, 0:8], [[1, 8], [0, 64]]), ALU.mult)
            tt("pool", v_tm32[0:C, :], v_tm32[0:C, :], lnb_bc[0:C, :], ALU.add)

        def gn_out(t0, C, mrbank, ysrc=None):
            ysrc = ytm if ysrc is None else ysrc
            y3 = ysrc[0:C, :].rearrange("p (h v) -> p h v", v=64)
            s1 = stat[0:C, 0:8]; s2 = stat[0:C, 8:16]; msq = stat[0:C, 24:32]
            bc8 = lambda s: apx(s, [[1, 8], [0, 64]])
            sq = gn2[0:C, :]
            act(sq, ysrc[0:C, :], AF.Square)
            Sd.op("dve", lambda e: e.tensor_reduce(out=s1, in_=y3, axis=AX.X, op=ALU.add), [s1], [y3])
            ts("dve", s1, s1, 1.0 / 64, None, ALU.mult)
            Sd.op("dve", lambda e: e.tensor_reduce(out=s2, in_=sq.rearrange("p (h v) -> p h v", v=64), axis=AX.X, op=ALU.add), [s2], [sq])
            ts("dve", s2, s2, 1.0 / 64, GN_EPS, ALU.mult, ALU.add)
            tt("dve", msq, s1, s1, ALU.mult)
            tt("dve", s2, s2, msq, ALU.subtract)
            tt("pool", s2, s2, mhalf_c[0:C, 0:8], ALU.pow)
            yc = gn1[0:C, :].rearrange("p (h v) -> p h v", v=64)
            tt("dve", yc, y3, bc8(s1), ALU.subtract)
            tt("dve", gn1[0:C, :], gn1[0:C, :], lnw_bc[0:C, :], ALU.mult)
            tt("dve", yc, yc, bc8(s2), ALU.mult)
            tt("dve", gn1[0:C, :], gn1[0:C, :], v_tm32[0:C, :], ALU.add)
            for c in range(4):
                o_ = mrbank[:, c * 128 + t0:c * 128 + t0 + C]
                i_ = gn1[0:C, c * 128:(c + 1) * 128]
                Sd.op("pe", lambda e, o=o_, i=i_: e.transpose(out=o, in_=i, identity=ident[0:C, 0:C]), [o_], [i_, ident[0:C, 0:C]])

        def attn_qk(t0, C, cur, prev, has_prev):
            if has_prev:
                for hb in range(2):
                    b_ = hb
                    for hh in range(4):
                        hq = hb * 4 + hh
                        c, par = hq % 4, hq // 4
                        mm(bank(b_)[:, hh * C:(hh + 1) * C], krpad[prev][:, par, :], qr_b[:, c, t0:t0 + C])
                    bv_ = bank(b_)[:, 0:4 * C].rearrange("p (h t) -> p h t", t=C)
                    tt("dve", bv_, bv_, apx(negprev[:, 0:C], [[0, 4], [1, C]]), ALU.add)
                    act(PTp[:, hb * 4:hb * 4 + 4, 0:C], bv_, AF.Exp, scale=0.125)
            for hb in range(2):
                b_ = 2 + hb
                for hh in range(4):
                    hq = hb * 4 + hh
                    c, par = hq % 4, hq // 4
                    mm(bank(b_)[0:C, hh * C:(hh + 1) * C], krpad[cur][:, par, t0:t0 + C], qr_b[:, c, t0:t0 + C])
                bv_ = bank(b_)[0:C, 0:4 * C].rearrange("p (h t) -> p h t", t=C)
                tt("dve", bv_, bv_, apx(negcur[0:C, 0:C], [[0, 4], [1, C]]), ALU.add)
                act(PTc[0:C, hb * 4:hb * 4 + 4, 0:C], bv_, AF.Exp, scale=0.125)

        def attn_pv(t0, C, cur, prev, has_prev):
            for c in range(4):
                oo = bank(6)[:, c * 128 + t0:c * 128 + t0 + C]
                dd = bank(7)[:, c * 128 + t0:c * 128 + t0 + C]
                seq = []
                if has_prev:
                    seq += [(vpad[prev][:, par, :], opad_b[:, par, :], PTp[:, par * 4 + c, 0:C]) for par in range(2)]
                seq += [(vpad[cur][t0:t0 + C, par, :] if C == 128 else vpad[cur][0:C, par, :], opad_b[0:C, par, :], PTc[0:C, par * 4 + c, 0:C]) for par in range(2)]
                for i, (vv, on, pt) in enumerate(seq):
                    mm(oo, vv, pt, i == 0, i == len(seq) - 1)
                for i, (vv, on, pt) in enumerate(seq):
                    mm(dd, on, pt, i == 0, i == len(seq) - 1)


        def attn_pass(t0, C, cur, prev, has_prev):
            attn_qk(t0, C, cur, prev, has_prev)
            attn_pv(t0, C, cur, prev, has_prev)

        def load_x(T):
            Sd.dma(xins[T["gi"] % 2], T["xrows"])

        def p1(T):
            xin = xins[T["gi"] % 2]
            mark('P1 %s%d' % (T["kind"], T["ti"]))
            ss = stat[:, 16:17]
            act(xs32, xin, AF.Square, accum=ss)
            rsqrt_col(ss, ss, 1.0 / 1024, NORM_EPS)
            act(xs32, xin, AF.Copy, scale=ss)
            for hb in range(2):
                b_ = gb()
                for kk_ in range(4):
                    k = hb * 4 + kk_
                    Sd.op("pe", lambda e, o=bank(b_)[:, kk_ * 128:(kk_ + 1) * 128], i=xs32[:, k * 128:(k + 1) * 128]: e.transpose(out=o, in_=i, identity=ident),
                          [bank(b_)[:, kk_ * 128:(kk_ + 1) * 128]], [xs32[:, k * 128:(k + 1) * 128], ident])
                cp("dve" if hb == 0 else "act", xT[:, hb * 4:hb * 4 + 4, :], bank(b_).rearrange("p (k t) -> p k t", t=128))

        def body_front(T):
            kind, ti, seq, first_of_seq, last_of_seq = T["kind"], T["ti"], T["seq"], T["first"], T["last"]
            prows, yrows, pos0 = T["prows"], T["yrows"], T["pos0"]
            xin = xins[T["gi"] % 2]
            nb, C = (1, 128) if kind == "p" else (nsb, 8)
            lv = 7 if kind == "p" else 3
            zb = zbuf_flat[:, 0:13 * nb * (C + 1)].rearrange("p (c n t) -> p c n t", c=13, n=nb)
            mark('P2')
            if kind == "p":
                if first_of_seq:
                    ms("pool", zb[:, :, 0, 0:1], 0.0)
                else:
                    cp("pool", zb[:, :, 0, 0], zcarry[:, 0:13])
            else:
                Sd.dma(shs, sh0.ap())
                b_ = gb()
                for c in range(13):
                    Sd.op("pe", lambda e, o=bank(b_)[:, c * nsb:(c + 1) * nsb], i=shs[:, c * 128:(c + 1) * 128]: e.transpose(out=o, in_=i, identity=ident[0:nsb, 0:nsb]),
                          [bank(b_)[:, c * nsb:(c + 1) * nsb]], [shs[:, c * 128:(c + 1) * 128], ident[0:nsb, 0:nsb]])
                cp("dve", zb[:, :, :, 0], bank(b_)[:, 0:13 * nsb].rearrange("p (c n) -> p c n", n=nsb))
            zcur = zb[:, :, :, 1:C + 1]
            zprev = zb[:, :, :, 0:C]
            zs4 = zs.rearrange("p c (n t) -> p c n t", n=nb)
            mark('P2')
            for (what, c0, n) in [("z", 12, 1), ("z", 4, 4), ("z", 0, 4), ("z", 8, 4)]:
                b_ = gb()
                for cc in range(n):
                    oc = c0 + cc
                    for k in range(8):
                        mm(bank(b_)[:, cc * 128:(cc + 1) * 128], win_b[:, k, oc * 128:(oc + 1) * 128], xT[:, k, :], k == 0, k == 7)
                src = bank(b_)[:, 0:n * 128]
                if what == "z":
                    cp("act", zb[:, c0:c0 + n, :, 1:C + 1], src.rearrange("p (c n t) -> p c n t", c=n, n=nb))
                    eng_ = "pool" if c0 == 0 else "dve"
                    tt(eng_, zs4[:, c0:c0 + n], zprev[:, c0:c0 + n], zcur[:, c0:c0 + n], ALU.subtract)
                    tt(eng_, zs[:, c0:c0 + n, :], zs[:, c0:c0 + n, :], apx(mu_c[:, c0:c0 + n], [[1, n], [0, 128]]), ALU.mult)
                    tt(eng_, zs4[:, c0:c0 + n], zs4[:, c0:c0 + n], zcur[:, c0:c0 + n], ALU.add)
                elif what in ("gr", "ga"):
                    tmpg = Wk[2] if what == "gr" else Wk[3]
                    dstg = sgr if what == "gr" else sga
                    act(tmpg, src, AF.Tanh, scale=0.5)
                    stt(dstg.rearrange("p c t -> p (c t)"), tmpg, 1.0, src, ALU.add, ALU.mult)
                elif what == "q":
                    cp("act", qf, src.rearrange("p (c t) -> p c t", t=128))
                else:
                    cp("act", kvf, src.rearrange("p (c t) -> p c t", t=128))
            if kind == "p":
                cp("dve", zcarry[:, 0:13], zb[:, :, 0, C])
                if last_of_seq:
                    b_ = gb()
                    Sd.op("pe", lambda e, o=bank(b_)[0:13, 0:128], i=zcarry[:, 0:13]: e.transpose(out=o, in_=i, identity=ident),
                          [bank(b_)[0:13, 0:128]], [zcarry[:, 0:13], ident])
                    stg = A[0:13, u2b:u2b + 128]
                    cp("act", stg, bank(b_)[0:13, 0:128])
                    Sd.dma(sh_p.ap()[seq:seq + 1, :].rearrange("a (c p) -> (a c) p", p=128), stg, out_onchip=False, in_onchip=True, is_output=True)
            else:
                zl = A[:, u2b:u2b + 13 * nsb]
                cp("dve", zl.rearrange("p (c n) -> p c n", n=nsb), zb[:, :, :, C])
                stg = A[0:nsb, u2b + 256:u2b + 256 + SHIFT_DIM]
                for c0_ in range(0, 13, 4):
                    n_ = min(4, 13 - c0_)
                    b_ = gb()
                    for cc in range(n_):
                        c = c0_ + cc
                        Sd.op("pe", lambda e, o=bank(b_)[0:nsb, cc * 128:(cc + 1) * 128], i=zl[:, c * nsb:(c + 1) * nsb]: e.transpose(out=o, in_=i, identity=ident),
                              [bank(b_)[0:nsb, cc * 128:(cc + 1) * 128]], [zl[:, c * nsb:(c + 1) * nsb], ident])
                    cp("act", stg[:, c0_ * 128:(c0_ + n_) * 128], bank(b_)[0:nsb, 0:n_ * 128])
                Sd.dma(sh_s.ap(), stg, out_onchip=False, in_onchip=True, is_output=True)

        def body_rest(T, nxt):
            kind, ti, seq, first_of_seq, last_of_seq = T["kind"], T["ti"], T["seq"], T["first"], T["last"]
            prows, yrows, pos0 = T["prows"], T["yrows"], T["pos0"]
            xin = xins[T["gi"] % 2]
            nb, C = (1, 128) if kind == "p" else (nsb, 8)
            lv = 7 if kind == "p" else 3
            zb = zbuf_flat[:, 0:13 * nb * (C + 1)].rearrange("p (c n t) -> p c n t", c=13, n=nb)
            zs4 = zs.rearrange("p c (n t) -> p c n t", n=nb)
            Sd.dma(pin, prows)
            Sd.dma(cs, rope_d.ap()[:, :, pos0:pos0 + 128].rearrange("a p t -> p a t"))
            if nxt is not None:
                load_x(nxt)
            if kind == "p":
                mark('P4')
                tm_prep(None, None, 0, 128, which=(2,))
                W = [w.rearrange("p (c t) -> p c t", t=128) for w in Wk]
                r3, kx3, v3 = zs[:, 0:4, :], zs[:, 4:8, :], zs[:, 8:12, :]
                act(L12[0:64, :], zs[0:64, 12, :], AF.Tanh)
                cp("act", L12[64:128, :], zs[64:128, 12, :])
                bw = gb(); ba = gb()
                for c in range(4):
                    mm(bank(bw)[:, c * 128:(c + 1) * 128], lora_b[:, 0, c * 128:(c + 1) * 128], L12)
                for c in range(4):
                    mm(bank(ba)[:, c * 128:(c + 1) * 128], lora_b[:, 1, c * 128:(c + 1) * 128], L12)
                sigw, a_, Linc, Lexc = W[0], W[1], Wk[2], Wk[3]
                for c in range(4):
                    act(sigw[:, c, :], bank(bw)[:, c * 128:(c + 1) * 128], AF.Tanh, bias=hw0_c[:, c:c + 1], scale=0.5)
                for c in range(4):
                    act(a_[:, c, :], bank(ba)[:, c * 128:(c + 1) * 128], AF.Tanh, bias=ha0_c[:, c:c + 1], scale=0.5)
                ts("dve", Wk[0], Wk[0], 0.5, 0.5, ALU.mult, ALU.add)
                ts("pool", Wk[1], Wk[1], 0.5, 0.5, ALU.mult, ALU.add)
                rsm = rsp if kind == "p" else rss
                Sd.op("dve", lambda e: e.tensor_tensor_scan(out=Linc, data0=rsm, data1=Wk[0], initial=0.0, op0=ALU.mult, op1=ALU.add), [Linc], [rsm, Wk[0]])
                tt("dve", Lexc, Linc, Wk[0], ALU.subtract)
                eLexc, emLinc, edk = Wk[4], Wk[5], Wk[6]
                act(eLinc, Linc, AF.Exp, scale=-C0)
                act(eLexc, Lexc, AF.Exp, scale=-C0)
                act(emLinc, Linc, AF.Exp, scale=C0)
                L4 = Linc.rearrange("p (c n t) -> p c n t", c=4, n=nb)
                ltot = apx(Linc[:, C - 1:C], [[128, 4], [C, nb], [0, C]])
                tt("dve", edk.rearrange("p (c n t) -> p c n t", c=4, n=nb), ltot, L4, ALU.subtract)
                act(edk, edk, AF.Exp, scale=-C0)
                for (what, c0, n) in [("gr", 13, 4), ("q", 17, 4)]:
                    b_ = gb()
                    for cc in range(n):
                        oc = c0 + cc
                        for k in range(8):
                            mm(bank(b_)[:, cc * 128:(cc + 1) * 128], win_b[:, k, oc * 128:(oc + 1) * 128], xT[:, k, :], k == 0, k == 7)
                    src = bank(b_)[:, 0:n * 128]
                    if what == "z":
                        cp("act", zb[:, c0:c0 + n, :, 1:C + 1], src.rearrange("p (c n t) -> p c n t", c=n, n=nb))
                        eng_ = "pool" if c0 == 0 else "dve"
                        tt(eng_, zs4[:, c0:c0 + n], zprev[:, c0:c0 + n], zcur[:, c0:c0 + n], ALU.subtract)
                        tt(eng_, zs[:, c0:c0 + n, :], zs[:, c0:c0 + n, :], apx(mu_c[:, c0:c0 + n], [[1, n], [0, 128]]), ALU.mult)
                        tt(eng_, zs4[:, c0:c0 + n], zs4[:, c0:c0 + n], zcur[:, c0:c0 + n], ALU.add)
                    elif what in ("gr", "ga"):
                        tmpg = Wk[2] if what == "gr" else Wk[3]
                        dstg = sgr if what == "gr" else sga
                        act(tmpg, src, AF.Tanh, scale=0.5)
                        stt(dstg.rearrange("p c t -> p (c t)"), tmpg, 1.0, src, ALU.add, ALU.mult)
                    elif what == "q":
                        cp("act", qf, src.rearrange("p (c t) -> p c t", t=128))
                    else:
                        cp("act", kvf, src.rearrange("p (c t) -> p c t", t=128))
                cur = ti % 2 if kind == "p" else 0
                prev = 1 - cur
                cosb = apx(cs[:, 0, :], [[0, 4], [1, 128]]); sinb = apx(cs[:, 1, :], [[0, 4], [1, 128]])
                bq_ = gb()
                for c in range(4):
                    mm(bank(bq_)[:, c * 128:(c + 1) * 128], rot, qf[:, c, :])
                tmpq = tmpq_b.rearrange("p (c t) -> p c t", t=128)
                tt("dve", tmpq, bank(bq_).rearrange("p (c t) -> p c t", t=128), sinb, ALU.mult)
                tt("pool", qf, qf, cosb, ALU.mult)
                tt("dve", qr_b, qf, tmpq, ALU.add)
                kkx = W[7]
                tt("dve", kkx, kx3, apx(kk_c, [[1, 4], [0, 128]]), ALU.mult)
                act(sqb, Wk[7], AF.Square)
                bq = gb()
                for c in range(4):
                    mm(bank(bq)[:, c * 128:(c + 1) * 128], bones_b, sqb[:, c * 128:(c + 1) * 128])
                rn = Wk[8]
                ts("dve", rn, bank(bq), 1e-18, None, ALU.max)
                act(rn, rn, AF.Ln)
                act(rn, rn, AF.Exp, scale=-0.5)
                tt("dve", Wk[7], Wk[7], rn, ALU.mult)
                t1 = W[8]
                for c in range(4):
                    ts("pool", t1[:, c, :], a_[:, c, :], ka_c[:, c:c + 1], omka_c[:, c:c + 1], ALU.mult, ALU.add)
                kf = W[9]
                tt("dve", kf, kx3, t1, ALU.mult)
                kka = W[8]
                tt("pool", kka, kkx, a_, ALU.mult)
                for (what, c0, n) in [("kv", 21, 2), ("ga", 23, 4)]:
                    b_ = gb()
                    for cc in range(n):
                        oc = c0 + cc
                        for k in range(8):
                            mm(bank(b_)[:, cc * 128:(cc + 1) * 128], win_b[:, k, oc * 128:(oc + 1) * 128], xT[:, k, :], k == 0, k == 7)
                    src = bank(b_)[:, 0:n * 128]
                    if what == "z":
                        cp("act", zb[:, c0:c0 + n, :, 1:C + 1], src.rearrange("p (c n t) -> p c n t", c=n, n=nb))
                        eng_ = "pool" if c0 == 0 else "dve"
                        tt(eng_, zs4[:, c0:c0 + n], zprev[:, c0:c0 + n], zcur[:, c0:c0 + n], ALU.subtract)
                        tt(eng_, zs[:, c0:c0 + n, :], zs[:, c0:c0 + n, :], apx(mu_c[:, c0:c0 + n], [[1, n], [0, 128]]), ALU.mult)
                        tt(eng_, zs4[:, c0:c0 + n], zs4[:, c0:c0 + n], zcur[:, c0:c0 + n], ALU.add)
                    elif what in ("gr", "ga"):
                        tmpg = Wk[2] if what == "gr" else Wk[3]
                        dstg = sgr if what == "gr" else sga
                        act(tmpg, src, AF.Tanh, scale=0.5)
                        stt(dstg.rearrange("p c t -> p (c t)"), tmpg, 1.0, src, ALU.add, ALU.mult)
                    elif what == "q":
                        cp("act", qf, src.rearrange("p (c t) -> p c t", t=128))
                    else:
                        cp("act", kvf, src.rearrange("p (c t) -> p c t", t=128))
                mark('P7')
                bk_ = gb()
                mm(bank(bk_)[:, 0:128], rot, kvf[:, 0, :])
                tmpk = tmpk_b
                tt("dve", tmpk, bank(bk_)[:, 0:128], cs[:, 1, :], ALU.mult)
                tt("pool", kr32, kvf[:, 0, :], cs[:, 0, :], ALU.mult)
                tt("dve", kr32, kr32, tmpk, ALU.add)
                dump("kr32", kr32); dump("qr", qr_b.rearrange("p c t -> p (c t)"))
                bt = gb()
                Sd.op("pe", lambda e: e.transpose(out=bank(bt)[:, 0:128], in_=kvf[:, 1, :], identity=ident), [bank(bt)[:, 0:128]], [kvf[:, 1, :], ident])
                Sd.op("pe", lambda e: e.transpose(out=bank(bt)[:, 128:256], in_=kr32, identity=ident), [bank(bt)[:, 128:256]], [kr32, ident])
                cp("act", vat32, bank(bt)[:, 0:128])
                cp("dve", kat32, bank(bt)[:, 128:256])
                for par in range(2):
                        pr = slice(par * 64, par * 64 + 64)
                        cp("act", krpad[cur][pr, par, :], kr32[pr, :])
                        cp("act", vpad[cur][:, par, par * 64:par * 64 + 64], vat32[:, par * 64:par * 64 + 64])
                attn_qk(0, 128, cur, prev, not first_of_seq)
                rkr = sqb.rearrange("p (c t) -> p c t", t=128)
                tt("dve", W[1], r3, kf, ALU.mult)
                tt("dve", rkr, W[1], apx(rk_c, [[1, 4], [0, 128]]), ALU.mult)
                for par in range(2):
                    pr = slice(par * 64, par * 64 + 64)
                    tt("dve", ARpad[pr, :, par, 0, :], kkx[pr], Wk[4].rearrange("p (c t) -> p c t", t=128)[pr], ALU.mult)
                    tt("pool", ARpad[pr, :, par, 1, :], r3[pr], eLinc.rearrange("p (c t) -> p c t", t=128)[pr], ALU.mult)
                tt("dve", BKpad[:, :, 0, 0, :], kka, Wk[5].rearrange("p (c t) -> p c t", t=128), ALU.mult)
                tt("pool", BKpad[:, :, 0, 1, :], kf, Wk[5].rearrange("p (c t) -> p c t", t=128), ALU.mult)
                khat, bhat = W[2], W[3]
                tt("dve", khat, kf, W[6], ALU.mult)
                tt("pool", bhat, kka, W[6], ALU.mult)
                dump("zs", zs.rearrange("p c t -> p (c t)")); dump("eLinc", eLinc); dump("kk", Wk[7]); dump("kf", Wk[9]); dump("khat", Wk[2])

                attn_pv(0, 128, cur, prev, not first_of_seq)
                if last_of_seq:
                    Sd.dma(k_p.ap()[seq], kat32, out_onchip=False, in_onchip=True, is_output=True)
                    Sd.dma(v_p.ap()[seq], vat32, out_onchip=False, in_onchip=True, is_output=True)
                for c in range(4):
                    act(rden[:, c * 128:(c + 1) * 128], bank(7)[:, c * 128:(c + 1) * 128], AF.Ln, bias=es_c[:, c:c + 1], scale=1.0)
                act(rden, rden, AF.Exp, scale=-1.0)
                tt("dve", g2, rden, sga.rearrange("p c t -> p (c t)"), ALU.mult)
                tt("dve", mixT[:, 4:8, :], bank(6).rearrange("p (c t) -> p c t", t=128), g2.rearrange("p (c t) -> p c t", t=128), ALU.mult)
                dump("mixT", mixT.rearrange("p c t -> p (c t)"))

            else:
                for (what, c0, n) in [("gr", 13, 4), ("q", 17, 4), ("kv", 21, 2), ("ga", 23, 4)]:
                    b_ = gb()
                    for cc in range(n):
                        oc = c0 + cc
                        for k in range(8):
                            mm(bank(b_)[:, cc * 128:(cc + 1) * 128], win_b[:, k, oc * 128:(oc + 1) * 128], xT[:, k, :], k == 0, k == 7)
                    src = bank(b_)[:, 0:n * 128]
                    if what == "z":
                        cp("act", zb[:, c0:c0 + n, :, 1:C + 1], src.rearrange("p (c n t) -> p c n t", c=n, n=nb))
                        eng_ = "pool" if c0 == 0 else "dve"
                        tt(eng_, zs4[:, c0:c0 + n], zprev[:, c0:c0 + n], zcur[:, c0:c0 + n], ALU.subtract)
                        tt(eng_, zs[:, c0:c0 + n, :], zs[:, c0:c0 + n, :], apx(mu_c[:, c0:c0 + n], [[1, n], [0, 128]]), ALU.mult)
                        tt(eng_, zs4[:, c0:c0 + n], zs4[:, c0:c0 + n], zcur[:, c0:c0 + n], ALU.add)
                    elif what in ("gr", "ga"):
                        tmpg = Wk[2] if what == "gr" else Wk[3]
                        dstg = sgr if what == "gr" else sga
                        act(tmpg, src, AF.Tanh, scale=0.5)
                        stt(dstg.rearrange("p c t -> p (c t)"), tmpg, 1.0, src, ALU.add, ALU.mult)
                    elif what == "q":
                        cp("act", qf, src.rearrange("p (c t) -> p c t", t=128))
                    else:
                        cp("act", kvf, src.rearrange("p (c t) -> p c t", t=128))
                mark('P7')
                cur = ti % 2 if kind == "p" else 0
                prev = 1 - cur
                cosb = apx(cs[:, 0, :], [[0, 4], [1, 128]]); sinb = apx(cs[:, 1, :], [[0, 4], [1, 128]])
                bq_ = gb()
                for c in range(4):
                    mm(bank(bq_)[:, c * 128:(c + 1) * 128], rot, qf[:, c, :])
                bk_ = gb()
                mm(bank(bk_)[:, 0:128], rot, kvf[:, 0, :])
                tmpq = tmpq_b.rearrange("p (c t) -> p c t", t=128)
                tt("dve", tmpq, bank(bq_).rearrange("p (c t) -> p c t", t=128), sinb, ALU.mult)
                tt("pool", qf, qf, cosb, ALU.mult)
                tt("dve", qr_b, qf, tmpq, ALU.add)
                tmpk = tmpk_b
                tt("dve", tmpk, bank(bk_)[:, 0:128], cs[:, 1, :], ALU.mult)
                tt("pool", kr32, kvf[:, 0, :], cs[:, 0, :], ALU.mult)
                tt("dve", kr32, kr32, tmpk, ALU.add)
                dump("kr32", kr32); dump("qr", qr_b.rearrange("p c t -> p (c t)"))
                bt = gb()
                Sd.op("pe", lambda e: e.transpose(out=bank(bt)[:, 0:128], in_=kvf[:, 1, :], identity=ident), [bank(bt)[:, 0:128]], [kvf[:, 1, :], ident])
                Sd.op("pe", lambda e: e.transpose(out=bank(bt)[:, 128:256], in_=kr32, identity=ident), [bank(bt)[:, 128:256]], [kr32, ident])
                cp("act", vat32, bank(bt)[:, 0:128])
                cp("dve", kat32, bank(bt)[:, 128:256])
                for par in range(2):
                    pr = slice(par * 64, par * 64 + 64)
                    cp("pool", krpad[0][pr, par, :], kr32[pr, :])
                for b in range(nsb):
                    Sd.dma(k_s.ap()[b, 0:120, :], ck.ap()[b, 8:128, :], out_onchip=False, in_onchip=False, is_output=True)
                    Sd.dma(v_s.ap()[b, 0:120, :], cv.ap()[b, 8:128, :], out_onchip=False, in_onchip=False, is_output=True)
                    Sd.dma(k_s.ap()[b, 120:128, :], kat32[b * 8:(b + 1) * 8, :], out_onchip=False, in_onchip=True, is_output=True)
                    Sd.dma(v_s.ap()[b, 120:128, :], vat32[b * 8:(b + 1) * 8, :], out_onchip=False, in_onchip=True, is_output=True)
                    kvb = [(Wk[8][:, 0:128], Wk[8][:, 128:256]), (Wk[8][:, 256:384], Wk[8][:, 384:512])]
                    if b == 0:
                        Sd.dma(kvb[0][0], ck.ap()[0]); Sd.dma(kvb[0][1], cv.ap()[0])
                    if b + 1 < nsb:
                        Sd.dma(kvb[(b + 1) % 2][0], ck.ap()[b + 1]); Sd.dma(kvb[(b + 1) % 2][1], cv.ap()[b + 1])
                    kc, vc = kvb[b % 2]
                    bt2 = gb()
                    Sd.op("pe", lambda e, o=bank(bt2)[:, 0:128], i=kc: e.transpose(out=o, in_=i, identity=ident), [bank(bt2)[:, 0:128]], [kc, ident])
                    Sd.op("pe", lambda e, o=bank(bt2)[0:8, 128:256], i=kvf[:, 1, b * 8:(b + 1) * 8]: e.transpose(out=o, in_=i, identity=ident),
                          [bank(bt2)[0:8, 128:256]], [kvf[:, 1, b * 8:(b + 1) * 8], ident])
                    for par in range(2):
                        pr = slice(par * 64, par * 64 + 64)
                        cp("act" if par == 0 else "dve", krpad[1][pr, par, :], bank(bt2)[pr, 0:128])
                        cp("pool", vpad[1][:, par, par * 64:par * 64 + 64], vc[:, par * 64:par * 64 + 64])
                        cp("act" if par == 0 else "dve", vpad[0][0:8, par, par * 64:par * 64 + 64], bank(bt2)[0:8, 128 + par * 64:128 + par * 64 + 64])
                    attn_pass(b * 8, 8, 0, 1, True)
                for c in range(4):
                    act(rden[:, c * 128:(c + 1) * 128], bank(7)[:, c * 128:(c + 1) * 128], AF.Ln, bias=es_c[:, c:c + 1], scale=1.0)
                act(rden, rden, AF.Exp, scale=-1.0)
                tt("dve", g2, rden, sga.rearrange("p c t -> p (c t)"), ALU.mult)
                tt("dve", mixT[:, 4:8, :], bank(6).rearrange("p (c t) -> p c t", t=128), g2.rearrange("p (c t) -> p c t", t=128), ALU.mult)
                dump("mixT", mixT.rearrange("p c t -> p (c t)"))

                mark('P4')
                W = [w.rearrange("p (c t) -> p c t", t=128) for w in Wk]
                r3, kx3, v3 = zs[:, 0:4, :], zs[:, 4:8, :], zs[:, 8:12, :]
                act(L12[0:64, :], zs[0:64, 12, :], AF.Tanh)
                cp("act", L12[64:128, :], zs[64:128, 12, :])
                bw = gb(); ba = gb()
                for c in range(4):
                    mm(bank(bw)[:, c * 128:(c + 1) * 128], lora_b[:, 0, c * 128:(c + 1) * 128], L12)
                for c in range(4):
                    mm(bank(ba)[:, c * 128:(c + 1) * 128], lora_b[:, 1, c * 128:(c + 1) * 128], L12)
                sigw, a_, Linc, Lexc = W[0], W[1], Wk[2], Wk[3]
                for c in range(4):
                    act(sigw[:, c, :], bank(bw)[:, c * 128:(c + 1) * 128], AF.Tanh, bias=hw0_c[:, c:c + 1], scale=0.5)
                for c in range(4):
                    act(a_[:, c, :], bank(ba)[:, c * 128:(c + 1) * 128], AF.Tanh, bias=ha0_c[:, c:c + 1], scale=0.5)
                ts("dve", Wk[0], Wk[0], 0.5, 0.5, ALU.mult, ALU.add)
                ts("pool", Wk[1], Wk[1], 0.5, 0.5, ALU.mult, ALU.add)
                rsm = rsp if kind == "p" else rss
                Sd.op("dve", lambda e: e.tensor_tensor_scan(out=Linc, data0=rsm, data1=Wk[0], initial=0.0, op0=ALU.mult, op1=ALU.add), [Linc], [rsm, Wk[0]])
                tt("dve", Lexc, Linc, Wk[0], ALU.subtract)
                eLexc, emLinc, edk = Wk[4], Wk[5], Wk[6]
                act(eLinc, Linc, AF.Exp, scale=-C0)
                act(eLexc, Lexc, AF.Exp, scale=-C0)
                act(emLinc, Linc, AF.Exp, scale=C0)
                L4 = Linc.rearrange("p (c n t) -> p c n t", c=4, n=nb)
                ltot = apx(Linc[:, C - 1:C], [[128, 4], [C, nb], [0, C]])
                tt("dve", edk.rearrange("p (c n t) -> p c n t", c=4, n=nb), ltot, L4, ALU.subtract)
                act(edk, edk, AF.Exp, scale=-C0)
                kkx = W[7]
                tt("dve", kkx, kx3, apx(kk_c, [[1, 4], [0, 128]]), ALU.mult)
                act(sqb, Wk[7], AF.Square)
                bq = gb()
                for c in range(4):
                    mm(bank(bq)[:, c * 128:(c + 1) * 128], bones_b, sqb[:, c * 128:(c + 1) * 128])
                rn = Wk[8]
                ts("dve", rn, bank(bq), 1e-18, None, ALU.max)
                act(rn, rn, AF.Ln)
                act(rn, rn, AF.Exp, scale=-0.5)
                tt("dve", Wk[7], Wk[7], rn, ALU.mult)
                t1 = W[8]
                for c in range(4):
                    ts("pool", t1[:, c, :], a_[:, c, :], ka_c[:, c:c + 1], omka_c[:, c:c + 1], ALU.mult, ALU.add)
                kf = W[9]
                tt("dve", kf, kx3, t1, ALU.mult)
                kka = W[8]
                tt("pool", kka, kkx, a_, ALU.mult)
                rkr = sqb.rearrange("p (c t) -> p c t", t=128)
                tt("dve", W[1], r3, kf, ALU.mult)
                tt("dve", rkr, W[1], apx(rk_c, [[1, 4], [0, 128]]), ALU.mult)
                for par in range(2):
                    pr = slice(par * 64, par * 64 + 64)
                    tt("dve", ARpad[pr, :, par, 0, :], kkx[pr], Wk[4].rearrange("p (c t) -> p c t", t=128)[pr], ALU.mult)
                    tt("pool", ARpad[pr, :, par, 1, :], r3[pr], eLinc.rearrange("p (c t) -> p c t", t=128)[pr], ALU.mult)
                tt("dve", BKpad[:, :, 0, 0, :], kka, Wk[5].rearrange("p (c t) -> p c t", t=128), ALU.mult)
                tt("pool", BKpad[:, :, 0, 1, :], kf, Wk[5].rearrange("p (c t) -> p c t", t=128), ALU.mult)
                khat, bhat = W[2], W[3]
                tt("dve", khat, kf, W[6], ALU.mult)
                tt("pool", bhat, kka, W[6], ALU.mult)
                dump("zs", zs.rearrange("p c t -> p (c t)")); dump("eLinc", eLinc); dump("kk", Wk[7]); dump("kf", Wk[9]); dump("khat", Wk[2])

            if nxt is not None:
                p1(nxt)
            mark('P5')
            mrb = bank(4)
            for pb in range(nb):
                t0 = pb * C
                if kind == "p":
                    if first_of_seq:
                        ms("pool", Hst.rearrange("p j v -> p (j v)"), 0.0)
                        ms("pool", Hbf.rearrange("p j v -> p (j v)"), 0.0)
                else:
                    wsb = [ws_st, Wk[0][0:64, :]]
                    if pb == 0:
                        Sd.dma(wsb[0].rearrange("p (h k) -> p h k", k=64), wkv0.ap()[0].rearrange("h v k -> v h k"))
                    if pb + 1 < nb:
                        Sd.dma(wsb[(pb + 1) % 2].rearrange("p (h k) -> p h k", k=64), wkv0.ap()[pb + 1].rearrange("h v k -> v h k"))
                    ws = wsb[pb % 2]
                    b_ = gb()
                    for j in range(4):
                        Sd.op("pe", lambda e, o=bank(b_)[:, j * 64:(j + 1) * 64], i=ws[:, j * 128:(j + 1) * 128]: e.transpose(out=o, in_=i, identity=ident[0:64, 0:64]),
                              [bank(b_)[:, j * 64:(j + 1) * 64]], [ws[:, j * 128:(j + 1) * 128], ident[0:64, 0:64]])
                    cp("act", Hst, bank(b_)[:, 0:256].rearrange("p (j v) -> p j v", v=64))
                    cp("dve", Hbf, bank(b_)[:, 0:256].rearrange("p (j v) -> p j v", v=64))
                if kind == "p":
                    b_ = gb()
                    for c in range(4):
                        mm(bank(b_)[0:C, 2 * c:2 * c + 2], rkr[:, c, t0:t0 + C], hsel_b)
                    cp("act", cbon[0:C, :], bank(b_)[0:C, 0:8])
                    gn_bonus(C)
                elif pb == 0:
                    b_ = gb()
                    for c in range(4):
                        mm(bank(b_)[:, 2 * c:2 * c + 2], rkr[:, c, :], hsel_b)
                    cp("act", cbon, bank(b_)[:, 0:8])
                    b_ = gb()
                    for c in range(4):
                        Sd.op("pe", lambda e, o=bank(b_)[:, c * 128:(c + 1) * 128], i=zs[:, 8 + c, :]: e.transpose(out=o, in_=i, identity=ident),
                              [bank(b_)[:, c * 128:(c + 1) * 128]], [zs[:, 8 + c, :], ident])
                    cp("act", v_tm32, bank(b_))
                if kind == "p":
                    tm_prep(khat, bhat, t0, C, which=(0, 1))
                else:
                    tm_prep(khat, bhat, t0, C, want32=False)
                    if pb == 0:
                        gn_bonus(128)
                scan_pass(t0, C, lv)
                if nxt is not None and pb == nb - 1:
                    body_front(nxt)
                if pb == 0:
                    dump("ytm", ytm); dump("Hst", Hst.rearrange("p j v -> p (j v)"))
                if kind == "p":
                    gn_out(t0, C, mrb)
                else:
                    Sd.dma(yall[t0:t0 + C, :], ytm[0:C, :], out_onchip=True, in_onchip=True)
                    if pb == nb - 1:
                        gn_out(0, 128, mrb, ysrc=yall)
                if kind == "s" or last_of_seq:
                    b_ = gb()
                    for j in range(4):
                        Sd.op("pe", lambda e, o=bank(b_)[0:64, j * 128:(j + 1) * 128], i=Hst[:, j, :]: e.transpose(out=o, in_=i, identity=ident),
                              [bank(b_)[0:64, j * 128:(j + 1) * 128]], [Hst[:, j, :], ident])
                    wo = wo_st
                    cp("act", wo, bank(b_)[0:64, :])
                    dst = (wkv_s.ap()[pb] if kind == "s" else wkv_p.ap()[seq]).rearrange("h v k -> v h k")
                    Sd.dma(dst, wo.rearrange("p (h k) -> p h k", k=64), out_onchip=False, in_onchip=True, is_output=True)
            tt("dve", mixT[:, 0:4, :], mrb.rearrange("p (c t) -> p c t", t=128), sgr, ALU.mult)

            mark('P8')
            for n in range(2):
                b_ = gb()
                for k in range(8):
                    mm(bank(b_), mixT[:, k, :], wout_b[:, k, n * 512:(n + 1) * 512], k == 0, k == 7)
                tt("dve", xin[:, n * 512:(n + 1) * 512], bank(b_), xin[:, n * 512:(n + 1) * 512], ALU.add)
            ss2 = stat[:, 17:18]
            act(hn32, xin, AF.Square, accum=ss2)
            rsqrt_col(ss2, ss2, 1.0 / 1024, NORM_EPS)
            hs2 = stat[:, 19:20]
            ts("dve", hs2, ss2, 0.5, None, ALU.mult)
            for hb in range(2):
                b_ = gb()
                for kk_ in range(4):
                    k = hb * 4 + kk_
                    Sd.op("pe", lambda e, o=bank(b_)[:, kk_ * 128:(kk_ + 1) * 128], i=xin[:, k * 128:(k + 1) * 128]: e.transpose(out=o, in_=i, identity=ident),
                          [bank(b_)[:, kk_ * 128:(kk_ + 1) * 128]], [xin[:, k * 128:(k + 1) * 128], ident])
                cp("dve" if hb == 0 else "act", hnT[:, hb * 4:hb * 4 + 4, :], bank(b_).rearrange("p (k t) -> p k t", t=128))
            b_ = gb()
            for k in range(2):
                Sd.op("pe", lambda e, o=bank(b_)[:, k * 128:(k + 1) * 128], i=pin[:, k * 128:(k + 1) * 128]: e.transpose(out=o, in_=i, identity=ident),
                      [bank(b_)[:, k * 128:(k + 1) * 128]], [pin[:, k * 128:(k + 1) * 128], ident])
            cp("act", pT, bank(b_)[:, 0:256].rearrange("p (k t) -> p k t", t=128))
            for n in range(2):
                bg = gb()
                for k in range(8):
                    mm(bank(bg), hnT[:, k, :], wpg_b[:, k, n * 512:(n + 1) * 512], k == 0, k == 7)
                gate = hn32[:, n * 512:(n + 1) * 512]
                act(gate, bank(bg), AF.Tanh, scale=hs2)
                bp = gb()
                for k in range(2):
                    mm(bank(bp), pT[:, k, :], wpp_b[:, k, n * 512:(n + 1) * 512], k == 0, k == 1)
                stt(gate, gate, 1.0, bank(bp), ALU.add, ALU.mult)
                stt(xin[:, n * 512:(n + 1) * 512], gate, 0.5, xin[:, n * 512:(n + 1) * 512], ALU.mult, ALU.add)
            ss3 = stat[:, 18:19]
            act(hn32, xin, AF.Square, accum=ss3)
            rsqrt_col(ss3, ss3, 1.0 / 1024, NORM_EPS)
            stt(yout, xin, ss3, gfin_bc, ALU.mult, ALU.mult)
            Sd.dma(yrows, yout, out_onchip=False, in_onchip=True, is_output=True)

        tiles = []
        for seq in range(nseq):
            for ti in range(NT):
                r0 = seq * S + ti * 128
                tiles.append(dict(kind="p", ti=ti, seq=seq, first=ti == 0, last=ti == NT - 1, xrows=xp.ap()[r0:r0 + 128, :],
                                  prows=pp.ap()[r0:r0 + 128, :], yrows=y_p.ap()[r0:r0 + 128, :], pos0=ti * 128))
        if nsb > 0 and not _SKIP_S:
            tiles.append(dict(kind="s", ti=0, seq=0, first=True, last=True, xrows=xsm.ap(), prows=psm.ap(), yrows=y_s.ap(), pos0=S))
        for gi, T in enumerate(tiles):
            T["gi"] = gi
        load_x(tiles[0])
        p1(tiles[0])
        body_front(tiles[0])
        for gi, T in enumerate(tiles):
            body_rest(T, tiles[gi + 1] if gi + 1 < len(tiles) else None)
        mark('END')
        Sd.finish()
        if _os.environ.get("KMARKS"):
            import json
            json.dump(marks, open(_os.environ["KMARKS"], "w"))
        print("instr counts", Sd.cnt, "waits", sum(1 for e in ENGS for it in Sd.ops[e] if it[0] == "wait"))
    return nc


def _host_consts(S):
    c = np.zeros((128, NCONST), np.float32)
    i = np.arange(128)
    s_, t_ = np.meshgrid(i, i, indexing="ij")
    c[:, 0:128] = np.eye(128)
    lt = (s_ < t_).astype(np.float32); le = (s_ <= t_).astype(np.float32); gt = (s_ > t_).astype(np.float32)
    c[:, 128:256] = -lt; c[:, 256:384] = le; c[:, 384:512] = lt; c[:, 512:640] = le
    c[:, 640:768] = -gt; c[:, 768:896] = gt
    c[:, 896:1024] = (s_ // 64 == t_ // 64)
    rot = np.zeros((128, 128), np.float32)
    for d in range(128):
        j = d % 64
        if j < 8:
            rot[d + 8, d] = 1
        elif j < 16:
            rot[d - 8, d] = 1
    c[:, 1024:1152] = rot
    rsp = np.ones((4, 128), np.float32); rsp[:, 0] = 0
    c[:, 1152:1664] = rsp.reshape(-1)[None]
    rss = np.ones(512, np.float32); rss[::8] = 0
    c[:, 1664:2176] = rss[None]
    c[:, 2176] = (i < 64); c[:, 2177] = (i >= 64)
    op = np.zeros((2, 128), np.float32); op[0, 0:64] = 1; op[1, 64:128] = 1
    c[:, 2178:2434] = op.reshape(-1)[None]
    npos = S + 128
    pos = np.concatenate([np.arange(S), np.tile(PAST_LEN + np.arange(8), 16)]).astype(np.float32)
    inv = (np.float32(500000.0) ** (-np.arange(8, dtype=np.float32) / np.float32(8))).astype(np.float32)
    ang = pos[:, None] * inv[None, :]
    co, si = np.cos(ang).astype(np.float32), np.sin(ang).astype(np.float32)
    rope = np.zeros((2, 128, npos), np.float32)
    rope[0] = 1.0
    for p in range(128):
        j = p % 64
        if j < 16:
            rope[0, p] = co[:, j % 8]
            rope[1, p] = -si[:, j % 8] if j < 8 else si[:, j % 8]
    return c, rope


_CACHE = {}
import os as _os
_PH = int(_os.environ.get("KPH", "9"))
_SKIP_S = bool(int(_os.environ.get("KSKIPS", "0")))


def _prep_weights(inp, S):
    n2o = np.array([(c + 4 * par) * 64 + d for c in range(4) for par in range(2) for d in range(64)])
    w_in = np.ascontiguousarray(inp["w_in"][0]).copy()
    o2 = SHIFT_DIM + 512
    o5 = o2 + 512 + 256
    w_in[:, o2:o2 + 512] = inp["w_in"][0][:, o2 + n2o]
    w_in[:, o5:o5 + 512] = inp["w_in"][0][:, o5 + n2o]
    w_out = np.ascontiguousarray(inp["w_out"][0]).copy()
    w_out[512:1024] = inp["w_out"][0][512 + n2o]
    pv = np.zeros((128, NV), np.float32)
    col = lambda v, n: np.asarray(v, np.float32).reshape(n, 128).T
    pv[:, 0:13] = col(inp["mu_shift"][0], 13)
    pv[:, 13:17] = col(inp["k_k"][0], 4); pv[:, 17:21] = col(inp["k_a"][0], 4); pv[:, 21:25] = col(inp["r_k"][0], 4)
    pv[:, 25:29] = col(inp["w0"][0], 4); pv[:, 29:33] = col(inp["a0"][0], 4)
    pv[:, 33:41] = col(inp["g_norm"][0], 8); pv[:, 41:49] = col(inp["g_ple"][0], 8)
    sk = np.asarray(inp["sinks"][0], np.float32)
    for c in range(4):
        for p in range(128):
            pv[p, 49 + c] = sk[c + 4 * (p // 64)]
    consts, rope = _host_consts(S)
    return dict(w_in=w_in, w_out=w_out, w_pg=np.ascontiguousarray(inp["w_ple_gate"][0]), w_pp=np.ascontiguousarray(inp["w_ple_proj"][0]),
                w2=np.ascontiguousarray(inp["w2"][0]), a2=np.ascontiguousarray(inp["a2"][0]), pvec=pv,
                lnw=np.ascontiguousarray(inp["ln_w"]).reshape(1, 512), lnb=np.ascontiguousarray(inp["ln_b"]).reshape(1, 512),
                gfin=np.ascontiguousarray(inp["g_final"]).reshape(1, 1024), consts=consts, rope=rope)


def run_cores(inp, n_cores, nseq, S, nsb, dbg=None):
    key = (nseq, S, nsb, dbg is not None)
    if key not in _CACHE:
        _CACHE[key] = build(nseq, S, nsb, dbg)
    nc = _CACHE[key]
    shared = _prep_weights(inp, S)
    f = lambda a: np.ascontiguousarray(a, dtype=np.float32)
    in_maps = []
    for c in range(n_cores):
        bs = slice(c * nseq, (c + 1) * nseq)
        ss = slice(c * nsb, (c + 1) * nsb)
        m = dict(shared)
        m["xp"] = f(inp["x_prompt"][bs]).reshape(nseq * S, 1024)
        m["pp"] = f(inp["p_prompt"][0, bs]).reshape(nseq * S, 256)
        m["xsm"] = f(inp["x_sample"][ss]).reshape(nsb * 8, 1024)
        m["psm"] = f(inp["p_sample"][0, ss]).reshape(nsb * 8, 256)
        m["wkv0"] = f(inp["state_rwkv_wkv"][0, ss])
        m["sh0"] = f(inp["state_rwkv_shift"][0, ss])
        m["ck"] = f(inp["cache_swa_k"][0, ss]).reshape(nsb, 128, 128)
        m["cv"] = f(inp["cache_swa_v"][0, ss]).reshape(nsb, 128, 128)
        in_maps.append(m)
    res = run_bass_kernel_spmd(nc, in_maps, core_ids=list(range(n_cores)))
    return res.results


def kernel(**inp):
    n = 8
    B, S = inp["x_prompt"].shape[0], inp["x_prompt"].shape[1]
    DB = inp["x_sample"].shape[0]
    nseq, nsb = B // n, DB // n
    r = run_cores(inp, n, nseq, S, nsb)
    cat = lambda k: np.concatenate([x[k] for x in r], axis=0)
    y_p = cat("y_p").reshape(B, S, 1024)
    y_s = cat("y_s").reshape(DB, 8, 1024)
    return (y_p, y_s,
            cat("wkv_p").reshape(1, B, 8, 64, 64), cat("sh_p").reshape(1, B, SHIFT_DIM),
            cat("k_p").reshape(1, B, 128, 2, 64), cat("v_p").reshape(1, B, 128, 2, 64),
            cat("wkv_s").reshape(1, DB, 8, 64, 64), cat("sh_s").reshape(1, DB, SHIFT_DIM),
            cat("k_s").reshape(1, DB, 128, 2, 64), cat("v_s").reshape(1, DB, 128, 2, 64))
```
